# Optimizing a Trainium2 kernel written in Bass

```python
import math
import jax, jax.numpy as jnp
from jax import lax
import numpy as np

D_MODEL = 1024
BATCH = 4
SEQ = 4096
DEPTH = 2

D_MIX = D_MODEL
EPS = 1e-6
NEG = -1e30
FORCE = 1e6
F_FLOOR = 1e-30
NSA_HEADS = 8
NSA_KV_GROUPS = 2
NSA_HD = 64
CMP_LEN = 32
CMP_STRIDE = 16
CMP_HID = 256
SEL_BLOCK = 64
SEL_TOPK = 16
SEL_QBLOCK = 32
WINDOW = 512
WIN_QBLOCK = 128
HG_HEADS = 4
HG_DK = 128
HG_DV = 128
HG_CHUNK = 64
D_FF = 2752

NSA_WIDTH = NSA_HEADS * NSA_HD
HG_WIDTH = HG_HEADS * HG_DV
KV_W = NSA_KV_GROUPS * NSA_HD
IN_SIZES = (NSA_WIDTH, KV_W, KV_W, KV_W, KV_W, KV_W, KV_W, NSA_HEADS * 3,
            HG_HEADS * HG_DK, HG_HEADS * HG_DK, HG_WIDTH, HG_WIDTH)
IN_COLS = sum(IN_SIZES)
IN_SPLITS = tuple(int(v) for v in np.cumsum(IN_SIZES)[:-1])

kernel_name = "hymba_nsa_hgrn2_macaron_alibi"


def rms_norm(x, g):
    xf = x.astype(jnp.float32)
    y = xf * lax.rsqrt(jnp.mean(xf * xf, axis=-1, keepdims=True) + EPS)
    return (y * g.astype(jnp.float32)).astype(x.dtype)


def swiglu(x, w_gu, w_down):
    gate, up = jnp.split(x @ w_gu, 2, axis=-1)
    return (jax.nn.silu(gate) * up) @ w_down


def alibi_slopes(n):
    return jnp.asarray([2.0 ** (-8.0 * (i + 1) / n) for i in range(n)], dtype=jnp.float32)


def compress(kv, pos_emb, w1, w2):
    B, S, G, dk = kv.shape
    chunks = kv.reshape(B, S // CMP_STRIDE, CMP_STRIDE, G, dk)
    blocks = jnp.concatenate([chunks[:, :-1], chunks[:, 1:]], axis=2) + pos_emb[:, None, :]
    nb = blocks.shape[1]
    flat = blocks.transpose(0, 1, 3, 2, 4).reshape(B, nb, G, CMP_LEN * dk)
    return jax.nn.gelu(flat @ w1) @ w2


def nsa_group(q, k_c, v_c, k_s, v_s, k_w, v_w, gate_logits,
              cmp_pos_k, cmp_pos_v, cmp_k_w1, cmp_k_w2, cmp_v_w1, cmp_v_w2):
    B, S = q.shape[:2]
    G, Hg, dk = NSA_KV_GROUPS, NSA_HEADS // NSA_KV_GROUPS, NSA_HD
    scale = dk ** -0.5
    slopes = alibi_slopes(NSA_HEADS).reshape(G, Hg)
    q5 = q.reshape(B, S, G, Hg, dk)
    kv4 = lambda t: t.reshape(B, S, G, dk)
    pos = jnp.arange(S, dtype=jnp.int32)

    kc = compress(kv4(k_c), cmp_pos_k, cmp_k_w1, cmp_k_w2)
    vc = compress(kv4(v_c), cmp_pos_v, cmp_v_w1, cmp_v_w2)
    nb = kc.shape[1]
    blk_end = jnp.arange(nb, dtype=jnp.int32) * CMP_STRIDE + (CMP_LEN - 1)
    dist_c = pos[:, None] - blk_end[None, :]
    s_c = jnp.einsum('bsgnd,bcgd->bsgnc', q5, kc).astype(jnp.float32) * scale
    s_c = s_c - slopes[:, :, None] * dist_c.astype(jnp.float32)[:, None, None, :]
    s_c = jnp.where((dist_c >= 0)[:, None, None, :], s_c, NEG)
    p_c = jax.nn.softmax(s_c, axis=-1)
    p_c = jnp.where((pos >= CMP_LEN - 1)[:, None, None, None], p_c, 0.0)
    o_cmp = jnp.einsum('bsgnc,bcgd->bsgnd', p_c.astype(vc.dtype), vc)

    imp = p_c.sum(axis=3)
    padded = jnp.pad(imp, ((0, 0), (0, 0), (0, 0), (1, 1)))
    chunk_imp = padded[..., :-1] + padded[..., 1:]
    ns = S // SEL_BLOCK
    p_slc = chunk_imp.reshape(B, S, G, ns, SEL_BLOCK // CMP_STRIDE).sum(-1)
    blk = jnp.arange(ns, dtype=jnp.int32)
    cur = pos // SEL_BLOCK
    forced = (blk[None] == 0) | (blk[None] == cur[:, None]) | (blk[None] == cur[:, None] - 1)
    causal = blk[None] * SEL_BLOCK <= pos[:, None]
    score = jnp.where(forced[:, None], FORCE, jnp.where(causal[:, None], p_slc, NEG))
    n_sel = min(SEL_TOPK, ns)
    _, sel_idx = lax.top_k(score, n_sel)

    k_blocks = kv4(k_s).reshape(B, ns, SEL_BLOCK, G, dk).transpose(0, 3, 1, 2, 4)
    v_blocks = kv4(v_s).reshape(B, ns, SEL_BLOCK, G, dk).transpose(0, 3, 1, 2, 4)
    nq = S // SEL_QBLOCK
    b_ix = jnp.arange(B)[:, None, None, None]
    g_ix = jnp.arange(G)[None, None, :, None]
    l_ar = jnp.arange(SEL_BLOCK, dtype=jnp.int32)

    def sel_step(args):
        qb, idx, qi = args
        kg = k_blocks[b_ix, g_ix, idx]
        vg = v_blocks[b_ix, g_ix, idx]
        s = jnp.einsum('btgnd,btgkld->btgnkl', qb, kg).astype(jnp.float32) * scale
        qpos = qi * SEL_QBLOCK + jnp.arange(SEL_QBLOCK, dtype=jnp.int32)
        kpos = idx[..., None] * SEL_BLOCK + l_ar
        dist = qpos[None, :, None, None, None] - kpos
        s = s - slopes[None, None, :, :, None, None] * dist.astype(jnp.float32)[:, :, :, None]
        s = jnp.where((dist >= 0)[:, :, :, None], s, NEG)
        p = jax.nn.softmax(s.reshape(s.shape[:4] + (-1,)), axis=-1).reshape(s.shape)
        return jnp.einsum('btgnkl,btgkld->btgnd', p.astype(vg.dtype), vg)

    o_sel = lax.map(sel_step, (q5.reshape(B, nq, SEL_QBLOCK, G, Hg, dk).swapaxes(0, 1),
                               sel_idx.reshape(B, nq, SEL_QBLOCK, G, n_sel).swapaxes(0, 1),
                               jnp.arange(nq, dtype=jnp.int32)))
    o_sel = o_sel.swapaxes(0, 1).reshape(B, S, G, Hg, dk)

    nwq = S // WIN_QBLOCK
    n_kb = WINDOW // WIN_QBLOCK + 1
    kw_len = n_kb * WIN_QBLOCK

    def band(t):
        tp = jnp.pad(kv4(t), ((0, 0), (WINDOW, 0), (0, 0), (0, 0)))
        tp = tp.reshape(B, nwq + WINDOW // WIN_QBLOCK, WIN_QBLOCK, G, dk)
        return jnp.concatenate([tp[:, j:j + nwq] for j in range(n_kb)], axis=2).swapaxes(0, 1)

    def win_step(args):
        qb, kb, vb, qi = args
        s = jnp.einsum('btgnd,bkgd->btgnk', qb, kb).astype(jnp.float32) * scale
        qpos = qi * WIN_QBLOCK + jnp.arange(WIN_QBLOCK, dtype=jnp.int32)
        kpos = qi * WIN_QBLOCK - WINDOW + jnp.arange(kw_len, dtype=jnp.int32)
        dist = qpos[:, None] - kpos[None, :]
        valid = (dist >= 0) & (dist < WINDOW) & (kpos >= 0)[None, :]
        s = s - slopes[None, None, :, :, None] * dist.astype(jnp.float32)[None, :, None, None, :]
        s = jnp.where(valid[None, :, None, None, :], s, NEG)
        p = jax.nn.softmax(s, axis=-1)
        return jnp.einsum('btgnk,bkgd->btgnd', p.astype(vb.dtype), vb)

    o_win = lax.map(win_step, (q5.reshape(B, nwq, WIN_QBLOCK, G, Hg, dk).swapaxes(0, 1),
                               band(k_w), band(v_w), jnp.arange(nwq, dtype=jnp.int32)))
    o_win = o_win.swapaxes(0, 1).reshape(B, S, G, Hg, dk)

    g = jax.nn.sigmoid(gate_logits.astype(jnp.float32)).reshape(B, S, G, Hg, 3)
    o = (g[..., 0:1] * o_cmp.astype(jnp.float32) + g[..., 1:2] * o_sel.astype(jnp.float32)
         + g[..., 2:3] * o_win.astype(jnp.float32))
    return o.reshape(B, S, NSA_WIDTH).astype(q.dtype)


def hgrn2_group(q, f_logit, i_in, g_in, lower_bound, out_norm):
    B, S = q.shape[:2]
    H, dk, dv, C = HG_HEADS, HG_DK, HG_DV, HG_CHUNK
    qf = jax.nn.silu(q.astype(jnp.float32)).reshape(B, S, H, dk)
    lb = lower_bound.astype(jnp.float32)
    z = f_logit.astype(jnp.float32)
    f = lb + (1.0 - lb) * jax.nn.sigmoid(z)
    logf = jnp.log(jnp.maximum(f, F_FLOOR)).reshape(B, S, H, dk)
    kf = ((1.0 - lb) * jax.nn.sigmoid(-z)).reshape(B, S, H, dk)
    vf = i_in.astype(jnp.float32).reshape(B, S, H, dv)
    nc = S // C
    to_chunks = lambda t: t.reshape(B, nc, C, H, t.shape[-1]).transpose(1, 0, 3, 2, 4)
    tril = jnp.tril(jnp.ones((C, C), dtype=bool))

    def step(state, inp):
        qc, kc, vc, lf = inp
        b = jnp.cumsum(lf, axis=2)
        decay = b[:, :, :, None, :] - b[:, :, None, :, :]
        decay = jnp.exp(jnp.where(tril[:, :, None], decay, NEG))
        a = jnp.einsum('bhtd,bhtsd,bhsd->bhts', qc, decay, kc)
        o_intra = jnp.einsum('bhts,bhsv->bhtv', a, vc)
        o_inter = jnp.einsum('bhtd,bhdv->bhtv', qc * jnp.exp(b), state)
        b_last = b[:, :, -1]
        new_state = (jnp.exp(b_last)[..., None] * state
                     + jnp.einsum('bhsd,bhsv->bhdv', kc * jnp.exp(b_last[:, :, None] - b), vc))
        return new_state, o_intra + o_inter

    state0 = jnp.zeros((B, H, dk, dv), jnp.float32)
    _, o = lax.scan(step, state0, (to_chunks(qf), to_chunks(kf), to_chunks(vf), to_chunks(logf)))
    o = o.transpose(1, 0, 3, 2, 4).reshape(B, S, H, dv)
    o = o * lax.rsqrt(jnp.mean(o * o, axis=-1, keepdims=True) + EPS) * out_norm.astype(jnp.float32)
    o = o.reshape(B, S, HG_WIDTH) * jax.nn.silu(g_in.astype(jnp.float32))
    return o.astype(q.dtype)


def hybrid_mixer(h, w_in, cmp_pos_k, cmp_pos_v, cmp_k_w1, cmp_k_w2, cmp_v_w1, cmp_v_w2,
                 lower_bound, hg_norm, w_out):
    parts = jnp.split(h @ w_in, IN_SPLITS, axis=-1)
    nq, k_c, v_c, k_s, v_s, k_w, v_w, gl, hq, hf, hi, hg = parts
    o_nsa = nsa_group(nq, k_c, v_c, k_s, v_s, k_w, v_w, gl,
                      cmp_pos_k, cmp_pos_v, cmp_k_w1, cmp_k_w2, cmp_v_w1, cmp_v_w2)
    o_hg = hgrn2_group(hq, hf, hi, hg, lower_bound, hg_norm)
    return jnp.concatenate([o_nsa, o_hg], axis=-1) @ w_out


def setup_inputs(seed: int = 0) -> dict:
    key = jax.random.key(seed)
    ks = jax.random.split(key, 20)
    nrm = lambda k, shape, s: jax.random.normal(k, shape, jnp.float32) * s
    gain = lambda k, shape: 1.0 + 0.02 * jax.random.normal(k, shape, jnp.float32)
    L = DEPTH
    return {
        "x": jax.random.normal(ks[0], (BATCH, SEQ, D_MODEL), jnp.float32),
        "ffn1_norm": gain(ks[1], (L, D_MODEL)),
        "ffn1_w_gu": nrm(ks[2], (L, D_MODEL, 2 * D_FF), D_MODEL ** -0.5),
        "ffn1_w_down": nrm(ks[3], (L, D_FF, D_MODEL), D_FF ** -0.5),
        "mix_norm": gain(ks[4], (L, D_MODEL)),
        "w_in": nrm(ks[5], (L, D_MODEL, IN_COLS), D_MODEL ** -0.5),
        "cmp_pos_k": nrm(ks[6], (L, CMP_LEN, NSA_HD), 0.02),
        "cmp_pos_v": nrm(ks[7], (L, CMP_LEN, NSA_HD), 0.02),
        "cmp_k_w1": nrm(ks[8], (L, CMP_LEN * NSA_HD, CMP_HID), (CMP_LEN * NSA_HD) ** -0.5),
        "cmp_k_w2": nrm(ks[9], (L, CMP_HID, NSA_HD), CMP_HID ** -0.5),
        "cmp_v_w1": nrm(ks[10], (L, CMP_LEN * NSA_HD, CMP_HID), (CMP_LEN * NSA_HD) ** -0.5),
        "cmp_v_w2": nrm(ks[11], (L, CMP_HID, NSA_HD), CMP_HID ** -0.5),
        "hgrn_lower_bound": nrm(ks[12], (L, HG_HEADS * HG_DK), 0.1),
        "hgrn_out_norm": gain(ks[13], (L, HG_DV)),
        "w_out": nrm(ks[14], (L, D_MIX, D_MODEL), D_MIX ** -0.5),
        "ffn2_norm": gain(ks[15], (L, D_MODEL)),
        "ffn2_w_gu": nrm(ks[16], (L, D_MODEL, 2 * D_FF), D_MODEL ** -0.5),
        "ffn2_w_down": nrm(ks[17], (L, D_FF, D_MODEL), D_FF ** -0.5),
        "final_norm": gain(ks[18], (D_MODEL,)),
    }


def reference(x, ffn1_norm, ffn1_w_gu, ffn1_w_down, mix_norm, w_in, cmp_pos_k, cmp_pos_v,
              cmp_k_w1, cmp_k_w2, cmp_v_w1, cmp_v_w2, hgrn_lower_bound, hgrn_out_norm, w_out,
              ffn2_norm, ffn2_w_gu, ffn2_w_down, final_norm):
    lb_sm = jax.nn.softmax(hgrn_lower_bound.astype(jnp.float32), axis=0)
    lower_bounds = jnp.cumsum(lb_sm, axis=0) - lb_sm[0]
    h = x
    for l in range(DEPTH):
        h = h + 0.5 * swiglu(rms_norm(h, ffn1_norm[l]), ffn1_w_gu[l], ffn1_w_down[l])
        h = h + hybrid_mixer(rms_norm(h, mix_norm[l]), w_in[l], cmp_pos_k[l], cmp_pos_v[l],
                             cmp_k_w1[l], cmp_k_w2[l], cmp_v_w1[l], cmp_v_w2[l],
                             lower_bounds[l], hgrn_out_norm[l], w_out[l])
        h = h + 0.5 * swiglu(rms_norm(h, ffn2_norm[l]), ffn2_w_gu[l], ffn2_w_down[l])
    return rms_norm(h, final_norm)
```

```python
import numpy as np
from contextlib import ExitStack
import concourse.bass as bass
import concourse.mybir as mybir
from concourse.bass_utils import run_bass_kernel_spmd
from concourse.alu_op_type import AluOpType as ALU

F32 = mybir.dt.float32
BF16 = mybir.dt.bfloat16
AF = mybir.ActivationFunctionType


COST_US = {}; PE_SCALE = 0.6; FFN_PE_SCALE = 0.05; MIX_PE_SCALE = 0.62


class Res:
    __slots__ = ("name", "t", "writer", "readers", "excl")

    def __init__(self, name, t=None):
        self.name = name
        self.t = t
        self.excl = False
        self.writer = None
        self.readers = []


class Op:
    __slots__ = ("eng", "fn", "kind", "deps", "needed", "token", "idx", "cost", "tbl", "eidx", "fin", "nd", "succ", "waits")

    def __init__(self, eng, fn, kind):
        self.eng = eng
        self.fn = fn
        self.kind = kind
        self.deps = []
        self.needed = False
        self.token = None


class _FakeT:
    def __getitem__(self, k):
        return self

    def rearrange(self, *a, **k):
        return self

    def to_broadcast(self, *a, **k):
        return self


class Sched:
    DMA_SLOTS = {"sp": 12, "pool": 8, "act": 4}

    def __init__(self, nc, stack):
        self.nc = nc
        self.stack = stack
        self.E = dict(pe=nc.tensor, act=nc.scalar, dve=nc.vector, pool=nc.gpsimd, sp=nc.sync)
        self.ops = []
        self.all_res = []
        self.uid = 0
        self.dry = False

    def sb(self, name, shape, dtype, stack=None):
        if self.dry:
            r = Res(name, _FakeT())
            self.all_res.append(r)
            return r
        self.uid += 1
        name = "%s_u%d" % (name, self.uid)
        t = (stack or self.stack).enter_context(self.nc.sbuf_tensor(name, shape, dtype))
        r = Res(name, t)
        self.all_res.append(r)
        return r

    def ps(self, name, shape, dtype, stack=None):
        if self.dry:
            r = Res(name, _FakeT())
            r.excl = True
            self.all_res.append(r)
            return r
        self.uid += 1
        name = "%s_u%d" % (name, self.uid)
        t = (stack or self.stack).enter_context(self.nc.psum_tensor(name, shape, dtype))
        r = Res(name, t)
        r.excl = True
        self.all_res.append(r)
        return r

    def res(self, name, t=None):
        r = Res(name, t)
        self.all_res.append(r)
        return r

    DEFAULT_COST = {"pe": 0.16, "act": 0.45, "dve": 0.30, "pool": 0.45}

    def _add(self, o, r, w):
        ex = [x for x in r if x.excl]
        if ex:
            r = [x for x in r if not x.excl]
            w = list(w) + [x for x in ex if x not in w]
        deps = []
        for x in r:
            if x.writer is not None:
                deps.append(x.writer)
        for x in w:
            if x.writer is not None:
                deps.append(x.writer)
            deps.extend(x.readers)
        for x in r:
            x.readers.append(o)
        for x in w:
            x.writer = o
            x.readers = []
        seen = set()
        for d in deps:
            if d is o or id(d) in seen:
                continue
            seen.add(id(d))
            o.deps.append(d)
        o.idx = len(self.ops)
        self.ops.append(o)
        return o

    def op(self, eng, fn, r=(), w=(), cost=None, tbl=None):
        o = Op(eng, fn, "c")
        o.cost = cost if cost is not None else self.DEFAULT_COST[eng]
        mc = COST_US.get(fn.__code__.co_firstlineno)
        if mc is not None:
            o.cost = (mc * PE_SCALE if eng == "pe" else mc) + 0.03
        o.tbl = tbl
        return self._add(o, r, w)

    def dma(self, q, out, in_, r=(), w=(), cost=3.0, **kw):
        e = self.E[q]
        o = Op(q, (lambda: e.dma_start(out=out, in_=in_, **kw)), "d")
        o.cost = cost
        o.tbl = None
        return self._add(o, r, w)

    def coll(self, kind, in_ap, out_ap, groups, r=(), w=()):
        g = self.nc.gpsimd
        o = Op("pool", (lambda: g.collective_compute(kind, ALU.bypass, groups, [in_ap], [out_ap])), "d")
        o.cost = 50.0
        o.tbl = None
        return self._add(o, r, w)

    def begin(self):
        nc = self.nc
        self.sem = {k: self.stack.enter_context(nc.semaphore("s_" + k)) for k in self.E}
        self.dsem = {q: [self.stack.enter_context(nc.semaphore("d_%s%d" % (q, i))) for i in range(n)]
                     for q, n in self.DMA_SLOTS.items()}
        self.cnt = {k: 0 for k in self.E}
        self.dcnt = {q: 0 for q in self.dsem}
        self.dhist = {q: [] for q in self.dsem}
        self.waited = {k: {} for k in self.E}
        self.barrier_tokens = []
        self.total_ops = 0

    def _wait(self, eng, tok):
        s, v = tok
        key = id(s)
        if self.waited[eng].get(key, 0) >= v:
            return
        self.waited[eng][key] = v
        self.E[eng].wait_ge(s, v)

    REORDER = True

    def _schedule(self):
        import heapq
        ops = self.ops
        if not self.REORDER:
            return list(ops)
        for o in ops:
            o.nd = len(o.deps)
            o.succ = []
            o.fin = 0.0
        for o in ops:
            for d in o.deps:
                d.succ.append(o)
        engs = list(self.E.keys())
        fut = {e: [] for e in engs}
        now = {e: {} for e in engs}
        free = {e: 0.0 for e in engs}
        last_tbl = [None]
        LIMIT = 3.0
        for o in ops:
            if o.nd == 0:
                heapq.heappush(fut[o.eng], (0.0, o.idx, o))
        order = []
        XLAT = 0.7
        n = len(ops)

        def pick_now(e):
            hs = now[e]
            if e != "act":
                h = hs.get(None)
                return (h[0][0], None) if h else None
            best = None
            cur = last_tbl[0]
            for tag in (cur, None):
                h = hs.get(tag)
                if h and (best is None or h[0][0] < best[0]):
                    best = (h[0][0], tag)
            other = None
            for tag, h in hs.items():
                if not h or tag in (cur, None):
                    continue
                if other is None or h[0][0] < other[0]:
                    other = (h[0][0], tag, h[0][1])
            if other is not None and (best is None or other[2] < free[e] - LIMIT):
                return (other[0], other[1])
            return best

        while len(order) < n:
            best = None
            for e in engs:
                f = fut[e]
                while f and f[0][0] <= free[e]:
                    rdy, ix, o = heapq.heappop(f)
                    tag = o.tbl if e == "act" else None
                    heapq.heappush(now[e].setdefault(tag, []), (ix, rdy, o))
                pk = pick_now(e)
                if pk is not None:
                    cand = (free[e], pk[0], e, 0, pk[1])
                elif f:
                    cand = (f[0][0], f[0][1], e, 1, None)
                else:
                    continue
                if best is None or cand[:2] < best[:2]:
                    best = cand
            st, _, e, which, tag = best
            if which == 0:
                _, _, o = heapq.heappop(now[e][tag])
            else:
                _, _, o = heapq.heappop(fut[e])
            c = o.cost
            if o.eng == "act" and o.tbl is not None:
                if last_tbl[0] is not None and last_tbl[0] != o.tbl:
                    c += 1.3
                last_tbl[0] = o.tbl
            if o.kind == "d":
                free[e] = st + 0.15
                o.fin = st + c
            else:
                free[e] = st + c
                o.fin = st + c
            order.append(o)
            for q in o.succ:
                q.nd -= 1
                if q.nd == 0:
                    rdy = 0.0
                    for d in q.deps:
                        t = d.fin + (XLAT if d.eng != q.eng else 0.0)
                        if t > rdy:
                            rdy = t
                    heapq.heappush(fut[q.eng], (rdy, q.idx, q))
        self.est_us = max(o.fin for o in ops) if ops else 0.0
        return order

    def flush(self):
        self.phase_no = getattr(self, "phase_no", 0) + 1
        self.sem = {k: self.stack.enter_context(self.nc.semaphore("s%d_%s" % (self.phase_no, k))) for k in self.E}
        self.cnt = {k: 0 for k in self.E}
        order = self._schedule()
        if self.dry:
            print("DRY phase ops %d est_us %.1f" % (len(order), self.est_us))
            self.ops = []
            for r in self.all_res:
                r.writer = None
                r.readers = []
            return
        ecount = {k: 0 for k in self.E}
        for o in order:
            o.needed = False
            if o.kind == "c":
                ecount[o.eng] += 1
                o.eidx = ecount[o.eng]
        last = {}
        for o in order:
            if o.kind == "c":
                last[o.eng] = o
            best = {}
            keep = []
            for d in o.deps:
                if d.kind == "d":
                    keep.append(d)
                    continue
                if o.kind == "c" and d.eng == o.eng and o.eng == "pe":
                    continue
                b = best.get(d.eng)
                if b is None or d.eidx > b.eidx:
                    best[d.eng] = d
            keep.extend(best.values())
            o.waits = keep
            for d in keep:
                d.needed = True
        for o in last.values():
            o.needed = True
        first_seen = set()
        for o in order:
            if o.eng not in first_seen:
                first_seen.add(o.eng)
                for tok in self.barrier_tokens:
                    self._wait(o.eng, tok)
            for d in o.waits:
                self._wait(o.eng, d.token)
            if o.kind == "c":
                ins = o.fn()
                if o.needed:
                    self.cnt[o.eng] += 1
                    ins.then_inc(self.sem[o.eng], 1)
                    o.token = (self.sem[o.eng], self.cnt[o.eng])
            else:
                q = o.eng
                K = len(self.dsem[q])
                n = self.dcnt[q]
                if n >= K:
                    self._wait(q, self.dhist[q][n - K])
                s = self.dsem[q][n % K]
                ins = o.fn()
                ins.then_inc(s, 16)
                o.token = (s, 16 * (n // K + 1))
                self.dhist[q].append(o.token)
                self.dcnt[q] += 1
            o.fn = None
            o.deps = None
            o.succ = None
            o.waits = None
        toks = [o.token for o in last.values()]
        for q in self.dsem:
            toks.extend(self.dhist[q][-len(self.dsem[q]):])
        self.barrier_tokens = toks + [t for t in self.barrier_tokens]
        best = {}
        for s_, v in self.barrier_tokens:
            if id(s_) not in best or best[id(s_)][1] < v:
                best[id(s_)] = (s_, v)
        self.barrier_tokens = list(best.values())
        self.total_ops += len(self.ops)
        self.ops = []
        for r in self.all_res:
            r.writer = None
            r.readers = []

    def end(self):
        for tok in self.barrier_tokens:
            self._wait("sp", tok)


D = 1024
SEQ = 4096
NB_ = 4
DEPTH = 2
DFF = 2752
NFT = 22
TT = 256
NT = SEQ // TT
EPS = 1e-6
IN_COLS = 3352


class Ctx:
    pass


def dram_tiles(S, name, ap):
    c = Ctx()
    c.ap = ap
    c.res = [S.res("%s_%d" % (name, i)) for i in range(NT)]
    return c


def tok_tile(ap, i):
    return ap[i * TT:(i + 1) * TT, :].rearrange("(s p) d -> p s d", p=128)


def prologue_ss(S, nc, C, hin):
    with ExitStack() as ph:
        xt = [S.sb("pro_x%d" % k, [128, 2, D], F32, ph) for k in range(2)]
        junk = S.sb("pro_junk", [128, D], BF16, ph)
        for i in range(NT):
            t = xt[i % 2]
            S.dma("sp", t.t[:], tok_tile(hin.ap, i), r=[hin.res[i]], w=[t])
            for s in range(2):
                S.op("dve", (lambda t=t, s=s, i=i: nc.vector.scalar_tensor_tensor(
                    out=junk.t[:], in0=t.t[:, s, :], scalar=1.0, in1=t.t[:, s, :],
                    op0=ALU.mult, op1=ALU.mult, accum_out=C.ss.t[:, 2 * i + s:2 * i + s + 1])),
                    r=[t], w=[junk, C.ss])
        S.flush()


def rstd_from_ss(S, nc, C, ph):
    tmp = S.sb("rs_tmp", [128, 32], F32, ph)
    S.op("dve", lambda: nc.vector.tensor_scalar(out=tmp.t[:], in0=C.ss.t[:], scalar1=1.0 / D, scalar2=EPS,
                                                op0=ALU.mult, op1=ALU.add), r=[C.ss], w=[tmp])
    S.op("act", lambda: nc.scalar.activation(out=tmp.t[:], in_=tmp.t[:], func=AF.Ln), r=[tmp], w=[tmp])
    S.op("act", lambda: nc.scalar.activation(out=C.rstd.t[:], in_=tmp.t[:], func=AF.Exp, scale=-0.5), r=[tmp], w=[C.rstd])


def ffn_phase(S, nc, C, w_gu, w_down, gvec, hin, hout, tag):
    global PE_SCALE; PE_SCALE = FFN_PE_SCALE
    with ExitStack() as ph:
        wgu = S.sb(tag + "wgu", [128, 8, 2 * DFF], BF16, ph)
        wd = S.sb(tag + "wd", [128, NFT, D], BF16, ph)
        gbc = S.sb(tag + "gbc", [128, D], F32, ph)
        ht = [S.sb(tag + "ht%d" % k, [128, 2, D], F32, ph) for k in range(2)]
        hn = [S.sb(tag + "hn%d" % k, [128, D], BF16, ph) for k in range(2)]
        hnT = [S.sb(tag + "hnT%d" % k, [128, 8, TT], BF16, ph) for k in range(2)]
        actT = [S.sb(tag + "actT%d" % k, [128, NFT, TT], BF16, ph) for k in range(2)]
        sg = [S.sb(tag + "sg%d" % k, [128, TT], F32, ph) for k in range(2)]
        junk = S.sb(tag + "junk", [128, D], BF16, ph)
        pt = S.ps(tag + "pt", [128, 1024], BF16, ph)
        gups = [S.ps(tag + "gu%d" % k, [128, 512], F32, ph) for k in range(3)]
        dps = [S.ps(tag + "dp%d" % k, [128, 512], F32, ph) for k in range(2)]

        rstd_from_ss(S, nc, C, ph)
        cgb = [0, 6, 12, 17, NFT]
        wguR = [[S.res(tag + "wguR%d_%d" % (k, kc)) for kc in range(8)] for k in range(4)]
        cg_of = [max(k for k in range(4) if cgb[k] <= j) for j in range(NFT)]
        for cg in range(4):
            a0, a1 = cgb[cg] * 128, min(cgb[cg + 1] * 128, DFF)
            for kc in range(8):
                S.dma("pool", wgu.t[:, kc, :].rearrange("p (h f) -> p h f", h=2)[:, :, a0:a1],
                      w_gu[kc * 128:(kc + 1) * 128, :].rearrange("p (h f) -> p h f", h=2)[:, :, a0:a1],
                      w=[wguR[cg][kc]], cost=6.0)
        S.dma("pool", wd.t[:, 0:NFT - 1, :], w_down[0:(NFT - 1) * 128, :].rearrange("(c p) n -> p c n", p=128), w=[wd])
        S.dma("pool", wd.t[0:64, NFT - 1, :], w_down[(NFT - 1) * 128:DFF, :], w=[wd])
        S.dma("sp", gbc.t[:], gvec.partition_broadcast(128), w=[gbc])

        def load(i):
            S.dma("sp", ht[i % 2].t[:], tok_tile(hin.ap, i), r=[hin.res[i]], w=[ht[i % 2]])

        load(0)
        gi = 0
        di = 0
        for i in range(NT):
            if i + 1 < NT:
                load(i + 1)
            h = ht[i % 2]
            xT = hnT[i % 2]
            aT = actT[i % 2]
            for s in range(2):
                n_ = hn[s]
                col = 2 * i + s
                S.op("dve", (lambda h=h, s=s, n_=n_, col=col: nc.vector.scalar_tensor_tensor(
                    out=n_.t[:], in0=h.t[:, s, :], scalar=C.rstd.t[:, col:col + 1], in1=gbc.t[:],
                    op0=ALU.mult, op1=ALU.mult)), r=[h, C.rstd, gbc], w=[n_])
                for kc in range(8):
                    S.op("pe", (lambda n_=n_, kc=kc: nc.tensor.transpose(
                        pt.t[:, kc * 128:(kc + 1) * 128], n_.t[:, kc * 128:(kc + 1) * 128], C.ident.t[:])),
                        r=[n_, C.ident], w=[pt])
                S.op("act", (lambda xT=xT, s=s: nc.scalar.copy(
                    out=xT.t[:, :, s * 128:(s + 1) * 128], in_=pt.t[:].rearrange("p (k t) -> p k t", k=8))),
                    r=[pt], w=[xT])
            for j in range(NFT):
                cw = 128 if j < NFT - 1 else 64
                g = gups[gi % 3]
                gi += 1
                for half in range(2):
                    c0 = half * DFF + j * 128
                    for kc in range(8):
                        S.op("pe", (lambda g=g, half=half, c0=c0, cw=cw, kc=kc, xT=xT: nc.tensor.matmul(
                            g.t[0:cw, half * TT:(half + 1) * TT], lhsT=wgu.t[:, kc, c0:c0 + cw], rhs=xT.t[:, kc, :],
                            start=(kc == 0), stop=(kc == 7))), r=[wguR[cg_of[j]][kc], xT], w=[g])
                sgt = sg[j % 2]
                S.op("act", (lambda g=g, cw=cw, sgt=sgt: nc.scalar.activation(
                    out=sgt.t[0:cw, :], in_=g.t[0:cw, 0:TT], func=AF.Silu)), r=[g], w=[sgt])
                S.op("dve", (lambda g=g, cw=cw, sgt=sgt, aT=aT, j=j: nc.vector.tensor_tensor(
                    out=aT.t[0:cw, j, :], in0=sgt.t[0:cw, :], in1=g.t[0:cw, TT:2 * TT], op=ALU.mult)),
                    r=[g, sgt], w=[aT])
            for s in range(2):
                for half in range(2):
                    dp = dps[di % 2]
                    di += 1
                    for c in range(NFT):
                        kw = 128 if c < NFT - 1 else 64
                        S.op("pe", (lambda dp=dp, kw=kw, c=c, s=s, half=half, aT=aT: nc.tensor.matmul(
                            dp.t[:, :], lhsT=aT.t[0:kw, c, s * 128:(s + 1) * 128],
                            rhs=wd.t[0:kw, c, half * 512:(half + 1) * 512],
                            start=(c == 0), stop=(c == NFT - 1))), r=[aT, wd], w=[dp])
                    S.op("dve", (lambda dp=dp, h=h, s=s, half=half: nc.vector.scalar_tensor_tensor(
                        out=h.t[:, s, half * 512:(half + 1) * 512], in0=dp.t[:, :], scalar=0.5,
                        in1=h.t[:, s, half * 512:(half + 1) * 512], op0=ALU.mult, op1=ALU.add)),
                        r=[dp, h], w=[h])
                col = 2 * i + s
                S.op("dve", (lambda h=h, s=s, col=col: nc.vector.scalar_tensor_tensor(
                    out=junk.t[:], in0=h.t[:, s, :], scalar=1.0, in1=h.t[:, s, :],
                    op0=ALU.mult, op1=ALU.mult, accum_out=C.ss.t[:, col:col + 1])), r=[h], w=[junk, C.ss])
            S.dma("sp", tok_tile(hout.ap, i), h.t[:], r=[h], w=[hout.res[i]])
        S.flush()


def final_phase(S, nc, C, gvec, hin, y):
    with ExitStack() as ph:
        gbc = S.sb("fin_gbc", [128, D], F32, ph)
        ht = [S.sb("fin_ht%d" % k, [128, 2, D], F32, ph) for k in range(2)]
        ot = [S.sb("fin_ot%d" % k, [128, 2, D], F32, ph) for k in range(2)]
        rstd_from_ss(S, nc, C, ph)
        S.dma("sp", gbc.t[:], gvec.partition_broadcast(128), w=[gbc])
        for i in range(NT):
            h = ht[i % 2]
            o = ot[i % 2]
            S.dma("sp", h.t[:], tok_tile(hin.ap, i), r=[hin.res[i]], w=[h])
            for s in range(2):
                col = 2 * i + s
                S.op("dve", (lambda h=h, o=o, s=s, col=col: nc.vector.scalar_tensor_tensor(
                    out=o.t[:, s, :], in0=h.t[:, s, :], scalar=C.rstd.t[:, col:col + 1], in1=gbc.t[:],
                    op0=ALU.mult, op1=ALU.mult)), r=[h, C.rstd, gbc], w=[o])
            S.dma("pool", tok_tile(y.ap, i), o.t[:], r=[o], w=[y.res[i]])
        S.flush()


WEIGHT_SPECS = [
    ("ffn1_norm", [DEPTH, D]), ("ffn1_w_gu", [DEPTH, D, 2 * DFF]), ("ffn1_w_down", [DEPTH, DFF, D]),
    ("mix_norm", [DEPTH, D]), ("w_in", [DEPTH, D, IN_COLS]),
    ("cmp_pos_k", [DEPTH, 32, 64]), ("cmp_pos_v", [DEPTH, 32, 64]),
    ("cmp_k_w1", [DEPTH, 2048, 256]), ("cmp_k_w2", [DEPTH, 256, 64]),
    ("cmp_v_w1", [DEPTH, 2048, 256]), ("cmp_v_w2", [DEPTH, 256, 64]),
    ("hgrn_lower_bound", [DEPTH, 512]), ("hgrn_out_norm", [DEPTH, 128]),
    ("w_out", [DEPTH, D, D]), ("ffn2_norm", [DEPTH, D]), ("ffn2_w_gu", [DEPTH, D, 2 * DFF]),
    ("ffn2_w_down", [DEPTH, DFF, D]), ("final_norm", [D]),
]


def build(stages=None):
    nc = bass.Bass("TRN2", target_bir_lowering=False)
    I = {}
    I["x"] = nc.dram_tensor("x", [SEQ, D], F32, kind="ExternalInput").ap()
    for name, shp in WEIGHT_SPECS:
        I[name] = nc.dram_tensor(name, shp, F32, kind="ExternalInput").ap()
    for name, arr in host_consts().items():
        I[name] = nc.dram_tensor(name, list(arr.shape), F32, kind="ExternalInput").ap()
    yap = nc.dram_tensor("y", [SEQ, D], F32, kind="ExternalOutput").ap()
    ha = nc.dram_tensor("h_a", [SEQ, D], F32, kind="Internal").ap()
    hb = nc.dram_tensor("h_b", [SEQ, D], F32, kind="Internal").ap()
    if stages is None:
        stages = ["f1_0", "mix_0", "f2_0", "f1_1", "mix_1", "f2_1"]
    with ExitStack() as st:
        S = Sched(nc, st)
        S.begin()
        C = Ctx()
        C.I = I
        C.ident = S.sb("ident_sb", [128, 128], BF16)
        C.ss = S.sb("ss_sb", [128, 32], F32)
        C.rstd = S.sb("rstd_sb", [128, 32], F32)
        X = dram_tiles(S, "x", I["x"])
        HA = dram_tiles(S, "ha", ha)
        HB = dram_tiles(S, "hb", hb)
        Y = dram_tiles(S, "y", yap)
        S.dma("pool", C.ident.t[:], I["ident"][:, :], w=[C.ident])
        prologue_ss(S, nc, C, X)
        cur = X
        nxt = HA
        for stg in stages:
            kind, l = stg.split("_")
            l = int(l)
            if kind == "f1":
                ffn_phase(S, nc, C, I["ffn1_w_gu"][l], I["ffn1_w_down"][l], I["ffn1_norm"][l], cur, nxt, stg)
            elif kind == "f2":
                ffn_phase(S, nc, C, I["ffn2_w_gu"][l], I["ffn2_w_down"][l], I["ffn2_norm"][l], cur, nxt, stg)
            else:
                mix_phase(S, nc, C, l, cur, nxt, stg)
            cur = nxt
            nxt = HB if cur is HA else HA
        final_phase(S, nc, C, I["final_norm"], cur, Y)
        S.end()
    return nc


_CONSTS = None


def host_consts():
    global _CONSTS
    if _CONSTS is not None:
        return _CONSTS
    c = {}
    c["ident"] = np.eye(128, dtype=np.float32)
    rm = np.ones((128, TT), np.float32)
    rm[:, ::64] = 0.0
    c["c_rmask"] = rm
    c["c_masku"] = np.triu(np.ones((64, 64), np.float32))
    slopes = np.array(SLOPES, np.float64)
    key = np.arange(SEQ)
    c["c_onehot"] = (key[None, :] // 64 == np.arange(64)[:, None]).astype(np.float32)
    c["c_ones"] = np.ones((1, 1024), np.float32)
    vc1 = np.ones((128, 2), np.float32)
    vc1[127, 1] = 0.0
    c["c_vc1"] = vc1
    cc = np.arange(256)[:, None]
    nn = np.arange(64)[None, :]
    m = ((cc >= 4 * nn) & (cc <= 4 * nn + 3)).astype(np.float32) + ((cc >= 4 * nn - 1) & (cc <= 4 * nn + 2)).astype(np.float32)
    m[255, :] = 0.0
    c["c_mslc"] = m
    q = np.arange(6144)[None, :] - 2048
    dist = q - 16 * np.arange(128)[:, None] - 31
    c["c_dtab"] = np.where(dist >= 0, -dist, -1e7).astype(np.float32)
    pos = np.arange(SEQ)[:, None]
    cur = pos // 64
    blk = np.arange(64)[None, :]
    forced = (blk == 0) | (blk == cur) | (blk == cur - 1)
    c["c_force"] = np.where(forced, 1e6, 0.0).astype(np.float32)
    p = np.arange(128)[:, None, None]
    idx = np.arange(34)[None, None, :]
    c["c_btab"] = (slopes[None, :, None] * (p - 128.0 * (idx - 1))).astype(np.float32).reshape(128, 8 * 34)
    ss_ = np.arange(2)[None, None, :]
    c["c_negb"] = (-BIG - slopes[None, :, None] * (128.0 * ss_ + p)).astype(np.float32).reshape(128, 16)
    c["c_qrow"] = (-slopes[:, None] * np.arange(TT)[None, :]).astype(np.float32)
    kk = np.arange(128)[:, None]
    qq = np.arange(128)[None, :]
    c["c_tri"] = (kk <= qq).astype(np.float32)
    c["c_strict"] = (kk > qq).astype(np.float32)
    _CONSTS = c
    return c


def kernel(stages=None, ncores=4, **inputs):
    nc = build(stages)
    consts = host_consts()
    shared = {k: np.ascontiguousarray(np.asarray(inputs[k], dtype=np.float32)) for k, _ in WEIGHT_SPECS}
    x = np.asarray(inputs["x"], dtype=np.float32)
    in_maps = []
    for b in range(ncores):
        m = dict(shared)
        m.update(consts)
        m["x"] = np.ascontiguousarray(x[b])
        in_maps.append(m)
    res = run_bass_kernel_spmd(nc, in_maps, core_ids=list(range(ncores)))
    out = np.stack([np.asarray(res.results[b]["y"], dtype=np.float32) for b in range(ncores)], axis=0)
    return out


OQ, OKC, OVC, OKS, OVS, OKW, OVW, OGT, OHQ, OHF, OHI, OHG = 0, 512, 640, 768, 896, 1024, 1152, 1280, 1304, 1816, 2328, 2840
BIG = 30000.0
ENABLE_NSA = True
NPT, NPF, NPS = 4, 2, 3
SLOPES = [2.0 ** (-(h + 1)) for h in range(8)]
GELU_C = 1.5957691216057308


def mix_phase(S, nc, C, l, hin, hout, tag):
    global PE_SCALE; PE_SCALE = MIX_PE_SCALE; """h <- h + hybrid_mixer(rmsnorm(h)).  Per 256-token tile: a front end (norm, projections, compress,
    compressed attention, top-k, HGRN2) and a back end (selected/window attention, gating, output projection);
    tile-local buffers are double-buffered so the scheduler can overlap front end i+1 with back end i."""
    I = C.I
    with ExitStack() as ph:
        def sb(name, shape, dt):
            return S.sb(tag + name, shape, dt, ph)

        win = sb("win", [128, 8, IN_COLS], BF16)
        wout = sb("wout", [128, 8, D], BF16)
        gcol = sb("gcol", [128, 8], F32)
        ht = sb("ht", [128, 2, D], F32)
        hn = sb("hn", [128, D], BF16)
        hnT = sb("hnT", [128, 8, TT], BF16)
        oT = [sb("oT%d" % k, [128, 8, TT], BF16) for k in range(2)]
        hres = [sb("hres%d" % k, [128, 512], F32) for k in range(2)]
        ssh = sb("ssh", [128, 2], F32)
        lbt = sb("lbt", [128, 4], F32)
        omlt = sb("omlt", [128, 4], F32)
        nomlt = sb("nomlt", [128, 4], F32)
        lbtmp = sb("lbtmp", [128, 8], F32)
        rmask = sb("rmask", [128, TT], BF16)
        maskU = sb("maskU", [64, 64], F32)
        onbc = sb("onbc", [64, 512], F32)
        hv = sb("hv", [64, 4, 512], BF16)
        gn = sb("gn", [64, 4, 512], BF16)
        state = sb("state", [128, 4, 128], F32)
        statebf = sb("statebf", [128, 4, 128], BF16)
        osb = [sb("osb%d" % k, [64, 4, 128], F32) for k in range(2)]
        ssq = [sb("ssq%d" % k, [64, 4], F32) for k in range(2)]
        rsq = [sb("rsq%d" % k, [64, 4], F32) for k in range(2)]
        ohg = [sb("ohg%d" % k, [64, 128], BF16) for k in range(2)]
        ef = [sb("ef%d" % k, [128, TT], F32) for k in range(6)]
        sig2 = sb("sig2", [128, 2 * TT], F32)
        eb = [sb("eb%d" % k, [128, TT], BF16) for k in range(4)]
        aTm = [sb("aTm%d" % k, [64, 64], BF16) for k in range(4)]
        khT = sb("khT", [64, 512], BF16)
        junkh = sb("junkh", [64, 128], BF16)

        pt = S.ps(tag + "pt", [128, 1024], BF16, ph)
        poolF = [S.ps(tag + "pf%d" % k, [128, 512], F32, ph) for k in range(NPF)]
        poolS = [S.ps(tag + "psc%d" % k, [128, 512], F32, ph) for k in range(NPS)]
        acc = [S.ps(tag + "acc%d" % k, [128, 512], F32, ph) for k in range(2)]
        pcnt = [0, 0]

        def nb():
            pcnt[0] += 1
            return poolF[pcnt[0] % len(poolF)]

        def nbS():
            pcnt[1] += 1
            return poolS[pcnt[1] % len(poolS)]

        rstd_from_ss(S, nc, C, ph)
        wgb = [0, OKC, OHQ, IN_COLS]
        winR = [[S.res(tag + "winR%d_%d" % (k, kc)) for kc in range(8)] for k in range(3)]
        for cg in range(3):
            for kc in range(8):
                S.dma("pool", win.t[:, kc, wgb[cg]:wgb[cg + 1]], I["w_in"][l, kc * 128:(kc + 1) * 128, wgb[cg]:wgb[cg + 1]],
                      w=[winR[cg][kc]], cost=4.0)
        S.dma("pool", wout.t[:], I["w_out"][l].rearrange("(c p) n -> p c n", p=128), w=[wout], cost=10.0)
        S.dma("sp", gcol.t[:], I["mix_norm"][l].rearrange("(c p) -> p c", p=128), w=[gcol], allow_slow_non_contiguous=True)
        for cg in range(3):
            for kc in range(8):
                e_ = "dve" if kc % 2 == 0 else "pool"
                eng_ = nc.vector if kc % 2 == 0 else nc.gpsimd
                S.op(e_, (lambda kc=kc, eng_=eng_, cg=cg: eng_.tensor_scalar(
                    out=win.t[:, kc, wgb[cg]:wgb[cg + 1]], in0=win.t[:, kc, wgb[cg]:wgb[cg + 1]],
                    scalar1=gcol.t[:, kc:kc + 1], scalar2=None, op0=ALU.mult)),
                    r=[gcol, winR[cg][kc]], w=[winR[cg][kc]], cost=0.3 + 0.001 * (wgb[cg + 1] - wgb[cg]))
        S.dma("pool", rmask.t[:], I["c_rmask"][:, :], w=[rmask])
        S.dma("sp", maskU.t[:], I["c_masku"][:, :], w=[maskU])
        for hh in range(4):
            S.dma("sp", onbc.t[:, hh * 128:(hh + 1) * 128], I["hgrn_out_norm"][l].partition_broadcast(64), w=[onbc])
        S.dma("sp", lbtmp.t[:, 0:4], I["hgrn_lower_bound"][0].rearrange("(h d) -> d h", d=128), w=[lbtmp],
              allow_slow_non_contiguous=True)
        S.dma("sp", lbtmp.t[:, 4:8], I["hgrn_lower_bound"][1].rearrange("(h d) -> d h", d=128), w=[lbtmp],
              allow_slow_non_contiguous=True)
        if l == 0:
            S.op("dve", lambda: nc.vector.memset(lbt.t[:], 0.0), w=[lbt])
        else:
            S.op("dve", lambda: nc.vector.tensor_tensor(out=lbtmp.t[:, 0:4], in0=lbtmp.t[:, 4:8], in1=lbtmp.t[:, 0:4],
                                                        op=ALU.subtract), r=[lbtmp], w=[lbtmp])
            S.op("act", lambda: nc.scalar.activation(out=lbt.t[:], in_=lbtmp.t[:, 0:4], func=AF.Tanh, scale=0.5),
                 r=[lbtmp], w=[lbt], tbl="exp")
            S.op("dve", lambda: nc.vector.tensor_scalar(out=lbt.t[:], in0=lbt.t[:], scalar1=0.5, scalar2=0.5,
                                                        op0=ALU.mult, op1=ALU.add), r=[lbt], w=[lbt])
        S.op("dve", lambda: nc.vector.tensor_scalar(out=omlt.t[:], in0=lbt.t[:], scalar1=-1.0, scalar2=1.0,
                                                    op0=ALU.mult, op1=ALU.add), r=[lbt], w=[omlt])
        S.op("dve", lambda: nc.vector.tensor_scalar(out=nomlt.t[:], in0=omlt.t[:], scalar1=-1.0, scalar2=None,
                                                    op0=ALU.mult), r=[omlt], w=[nomlt])
        S.op("dve", lambda: nc.vector.memset(state.t[:], 0.0), w=[state])
        S.op("dve", lambda: nc.vector.memset(statebf.t[:], 0.0), w=[statebf])

        NS = None
        if ENABLE_NSA:
            NS = nsa_setup(S, nc, C, l, tag, ph, sb)

        def proj_fm(bank, c0, c1, col0, ncols):
            for kc in range(8):
                S.op("pe", (lambda kc=kc: nc.tensor.matmul(
                    bank.t[0:ncols, c0:c1], lhsT=win.t[:, kc, col0:col0 + ncols], rhs=hnT.t[:, kc, :],
                    start=(kc == 0), stop=(kc == 7))), r=[winR[0 if col0 < OKC else (1 if col0 < OHQ else 2)][kc], hnT], w=[bank], cost=0.12)

        def proj_tm(bank, t0, nt, col0, ncols, o0=0):
            for kc in range(8):
                S.op("pe", (lambda kc=kc: nc.tensor.matmul(
                    bank.t[0:nt, o0:o0 + ncols], lhsT=hnT.t[:, kc, t0:t0 + nt], rhs=win.t[:, kc, col0:col0 + ncols],
                    start=(kc == 0), stop=(kc == 7))), r=[winR[0 if col0 < OKC else (1 if col0 < OHQ else 2)][kc], hnT], w=[bank], cost=0.08 + ncols * 0.00045)

        for i in range(NT):
            par = i % 2
            oTi = oT[par]
            S.dma("sp", ht.t[:], tok_tile(hin.ap, i), r=[hin.res[i]], w=[ht])
            for s in range(2):
                col = 2 * i + s
                S.op("dve", (lambda s=s, col=col: nc.vector.tensor_scalar(
                    out=hn.t[:], in0=ht.t[:, s, :], scalar1=C.rstd.t[:, col:col + 1], scalar2=None, op0=ALU.mult)),
                    r=[ht, C.rstd], w=[hn], cost=0.6)
                for kc in range(8):
                    S.op("pe", (lambda kc=kc: nc.tensor.transpose(
                        pt.t[:, kc * 128:(kc + 1) * 128], hn.t[:, kc * 128:(kc + 1) * 128], C.ident.t[:])),
                        r=[hn, C.ident], w=[pt], cost=0.08)
                S.op("act", (lambda s=s: nc.scalar.copy(
                    out=hnT.t[:, :, s * 128:(s + 1) * 128], in_=pt.t[:].rearrange("p (k t) -> p k t", k=8))),
                    r=[pt], w=[hnT], cost=1.0)

            if ENABLE_NSA:
                nsa_front(S, nc, C, l, i, NS, win, hnT, pt, nb, acc, proj_fm, proj_tm)

            sgl = sig2
            for c in range(4):
                b1 = nb()
                proj_tm(b1, c * 64, 64, OHI, 512)
                S.op("act", (lambda b1=b1, c=c: nc.scalar.copy(out=hv.t[:, c, :], in_=b1.t[0:64, :])), r=[b1], w=[hv])
                b2 = nb()
                proj_tm(b2, c * 64, 64, OHG, 512)
                S.op("act", (lambda b2=b2: nc.scalar.activation(out=sgl.t[0:64, :], in_=b2.t[0:64, :], func=AF.Tanh, scale=0.5)),
                     r=[b2], w=[sgl], tbl="exp")
                S.op("pool", lambda: nc.gpsimd.tensor_scalar(out=sgl.t[0:64, :], in0=sgl.t[0:64, :], scalar1=0.5, scalar2=0.5,
                                                             op0=ALU.mult, op1=ALU.add), r=[sgl], w=[sgl], cost=0.6)
                S.op("dve", (lambda b2=b2: nc.vector.tensor_tensor(out=sgl.t[0:64, :], in0=sgl.t[0:64, :], in1=b2.t[0:64, :],
                                                                   op=ALU.mult)), r=[sgl, b2], w=[sgl], cost=0.55)
                S.op("pool", (lambda c=c: nc.gpsimd.tensor_tensor(out=gn.t[:, c, :], in0=sgl.t[0:64, :], in1=onbc.t[:],
                                                                  op=ALU.mult)), r=[sgl, onbc], w=[gn], cost=0.8)
            for hh in range(4):
                zq = nb()
                proj_fm(zq, 0, TT, OHF + hh * 128, 128)
                proj_fm(zq, TT, 2 * TT, OHQ + hh * 128, 128)
                f, logf, kf, b, bm, bl = ef
                e1, e2, e3, e4 = ef[0], ef[1], ef[4], ef[5]
                qt, kt, qd, kh = eb
                ob, sq_, rs_ = osb[hh % 2], ssq[hh % 2], rsq[hh % 2]
                S.op("act", (lambda zq=zq: nc.scalar.activation(out=sig2.t[:], in_=zq.t[:, 0:2 * TT], func=AF.Tanh, scale=0.5)),
                     r=[zq], w=[sig2], tbl="exp", cost=0.6)
                S.op("pool", lambda: nc.gpsimd.tensor_scalar(out=sig2.t[:], in0=sig2.t[:], scalar1=0.5, scalar2=0.5,
                                                             op0=ALU.mult, op1=ALU.add), r=[sig2], w=[sig2], cost=0.9)
                S.op("dve", (lambda zq=zq: nc.vector.tensor_tensor(out=sig2.t[:, TT:2 * TT], in0=sig2.t[:, TT:2 * TT],
                                                                   in1=zq.t[:, TT:2 * TT], op=ALU.mult)),
                     r=[zq, sig2], w=[sig2])
                S.op("dve", (lambda hh=hh: nc.vector.tensor_scalar(
                    out=f.t[:], in0=sig2.t[:, 0:TT], scalar1=omlt.t[:, hh:hh + 1], scalar2=lbt.t[:, hh:hh + 1],
                    op0=ALU.mult, op1=ALU.add)), r=[sig2, omlt, lbt], w=[f])
                S.op("act", lambda: nc.scalar.activation(out=logf.t[:], in_=f.t[:], func=AF.Ln), r=[f], w=[logf], tbl="exp")
                S.op("pool", (lambda hh=hh: nc.gpsimd.tensor_scalar(
                    out=kf.t[:], in0=sig2.t[:, 0:TT], scalar1=nomlt.t[:, hh:hh + 1], scalar2=omlt.t[:, hh:hh + 1],
                    op0=ALU.mult, op1=ALU.add)), r=[sig2, omlt, nomlt], w=[kf])
                S.op("dve", lambda: nc.vector.tensor_tensor_scan(out=b.t[:], data0=rmask.t[:], data1=logf.t[:],
                                                                 initial=0.0, op0=ALU.mult, op1=ALU.add),
                     r=[rmask, logf], w=[b], cost=0.6)
                b3 = b.t[:].rearrange("p (c t) -> p c t", c=4)
                S.op("dve", lambda: nc.vector.tensor_tensor(
                    out=bm.t[:].rearrange("p (c t) -> p c t", c=4), in0=b3,
                    in1=b3[:, :, 31:32].to_broadcast([128, 4, 64]), op=ALU.subtract), r=[b], w=[bm], cost=0.55)
                S.op("dve", lambda: nc.vector.tensor_tensor(
                    out=bl.t[:].rearrange("p (c t) -> p c t", c=4), in0=b3,
                    in1=b3[:, :, 63:64].to_broadcast([128, 4, 64]), op=ALU.subtract), r=[b], w=[bl], cost=0.55)
                S.op("act", lambda: nc.scalar.activation(out=e1.t[:], in_=bm.t[:], func=AF.Exp), r=[bm], w=[e1], tbl="exp")
                S.op("act", lambda: nc.scalar.activation(out=e2.t[:], in_=bm.t[:], func=AF.Exp, scale=-1.0),
                     r=[bm], w=[e2], tbl="exp")
                S.op("act", lambda: nc.scalar.activation(out=e4.t[:], in_=bl.t[:], func=AF.Exp, scale=-1.0),
                     r=[bl], w=[e4], tbl="exp")
                S.op("act", lambda: nc.scalar.activation(out=e3.t[:], in_=b.t[:], func=AF.Exp), r=[b], w=[e3], tbl="exp")
                S.op("pool", lambda: nc.gpsimd.tensor_tensor(out=qt.t[:], in0=sig2.t[:, TT:2 * TT], in1=e1.t[:], op=ALU.mult),
                     r=[sig2, e1], w=[qt])
                S.op("pool", lambda: nc.gpsimd.tensor_tensor(out=kt.t[:], in0=kf.t[:], in1=e2.t[:], op=ALU.mult),
                     r=[kf, e2], w=[kt])
                S.op("pool", lambda: nc.gpsimd.tensor_tensor(out=kh.t[:], in0=kf.t[:], in1=e4.t[:], op=ALU.mult),
                     r=[kf, e4], w=[kh])
                S.op("pool", lambda: nc.gpsimd.tensor_tensor(out=qd.t[:], in0=sig2.t[:, TT:2 * TT], in1=e3.t[:], op=ALU.mult),
                     r=[sig2, e3], w=[qd])
                for c in range(4):
                    cs = slice(c * 64, (c + 1) * 64)
                    ba = nb()
                    S.op("pe", (lambda ba=ba, cs=cs: nc.tensor.matmul(ba.t[0:64, 0:64], lhsT=kt.t[:, cs], rhs=qt.t[:, cs],
                                                                      start=True, stop=True)), r=[kt, qt], w=[ba], cost=0.07)
                    am = aTm[c]
                    S.op("dve", (lambda ba=ba, am=am: nc.vector.tensor_tensor(out=am.t[:], in0=ba.t[0:64, 0:64],
                                                                              in1=maskU.t[:], op=ALU.mult)),
                         r=[ba, maskU], w=[am], cost=0.12)
                    S.op("pe", (lambda cs=cs, c=c: nc.tensor.transpose(pt.t[0:64, c * 128:(c + 1) * 128], kh.t[:, cs],
                                                                       C.ident.t[:])), r=[kh, C.ident], w=[pt], cost=0.08)
                S.op("dve", lambda: nc.vector.tensor_copy(out=khT.t[:], in_=pt.t[0:64, 0:512]), r=[pt], w=[khT], cost=0.3)
                for c in range(4):
                    cs = slice(c * 64, (c + 1) * 64)
                    am = aTm[c]
                    bo = nb()
                    S.op("pe", (lambda bo=bo, am=am, c=c, hh=hh: nc.tensor.matmul(
                        bo.t[0:64, 0:128], lhsT=am.t[:], rhs=hv.t[:, c, hh * 128:(hh + 1) * 128],
                        start=True, stop=False)), r=[am, hv], w=[bo], cost=0.08)
                    S.op("pe", (lambda bo=bo, cs=cs, hh=hh: nc.tensor.matmul(
                        bo.t[0:64, 0:128], lhsT=qd.t[:, cs], rhs=statebf.t[:, hh, :],
                        start=False, stop=True)), r=[qd, statebf], w=[bo], cost=0.08)
                    S.op("pe", (lambda bo=bo, c=c, hh=hh: nc.tensor.matmul(
                        bo.t[:, 128:256], lhsT=khT.t[:, c * 128:(c + 1) * 128],
                        rhs=hv.t[:, c, hh * 128:(hh + 1) * 128], start=True, stop=True)), r=[khT, hv], w=[bo], cost=0.08)
                    S.op("dve", (lambda bo=bo, c=c, hh=hh: nc.vector.scalar_tensor_tensor(
                        out=state.t[:, hh, :], in0=state.t[:, hh, :], scalar=e3.t[:, c * 64 + 63:c * 64 + 64],
                        in1=bo.t[:, 128:256], op0=ALU.mult, op1=ALU.add)), r=[state, e3, bo], w=[state], cost=0.2)
                    S.op("pool", (lambda hh=hh: nc.gpsimd.tensor_copy(out=statebf.t[:, hh, :], in_=state.t[:, hh, :])),
                         r=[state], w=[statebf], cost=0.25)
                    S.op("act", (lambda bo=bo, c=c, ob=ob: nc.scalar.copy(out=ob.t[:, c, :], in_=bo.t[0:64, 0:128])),
                         r=[bo], w=[ob], cost=0.25)
                    S.op("dve", (lambda c=c, ob=ob, sq_=sq_: nc.vector.scalar_tensor_tensor(
                        out=junkh.t[:], in0=ob.t[:, c, :], scalar=1.0, in1=ob.t[:, c, :],
                        op0=ALU.mult, op1=ALU.mult, accum_out=sq_.t[:, c:c + 1])), r=[ob], w=[junkh, sq_], cost=0.25)
                S.op("dve", (lambda sq_=sq_, rs_=rs_: nc.vector.tensor_scalar(
                    out=rs_.t[:], in0=sq_.t[:], scalar1=1.0 / 128, scalar2=EPS, op0=ALU.mult, op1=ALU.add)),
                    r=[sq_], w=[rs_], cost=0.1)
                S.op("act", (lambda rs_=rs_: nc.scalar.activation(out=rs_.t[:], in_=rs_.t[:], func=AF.Ln)),
                     r=[rs_], w=[rs_], tbl="exp", cost=0.2)
                S.op("act", (lambda rs_=rs_: nc.scalar.activation(out=rs_.t[:], in_=rs_.t[:], func=AF.Exp, scale=-0.5)),
                     r=[rs_], w=[rs_], tbl="exp", cost=0.2)
                for c in range(4):
                    og = ohg[c % 2]
                    S.op("dve", (lambda og=og, c=c, hh=hh, ob=ob, rs_=rs_: nc.vector.scalar_tensor_tensor(
                        out=og.t[:], in0=ob.t[:, c, :], scalar=rs_.t[:, c:c + 1],
                        in1=gn.t[:, c, hh * 128:(hh + 1) * 128], op0=ALU.mult, op1=ALU.mult)),
                        r=[ob, rs_, gn], w=[og], cost=0.15)
                    S.op("pe", (lambda og=og, c=c: nc.tensor.transpose(
                        pt.t[:, c * 64:(c + 1) * 64], og.t[:], C.ident.t[0:64, 0:64])),
                        r=[og, C.ident], w=[pt], cost=0.08)
                S.op("act", (lambda hh=hh, oTi=oTi: nc.scalar.copy(out=oTi.t[:, 4 + hh, :], in_=pt.t[:, 0:TT])),
                     r=[pt], w=[oTi], cost=0.3)

            if ENABLE_NSA:
                nsa_back(S, nc, C, l, i, NS, oTi, pt, nbS, acc)
            else:
                S.op("dve", (lambda oTi=oTi: nc.vector.memset(oTi.t[:, 0:4, :], 0.0)), w=[oTi])

            ek = 0
            for s in range(2):
                for half in range(2):
                    hr = hres[ek % 2]
                    ek += 1
                    rows = slice(i * TT + s * 128, i * TT + (s + 1) * 128)
                    cols = slice(half * 512, (half + 1) * 512)
                    S.dma("sp", hr.t[:], hin.ap[rows, cols], r=[hin.res[i]], w=[hr])
                    dp = nbS()
                    for kc in range(8):
                        S.op("pe", (lambda dp=dp, kc=kc, s=s, half=half, oTi=oTi: nc.tensor.matmul(
                            dp.t[:, :], lhsT=oTi.t[:, kc, s * 128:(s + 1) * 128],
                            rhs=wout.t[:, kc, half * 512:(half + 1) * 512],
                            start=(kc == 0), stop=(kc == 7))), r=[oTi, wout], w=[dp], cost=0.23)
                    S.op("dve", (lambda dp=dp, hr=hr: nc.vector.tensor_tensor(
                        out=hr.t[:], in0=dp.t[:, :], in1=hr.t[:], op=ALU.add)), r=[dp, hr], w=[hr], cost=0.55)
                    jk = NS.onsa if ENABLE_NSA else hn
                    jk_ap = jk.t[:, 0, :] if ENABLE_NSA else jk.t[:, 0:512]
                    S.op("dve", (lambda hr=hr, half=half, jk_ap=jk_ap: nc.vector.scalar_tensor_tensor(
                        out=jk_ap, in0=hr.t[:], scalar=1.0, in1=hr.t[:],
                        op0=ALU.mult, op1=ALU.mult, accum_out=ssh.t[:, half:half + 1])), r=[hr], w=[jk, ssh], cost=0.6)
                    S.dma("sp", hout.ap[rows, cols], hr.t[:], r=[hr], w=[hout.res[i]])
                col = 2 * i + s
                S.op("dve", (lambda col=col: nc.vector.tensor_tensor(
                    out=C.ss.t[:, col:col + 1], in0=ssh.t[:, 0:1], in1=ssh.t[:, 1:2], op=ALU.add)),
                    r=[ssh], w=[C.ss], cost=0.1)
        S.flush()


def nsa_setup(S, nc, C, l, tag, ph, sb):
    I = C.I
    N = Ctx()
    N.KS = [sb("KS%d" % g, [128, SEQ], BF16) for g in range(2)]
    N.KSr = [[S.res("KSr%d_%d" % (g, k)) for k in range(32)] for g in range(2)]
    N.KW = [sb("KW%d" % g, [128, 1024], BF16) for g in range(2)]
    N.KWr = [[S.res("KWr%d_%d" % (g, k)) for k in range(8)] for g in range(2)]
    N.VS = sb("VS", [128, 2, 32, 65], BF16)
    N.VSr = [S.res("VSr%d" % k) for k in range(32)]
    N.VW = sb("VW", [128, 2, 8, 65], BF16)
    N.VWr = [S.res("VWr%d" % k) for k in range(8)]
    N.KcT = [sb("KcT%d" % g, [64, 256], BF16) for g in range(2)]
    N.VcA = sb("VcA", [128, 2, 2, 65], BF16)
    N.Mslc = sb("Mslc", [128, 2, 64], BF16)
    N.XC = [[sb("XC%d%d" % (kv, g), [128, 272], BF16) for g in range(2)] for kv in range(2)]
    N.cw1 = [sb("cw1_%d" % kv, [128, 16, 256], BF16) for kv in range(2)]
    N.cw2 = [sb("cw2_%d" % kv, [128, 2, 64], BF16) for kv in range(2)]
    N.posT = [sb("posT%d" % kv, [128, 16], BF16) for kv in range(2)]
    N.hb = [sb("hb%d" % kv, [128, 2], F32) for kv in range(2)]
    N.Dq = sb("Dq", [128, 2, TT], F32)
    N.force = sb("force", [128, 2, 64], F32)
    N.btab = sb("btab", [128, 8, 34], F32)
    N.negb = sb("negb", [128, 16], F32)
    N.QS = [[sb("QS%d_%d" % (p, h), [128, TT], BF16) for h in range(8)] for p in range(2)]
    N.tri = sb("tri", [128, 128], BF16)
    N.strict = sb("strict", [128, 128], BF16)
    N.gsig = [sb("gsig%d" % p, [128, 2, 24], F32) for p in range(2)]
    N.ocmp = [sb("ocmp%d" % p, [128, 2, 8, 64], F32) for p in range(2)]
    N.imp = sb("imp", [128, 2, 2, 64], F32)
    N.sbs = [sb("sbs%d" % k, [128, TT], F32) for k in range(1)]
    N.PT = [sb("PT%d" % k, [128, TT], BF16) for k in range(NPT)]
    N.PTc = [sb("PTc%d" % k, [128, TT], BF16) for k in range(2)]
    N.onsa = sb("onsa", [128, 2, 512], BF16)
    N.h1pad = sb("h1pad", [128, 2, 256], BF16)
    N.h1g = sb("h1g", [128, 2, 16], BF16)
    N.gx = sb("gx", [128, 32], F32)
    N.gt = sb("gt", [128, 32], F32)
    N.gs = sb("gs", [128, 32], F32)
    N.Mq = [sb("Mq%d" % k, [128, 128], BF16) for k in range(2)]
    N.scr = sb("scr", [128, 64], F32)
    N.scr2 = sb("scr2", [128, 64], F32)
    N.m8a = sb("m8a", [128, 8], F32)
    N.m8b = sb("m8b", [128, 8], F32)
    N.msk = sb("msk", [128, 64], F32)
    N.rzc = [sb("rzc%d" % k, [128, 2], F32) for k in range(2)]
    N.rz = [[sb("rz%d_%d" % (k, s), [128, 4], F32) for s in range(2)] for k in range(2)]
    N.fo = [[sb("fo%d_%d" % (k, s), [128, 64], F32) for s in range(2)] for k in range(2)]
    N.ptcnt = 0
    N.pccnt = 0
    N.sbcnt = 0

    for g in range(2):
        S.dma("pool", N.KS[g].t[64:128, :], I["c_onehot"][:, :], w=N.KSr[g], cost=8.0)
        S.op("pool", (lambda g=g: nc.gpsimd.memset(N.KW[g].t[:], 0.0)), w=N.KWr[g], cost=1.0)
        S.dma("pool", N.KW[g].t[64:65, :], I["c_ones"][:, :], w=N.KWr[g])
        S.op("pool", (lambda g=g: nc.gpsimd.memset(N.KcT[g].t[:], 0.0)), w=[N.KcT[g]])
        for kv in range(2):
            S.op("pool", (lambda g=g, kv=kv: nc.gpsimd.memset(N.XC[kv][g].t[:], 0.0)), w=[N.XC[kv][g]])
    S.op("pool", lambda: nc.gpsimd.memset(N.VS.t[:], 1.0), w=N.VSr, cost=4.0)
    S.op("pool", lambda: nc.gpsimd.memset(N.VW.t[:], 1.0), w=N.VWr, cost=1.0)
    S.op("pool", lambda: nc.gpsimd.memset(N.VcA.t[:], 0.0), w=[N.VcA])
    S.op("pool", lambda: nc.gpsimd.memset(N.h1pad.t[:], 0.0), w=[N.h1pad])
    for k in range(2):
        S.op("pool", (lambda k=k: nc.gpsimd.memset(N.Mq[k].t[:], 0.0)), w=[N.Mq[k]])
    for g in range(2):
        S.dma("pool", N.VcA.t[:, g, :, 64], I["c_vc1"][:, :], w=[N.VcA], allow_slow_non_contiguous=True)
    S.dma("pool", N.Mslc.t[:], I["c_mslc"].rearrange("(t p) n -> p t n", p=128), w=[N.Mslc])
    for kv in range(2):
        w1 = I["cmp_k_w1" if kv == 0 else "cmp_v_w1"][l]
        w2 = I["cmp_k_w2" if kv == 0 else "cmp_v_w2"][l]
        pe_ = I["cmp_pos_k" if kv == 0 else "cmp_pos_v"][l]
        S.dma("pool", N.cw1[kv].t[:], w1.rearrange("(c p) h -> p c h", p=128), w=[N.cw1[kv]], cost=8.0)
        S.dma("pool", N.cw2[kv].t[:], w2.rearrange("(c p) n -> p c n", p=128), w=[N.cw2[kv]])
        S.dma("pool", N.posT[kv].t[:], pe_.rearrange("(c two) d -> (two d) c", two=2), w=[N.posT[kv]],
              allow_slow_non_contiguous=True)
    S.dma("sp", N.btab.t[:], I["c_btab"].rearrange("p (h k) -> p h k", h=8), w=[N.btab])
    S.dma("sp", N.negb.t[:], I["c_negb"][:, :], w=[N.negb])
    S.dma("pool", N.tri.t[:], I["c_tri"][:, :], w=[N.tri])
    S.dma("pool", N.strict.t[:], I["c_strict"][:, :], w=[N.strict])
    return N


def nsa_hidden_bias(S, nc, N, nb):
    for kv in range(2):
        bank = nb()
        for half in range(2):
            for c in range(16):
                S.op("pe", (lambda kv=kv, half=half, c=c, bank=bank: nc.tensor.matmul(
                    bank.t[:, half:half + 1], lhsT=N.cw1[kv].t[:, c, half * 128:(half + 1) * 128],
                    rhs=N.posT[kv].t[:, c:c + 1], start=(c == 0), stop=(c == 15))),
                    r=[N.cw1[kv], N.posT[kv]], w=[bank], cost=0.07)
        S.op("dve", (lambda kv=kv, bank=bank: nc.vector.tensor_copy(out=N.hb[kv].t[:], in_=bank.t[:, 0:2])),
             r=[bank], w=[N.hb[kv]])


def nsa_front(S, nc, C, l, i, N, win, hnT, pt, nb, acc, proj_fm, proj_tm):
    I = C.I
    q0 = i * TT
    par = i % 2
    QS = N.QS[par]
    gsig = N.gsig[par]
    ocmp = N.ocmp[par]
    if i == 0:
        nsa_hidden_bias(S, nc, N, nb)

    S.dma("sp", N.Dq.t[:, 0, :], I["c_dtab"][:, 2048 + q0:2048 + q0 + TT], w=[N.Dq])
    S.dma("sp", N.Dq.t[:, 1, :], I["c_dtab"][:, q0:q0 + TT], w=[N.Dq])
    S.dma("sp", N.force.t[:], I["c_force"][q0:q0 + TT, :].rearrange("(s p) n -> p s n", p=128), w=[N.force])

    for hp in range(4):
        bank = nb()
        for k in range(2):
            proj_fm(bank, k * TT, (k + 1) * TT, OQ + 64 * (2 * hp + k), 64)
        for k in range(2):
            hq = 2 * hp + k
            if k == 0:
                S.op("act", (lambda bank=bank, k=k, hq=hq: nc.scalar.activation(
                    out=QS[hq].t[0:64, :], in_=bank.t[0:64, k * TT:(k + 1) * TT], func=AF.Copy, scale=0.125)),
                    r=[bank], w=[QS[hq]], cost=0.3)
            else:
                S.op("dve", (lambda bank=bank, k=k, hq=hq: nc.vector.tensor_scalar(
                    out=QS[hq].t[0:64, :], in0=bank.t[0:64, k * TT:(k + 1) * TT], scalar1=0.125, scalar2=None,
                    op0=ALU.mult)), r=[bank], w=[QS[hq]], cost=0.3)
    ws = ((2 * i) % 8) * 128
    for g in range(2):
        bank = nb()
        proj_fm(bank, 0, TT, OKS + 64 * g, 64)
        proj_fm(bank, TT, 2 * TT, OKW + 64 * g, 64)
        S.op("act", (lambda bank=bank, g=g: nc.scalar.copy(out=N.KS[g].t[0:64, q0:q0 + TT], in_=bank.t[0:64, 0:TT])),
             r=[bank], w=[N.KSr[g][2 * i], N.KSr[g][2 * i + 1]], cost=0.3)
        S.op("dve", (lambda bank=bank, g=g: nc.vector.tensor_copy(out=N.KW[g].t[0:64, ws:ws + TT],
                                                                  in_=bank.t[0:64, TT:2 * TT])),
             r=[bank], w=[N.KWr[g][(2 * i) % 8], N.KWr[g][(2 * i + 1) % 8]], cost=0.3)
    for g in range(2):
        bank = nb()
        proj_fm(bank, 0, TT, OKC + 64 * g, 64)
        proj_fm(bank, TT, 2 * TT, OVC + 64 * g, 64)
        for kv in range(2):
            xc = N.XC[kv][g]
            if kv == 0:
                S.op("act", (lambda bank=bank, xc=xc: nc.scalar.copy(out=xc.t[0:64, 16:272], in_=bank.t[0:64, 0:TT])),
                     r=[bank], w=[xc], cost=0.3)
            else:
                S.op("dve", (lambda bank=bank, xc=xc: nc.vector.tensor_copy(out=xc.t[0:64, 16:272],
                                                                            in_=bank.t[0:64, TT:2 * TT])),
                     r=[bank], w=[xc], cost=0.3)
            S.dma("sp", xc.t[64:128, 15:271], xc.t[0:64, 16:272], r=[xc], w=[xc], cost=2.5)
    for s in range(2):
        bank = nb()
        proj_tm(bank, s * 128, 128, OVS, 128, o0=0)
        proj_tm(bank, s * 128, 128, OVW, 128, o0=128)
        proj_tm(bank, s * 128, 128, OGT, 24, o0=256)
        kt = 2 * i + s
        S.op("act", (lambda bank=bank, kt=kt: nc.scalar.copy(
            out=N.VS.t[:, :, kt, 0:64], in_=bank.t[:, 0:128].rearrange("p (g d) -> p g d", g=2))),
            r=[bank], w=[N.VSr[kt]], cost=0.25)
        S.op("dve", (lambda bank=bank, kt=kt: nc.vector.tensor_copy(
            out=N.VW.t[:, :, kt % 8, 0:64], in_=bank.t[:, 128:256].rearrange("p (g d) -> p g d", g=2))),
            r=[bank], w=[N.VWr[kt % 8]], cost=0.2)
        S.op("act", (lambda bank=bank, s=s: nc.scalar.activation(out=gsig.t[:, s, :], in_=bank.t[:, 256:280],
                                                                 func=AF.Tanh, scale=0.5)), r=[bank], w=[gsig], tbl="exp", cost=0.2)
        S.op("dve", (lambda s=s: nc.vector.tensor_scalar(out=gsig.t[:, s, :], in0=gsig.t[:, s, :], scalar1=0.5, scalar2=0.5,
                                                         op0=ALU.mult, op1=ALU.add)), r=[gsig], w=[gsig], cost=0.1)
    nb0 = 16 * i - 1
    m0 = 1 if i == 0 else 0
    for kv in range(2):
        for g in range(2):
            xc = N.XC[kv][g]
            xv = xc.t[:, 0:272].rearrange("p (m t) -> p t m", t=16)
            bank = nb()
            for half in range(2):
                for c in range(16):
                    S.op("pe", (lambda bank=bank, half=half, c=c, kv=kv, xv=xv: nc.tensor.matmul(
                        bank.t[:, half * 16:(half + 1) * 16], lhsT=N.cw1[kv].t[:, c, half * 128:(half + 1) * 128],
                        rhs=xv[:, (2 * c) % 16, (2 * c) // 16:(2 * c) // 16 + 16], start=(c == 0), stop=(c == 15))),
                        r=[N.cw1[kv], xc], w=[bank], cost=0.08)
            for half in range(2):
                S.op("dve", (lambda bank=bank, half=half, kv=kv: nc.vector.tensor_scalar(
                    out=N.gx.t[:, half * 16:(half + 1) * 16], in0=bank.t[:, half * 16:(half + 1) * 16],
                    scalar1=N.hb[kv].t[:, half:half + 1], scalar2=None, op0=ALU.add)), r=[bank, N.hb[kv]], w=[N.gx],
                    cost=0.1)
            S.op("dve", lambda: nc.vector.tensor_tensor(out=N.gt.t[:], in0=N.gx.t[:], in1=N.gx.t[:], op=ALU.mult),
                 r=[N.gx], w=[N.gt], cost=0.1)
            S.op("dve", lambda: nc.vector.tensor_scalar(out=N.gt.t[:], in0=N.gt.t[:], scalar1=0.044715, scalar2=1.0,
                                                        op0=ALU.mult, op1=ALU.add), r=[N.gt], w=[N.gt], cost=0.1)
            S.op("dve", lambda: nc.vector.tensor_tensor(out=N.gt.t[:], in0=N.gt.t[:], in1=N.gx.t[:], op=ALU.mult),
                 r=[N.gt, N.gx], w=[N.gt], cost=0.1)
            S.op("act", lambda: nc.scalar.activation(out=N.gs.t[:], in_=N.gt.t[:], func=AF.Tanh, scale=0.5 * GELU_C),
                 r=[N.gt], w=[N.gs], tbl="exp", cost=0.2)
            S.op("dve", lambda: nc.vector.tensor_scalar(out=N.gs.t[:], in0=N.gs.t[:], scalar1=0.5, scalar2=0.5,
                                                        op0=ALU.mult, op1=ALU.add), r=[N.gs], w=[N.gs], cost=0.1)
            if kv == 0:
                S.op("dve", lambda: nc.vector.tensor_tensor(
                    out=N.h1g.t[:].rearrange("p a b -> p (a b)"), in0=N.gx.t[:], in1=N.gs.t[:], op=ALU.mult),
                    r=[N.gx, N.gs], w=[N.h1g], cost=0.1)
                b2 = nb()
                for half in range(2):
                    S.op("pe", (lambda b2=b2, half=half: nc.tensor.matmul(
                        b2.t[0:64, 0:16], lhsT=N.cw2[0].t[:, half, :], rhs=N.h1g.t[:, half, :],
                        start=(half == 0), stop=(half == 1))), r=[N.cw2[0], N.h1g], w=[b2], cost=0.07)
                S.op("act", (lambda b2=b2, g=g: nc.scalar.copy(out=N.KcT[g].t[:, nb0 + m0:nb0 + 16],
                                                                in_=b2.t[0:64, m0:16])), r=[b2], w=[N.KcT[g]], cost=0.2)
            else:
                S.op("dve", (lambda: nc.vector.tensor_tensor(
                    out=N.h1pad.t[:, :, nb0 + m0:nb0 + 16],
                    in0=N.gx.t[:].rearrange("p (a b) -> p a b", a=2)[:, :, m0:16],
                    in1=N.gs.t[:].rearrange("p (a b) -> p a b", a=2)[:, :, m0:16], op=ALU.mult)),
                    r=[N.gx, N.gs], w=[N.h1pad], cost=0.1)
                cts = sorted(set([(nb0 + m0) // 128, (nb0 + 15) // 128]))
                for ct in cts:
                    b2 = nb()
                    for half in range(2):
                        S.op("pe", (lambda b2=b2, half=half, ct=ct: nc.tensor.matmul(
                            b2.t[:, 0:64], lhsT=N.h1pad.t[:, half, ct * 128:(ct + 1) * 128], rhs=N.cw2[1].t[:, half, :],
                            start=(half == 0), stop=(half == 1))), r=[N.cw2[1], N.h1pad], w=[b2], cost=0.08)
                    S.op("dve", (lambda b2=b2, g=g, ct=ct: nc.vector.tensor_tensor(
                        out=N.VcA.t[:, g, ct, 0:64], in0=b2.t[:, 0:64], in1=N.VcA.t[:, g, ct, 0:64], op=ALU.add)),
                        r=[b2, N.VcA], w=[N.VcA], cost=0.12)
                S.op("dve", (lambda: nc.vector.memset(N.h1pad.t[:, :, nb0 + m0:nb0 + 16], 0.0)), w=[N.h1pad], cost=0.1)
            S.op("pool", (lambda xc=xc: nc.gpsimd.tensor_copy(out=xc.t[:, 0:16], in_=xc.t[:, 256:272])),
                 r=[xc], w=[xc], cost=0.2)

    nct = 2 if i >= 8 else 1
    A0 = acc[0]
    cjobs = []
    for g in range(2):
        for n in range(4):
            for ct in range(nct):
                cjobs.append((g, n, 4 * g + n, ct))
    cbank = {}

    def c_score(k):
        g, n, hq, ct = cjobs[k]
        sc = nb()
        cbank[k] = sc
        S.op("pe", (lambda: nc.tensor.matmul(
            sc.t[:, 0:TT], lhsT=N.KcT[g].t[0:64, ct * 128:(ct + 1) * 128], rhs=QS[hq].t[0:64, :],
            start=True, stop=True)), r=[N.KcT[g], QS[hq]], w=[sc], cost=0.13)

    c_score(0)
    for k, (g, n, hq, ct) in enumerate(cjobs):
        if k + 1 < len(cjobs):
            c_score(k + 1)
        sc = cbank[k]
        A = A0
        N.sbcnt += 1
        sbs = N.sbs[0]
        S.op("dve", (lambda sc=sc, sbs=sbs, ct=ct, hq=hq: nc.vector.scalar_tensor_tensor(
            out=sbs.t[:], in0=N.Dq.t[:, ct, :], scalar=SLOPES[hq], in1=sc.t[:, 0:TT],
            op0=ALU.mult, op1=ALU.add)), r=[N.Dq, sc], w=[sbs], cost=0.4)
        N.pccnt += 1
        P = N.PTc[N.pccnt % 2]
        S.op("act", (lambda sbs=sbs, P=P: nc.scalar.activation(out=P.t[:], in_=sbs.t[:], func=AF.Exp)),
             r=[sbs], w=[P], tbl="exp", cost=0.4)
        for s in range(2):
            first = (ct == 0 and s == 0)
            S.op("pe", (lambda P=P, s=s, g=g, ct=ct, first=first, A=A: nc.tensor.matmul(
                A.t[:, s * 65:(s + 1) * 65], lhsT=P.t[:, s * 128:(s + 1) * 128], rhs=N.VcA.t[:, g, ct, :],
                start=first, stop=False, skip_group_check=True)), r=[P, N.VcA], w=[A], cost=0.08)
            S.op("pe", (lambda P=P, s=s, ct=ct, A=A: nc.tensor.matmul(
                A.t[:, 130 + s * 64:130 + (s + 1) * 64], lhsT=P.t[:, s * 128:(s + 1) * 128],
                rhs=N.Mslc.t[:, ct, :], start=False, stop=False, skip_group_check=True)),
                r=[P, N.Mslc], w=[A], cost=0.08)
        if ct == nct - 1:
            rzc = N.rzc[hq % 2]
            S.op("dve", (lambda A=A, rzc=rzc: nc.vector.tensor_scalar(
                out=rzc.t[:], in0=A.t[:, 0:130].rearrange("p (s c) -> p s c", s=2)[:, :, 64], scalar1=1e-30,
                scalar2=None, op0=ALU.max)), r=[A], w=[rzc], cost=0.1)
            S.op("dve", (lambda rzc=rzc: nc.vector.reciprocal(out=rzc.t[:], in_=rzc.t[:])), r=[rzc], w=[rzc], cost=0.1)
            for s in range(2):
                S.op("dve", (lambda s=s, hq=hq, A=A, rzc=rzc: nc.vector.tensor_scalar(
                    out=ocmp.t[:, s, hq, :], in0=A.t[:, s * 65:s * 65 + 64], scalar1=rzc.t[:, s:s + 1], scalar2=None,
                    op0=ALU.mult)), r=[A, rzc], w=[ocmp], cost=0.12)
                if n == 0:
                    S.op("dve", (lambda s=s, g=g, A=A, rzc=rzc: nc.vector.tensor_scalar(
                        out=N.imp.t[:, s, g, :], in0=A.t[:, 130 + s * 64:130 + (s + 1) * 64],
                        scalar1=rzc.t[:, s:s + 1], scalar2=None, op0=ALU.mult)), r=[A, rzc], w=[N.imp], cost=0.12)
                else:
                    S.op("dve", (lambda s=s, g=g, A=A, rzc=rzc: nc.vector.scalar_tensor_tensor(
                        out=N.imp.t[:, s, g, :], in0=A.t[:, 130 + s * 64:130 + (s + 1) * 64],
                        scalar=rzc.t[:, s:s + 1], in1=N.imp.t[:, s, g, :], op0=ALU.mult, op1=ALU.add)),
                        r=[A, rzc, N.imp], w=[N.imp], cost=0.12)

    mq = 0
    for g in range(2):
        for s in range(2):
            S.op("dve", (lambda s=s, g=g: nc.vector.tensor_tensor(out=N.scr.t[:], in0=N.imp.t[:, s, g, :],
                                                                  in1=N.force.t[:, s, :], op=ALU.add)),
                 r=[N.imp, N.force], w=[N.scr], cost=0.12)
            S.op("dve", lambda: nc.vector.max(out=N.m8a.t[:], in_=N.scr.t[:]), r=[N.scr], w=[N.m8a], cost=0.12)
            S.op("dve", lambda: nc.vector.match_replace(out=N.scr2.t[:], in_to_replace=N.m8a.t[:], in_values=N.scr.t[:],
                                                        imm_value=-1e30), r=[N.scr, N.m8a], w=[N.scr2], cost=0.15)
            S.op("dve", lambda: nc.vector.max(out=N.m8b.t[:], in_=N.scr2.t[:]), r=[N.scr2], w=[N.m8b], cost=0.12)
            S.op("dve", lambda: nc.vector.tensor_scalar(out=N.msk.t[:], in0=N.scr.t[:], scalar1=N.m8b.t[:, 7:8],
                                                        scalar2=None, op0=ALU.is_ge), r=[N.scr, N.m8b], w=[N.msk], cost=0.12)
            for n in range(4):
                hq = 4 * g + n
                M = N.Mq[mq % 2]
                mq += 1
                S.op("dve", (lambda M=M, hq=hq, s=s: nc.vector.tensor_scalar(
                    out=M.t[:, 64:128], in0=N.msk.t[:], scalar1=BIG, scalar2=N.negb.t[:, hq * 2 + s:hq * 2 + s + 1],
                    op0=ALU.mult, op1=ALU.add)), r=[N.msk, N.negb], w=[M], cost=0.12)
                S.op("pe", (lambda M=M: nc.tensor.transpose(pt.t[:, 0:128], M.t[:], C.ident.t[:])),
                     r=[M, C.ident], w=[pt], cost=0.08)
                S.op("act", (lambda hq=hq, s=s: nc.scalar.copy(out=QS[hq].t[64:128, s * 128:(s + 1) * 128],
                                                                in_=pt.t[64:128, 0:128])), r=[pt], w=[QS[hq]], cost=0.2)


def nsa_back(S, nc, C, l, i, N, oTi, pt, nbS, acc):
    par = i % 2
    QS = N.QS[par]
    gsig = N.gsig[par]
    ocmp = N.ocmp[par]

    def next_pt():
        N.ptcnt += 1
        return N.PT[N.ptcnt % NPT]

    def run_jobs(jobs):
        banks = [None] * len(jobs)

        def issue_score(k):
            sc = nbS()
            banks[k] = sc
            jobs[k]["score"](sc)

        LA = NPS - 1
        for k in range(min(LA, len(jobs))):
            issue_score(k)
        for k, jb in enumerate(jobs):
            if k + LA < len(jobs):
                issue_score(k + LA)
            sc = banks[k]
            P = next_pt()
            c0, c1 = jb["c0"], jb["c1"]
            S.op("act", (lambda sc=sc, P=P, c0=c0, c1=c1, bias=jb["bias"]: nc.scalar.activation(
                out=P.t[:, c0:c1], in_=sc.t[:, c0:c1], func=AF.Exp, bias=bias)), r=[sc, N.btab], w=[P],
                tbl="exp", cost=0.3 + 0.0007 * (c1 - c0))
            for d0, mk in jb["masks"]:
                S.op("pool", (lambda P=P, d0=d0, mk=mk: nc.gpsimd.tensor_tensor(
                    out=P.t[:, d0:d0 + 128], in0=P.t[:, d0:d0 + 128], in1=mk.t[:], op=ALU.mult)),
                    r=[P, mk], w=[P], cost=0.43)
            for pv in jb["pv"]:
                pv(P)

    def sel_jobs(g, hq, A, fr):
        jobs = []
        for kt in range(2 * i + 2):
            k0 = kt * 128
            c0 = 128 if kt == 2 * i + 1 else 0
            idx = 2 * i - kt + 1

            def score(sc, g=g, hq=hq, k0=k0, c0=c0, kt=kt):
                S.op("pe", (lambda: nc.tensor.matmul(
                    sc.t[:, c0:TT], lhsT=N.KS[g].t[:, k0:k0 + 128], rhs=QS[hq].t[:, c0:TT],
                    start=True, stop=True)), r=[N.KSr[g][kt], QS[hq]], w=[sc], cost=0.08 + 0.0004 * (TT - c0))

            masks = []
            if kt >= 2 * i:
                masks.append(((kt - 2 * i) * 128, N.tri))
            pvs = []
            for s in range(c0 // 128, 2):
                def pv(P, s=s, g=g, kt=kt):
                    fresh = fr[0]
                    fr[0] = False
                    S.op("pe", (lambda: nc.tensor.matmul(
                        A.t[:, s * 65:(s + 1) * 65], lhsT=P.t[:, s * 128:(s + 1) * 128], rhs=N.VS.t[:, g, kt, :],
                        start=fresh, stop=False, skip_group_check=True)), r=[P, N.VSr[kt]], w=[A], cost=0.08)
                pvs.append(pv)
            jobs.append(dict(score=score, c0=c0, c1=TT, bias=N.btab.t[:, hq, idx:idx + 1], masks=masks, pv=pvs))
        return jobs

    def win_jobs(g, hq, A):
        jobs = []
        for j in range(6):
            kt = 2 * i - 4 + j
            if kt < 0:
                continue
            slot = kt % 8
            s_lo, s_hi = max(0, j - 4), min(1, j)
            c0, c1 = s_lo * 128, (s_hi + 1) * 128
            idx = 5 - j

            def score(sc, g=g, hq=hq, slot=slot, c0=c0, c1=c1):
                S.op("pe", (lambda: nc.tensor.matmul(
                    sc.t[:, c0:c1], lhsT=N.KW[g].t[:, slot * 128:(slot + 1) * 128], rhs=QS[hq].t[:, c0:c1],
                    start=True, stop=True)), r=[N.KWr[g][slot], QS[hq]], w=[sc], cost=0.08 + 0.0004 * (c1 - c0))

            masks = []
            for s in range(s_lo, s_hi + 1):
                if j == s:
                    masks.append((s * 128, N.strict))
                elif j == s + 4:
                    masks.append((s * 128, N.tri))
            pvs = []
            for s in range(s_lo, s_hi + 1):
                def pv(P, s=s, g=g, slot=slot):
                    S.op("pe", (lambda: nc.tensor.matmul(
                        A.t[:, 130 + s * 65:130 + (s + 1) * 65], lhsT=P.t[:, s * 128:(s + 1) * 128],
                        rhs=N.VW.t[:, g, slot, :], start=False, stop=False, skip_group_check=True)),
                        r=[P, N.VWr[slot]], w=[A], cost=0.08)
                pvs.append(pv)
            jobs.append(dict(score=score, c0=c0, c1=c1, bias=N.btab.t[:, hq, idx:idx + 1], masks=masks, pv=pvs))
        return jobs

    def interleave(a, b):
        out = []
        for k in range(max(len(a), len(b))):
            if k < len(a):
                out.append(a[k])
            if k < len(b):
                out.append(b[k])
        return out

    def combine(hq, A, slot):
        for s in range(2):
            fo = N.fo[slot][s]
            rz = N.rz[slot][s]
            Av = A.t[:, 0:260].rearrange("p (b c) -> p b c", b=2)
            S.op("dve", (lambda s=s, rz=rz, Av=Av: nc.vector.reciprocal(out=rz.t[:, 0:2], in_=Av[:, :, s * 65 + 64])),
                 r=[A], w=[rz], cost=0.1)
            S.op("dve", (lambda s=s, rz=rz: nc.vector.tensor_tensor(
                out=rz.t[:, 2:4], in0=rz.t[:, 0:2], in1=gsig.t[:, s, hq * 3 + 1:hq * 3 + 3], op=ALU.mult)),
                r=[rz, gsig], w=[rz], cost=0.1)
            S.op("dve", (lambda s=s, fo=fo, rz=rz: nc.vector.tensor_scalar(
                out=fo.t[:], in0=A.t[:, s * 65:s * 65 + 64], scalar1=rz.t[:, 2:3], scalar2=None, op0=ALU.mult)),
                r=[A, rz], w=[fo], cost=0.12)
            S.op("dve", (lambda s=s, fo=fo, rz=rz: nc.vector.scalar_tensor_tensor(
                out=fo.t[:], in0=A.t[:, 130 + s * 65:130 + s * 65 + 64], scalar=rz.t[:, 3:4], in1=fo.t[:],
                op0=ALU.mult, op1=ALU.add)), r=[A, rz, fo], w=[fo], cost=0.12)
            S.op("dve", (lambda s=s, fo=fo: nc.vector.scalar_tensor_tensor(
                out=N.onsa.t[:, s, hq * 64:(hq + 1) * 64], in0=ocmp.t[:, s, hq, :],
                scalar=gsig.t[:, s, hq * 3:hq * 3 + 1], in1=fo.t[:], op0=ALU.mult, op1=ALU.add)),
                r=[ocmp, gsig, fo], w=[N.onsa], cost=0.12)

    for g in range(2):
        for n in range(4):
            hq = 4 * g + n
            run_jobs(sel_jobs(g, hq, acc[1], [True]) + win_jobs(g, hq, acc[1]))
            combine(hq, acc[1], hq % 2)
    for s in range(2):
        bk = nbS()
        for k in range(4):
            S.op("pe", (lambda s=s, k=k, bk=bk: nc.tensor.matmul(
                bk.t[:, k * 128:(k + 1) * 128], lhsT=N.onsa.t[:, s, k * 128:(k + 1) * 128], rhs=C.ident.t[:],
                start=True, stop=True)), r=[N.onsa, C.ident], w=[bk], cost=0.1)
        S.op("act", (lambda s=s, bk=bk: nc.scalar.copy(out=oTi.t[:, 0:4, s * 128:(s + 1) * 128],
                                                       in_=bk.t[:, 0:512].rearrange("p (k t) -> p k t", k=4))),
             r=[bk], w=[oTi], cost=0.5)


COST_US.update({147: 0.459, 411: 1.234, 421: 0.181, 423: 0.229, 424: 0.229, 474: 1.24, 478: 0.082, 481: 1.012, 491: 0.114, 495: 0.383, 497: 0.417, 506: 0.23, 510: 0.692, 515: 1.233, 535: 1.282, 555: 0.14, 745: 8.821, 759: 0.006, 767: 0.17, 769: 0.17, 771: 0.485, 772: 0.485, 781: 0.204, 788: 0.205, 799: 1.037, 803: 0.109, 806: 0.925, 819: 0.578, 822: 0.59, 824: 0.618, 826: 0.665, 828: 1.261, 838: 0.597, 840: 0.626, 842: 0.413, 845: 0.413, 848: 0.416, 849: 0.72, 852: 0.689, 856: 0.466, 859: 0.422, 862: 0.413, 863: 0.305, 865: 0.306, 867: 0.306, 868: 0.726, 870: 0.698, 872: 0.693, 874: 0.693, 880: 0.165, 883: 0.193, 886: 0.135, 888: 0.343, 894: 0.247, 897: 0.275, 900: 0.217, 903: 0.347, 906: 0.589, 908: 0.249, 910: 0.242, 914: 0.175, 917: 0.203, 919: 0.205, 923: 0.313, 927: 0.17, 930: 0.291, 950: 0.27, 954: 0.671, 958: 0.701, 963: 0.159, 1025: 0.969, 1027: 0.323, 1029: 0.271, 1030: 3.515, 1031: 0.916, 1032: 0.264, 1033: 0.475, 1035: 0.155, 1060: 0.031, 1064: 0.16, 1091: 0.407, 1095: 0.391, 1104: 0.309, 1106: 0.385, 1117: 0.305, 1120: 0.403, 1131: 0.193, 1134: 0.26, 1137: 0.173, 1139: 0.188, 1151: 0.041, 1156: 0.231, 1161: 0.191, 1163: 0.18, 1165: 0.188, 1167: 0.222, 1169: 0.176, 1172: 0.19, 1177: 0.12, 1180: 0.158, 1184: 0.195, 1193: 0.126, 1196: 0.223, 1199: 0.042, 1201: 0.242, 1218: 0.351, 1230: 0.373, 1235: 0.416, 1239: 0.133, 1242: 0.053, 1249: 0.154, 1252: 0.166, 1254: 0.283, 1258: 0.283, 1262: 0.284, 1271: 0.202, 1274: 0.228, 1275: 0.287, 1277: 0.228, 1278: 0.323, 1284: 0.284, 1287: 0.196, 1289: 0.26, 1321: 0.393, 1325: 0.425, 1339: 0.194, 1351: 0.098, 1371: 0.217, 1384: 0.089, 1406: 0.114, 1408: 0.185, 1411: 0.28, 1414: 0.278, 1417: 0.298, 1432: 0.12, 1435: 0.582})
```

```python
import numpy as np
from contextlib import ExitStack
import concourse.bass as bass
import concourse.mybir as mybir
from concourse.bass_utils import run_bass_kernel_spmd
from concourse.alu_op_type import AluOpType as ALU

F32 = mybir.dt.float32
BF16 = mybir.dt.bfloat16
AF = mybir.ActivationFunctionType


COST_US = {}; PRIO_W = 0.0; PE_SCALE = 0.6; FFN_PE_SCALE = 0.05; MIX_PE_SCALE = 0.62


class Res:
    __slots__ = ("name", "t", "writer", "readers", "excl")

    def __init__(self, name, t=None):
        self.name = name
        self.t = t
        self.excl = False
        self.writer = None
        self.readers = []


class Op:
    __slots__ = ("eng", "fn", "kind", "deps", "needed", "token", "idx", "cost", "tbl", "eidx", "fin", "nd", "succ", "waits", "pidx", "rem")

    def __init__(self, eng, fn, kind):
        self.eng = eng
        self.fn = fn
        self.kind = kind
        self.deps = []
        self.needed = False
        self.token = None


class _FakeT:
    def __getitem__(self, k):
        return self

    def rearrange(self, *a, **k):
        return self

    def to_broadcast(self, *a, **k):
        return self


class Sched:
    DMA_SLOTS = {"sp": 12, "pool": 8, "act": 4}

    def __init__(self, nc, stack):
        self.nc = nc
        self.stack = stack
        self.E = dict(pe=nc.tensor, act=nc.scalar, dve=nc.vector, pool=nc.gpsimd, sp=nc.sync)
        self.ops = []
        self.all_res = []
        self.uid = 0
        self.dry = False

    def sb(self, name, shape, dtype, stack=None):
        if self.dry:
            r = Res(name, _FakeT())
            self.all_res.append(r)
            return r
        self.uid += 1
        name = "%s_u%d" % (name, self.uid)
        t = (stack or self.stack).enter_context(self.nc.sbuf_tensor(name, shape, dtype))
        r = Res(name, t)
        self.all_res.append(r)
        return r

    def ps(self, name, shape, dtype, stack=None):
        if self.dry:
            r = Res(name, _FakeT())
            r.excl = True
            self.all_res.append(r)
            return r
        self.uid += 1
        name = "%s_u%d" % (name, self.uid)
        t = (stack or self.stack).enter_context(self.nc.psum_tensor(name, shape, dtype))
        r = Res(name, t)
        r.excl = True
        self.all_res.append(r)
        return r

    def res(self, name, t=None):
        r = Res(name, t)
        self.all_res.append(r)
        return r

    DEFAULT_COST = {"pe": 0.16, "act": 0.45, "dve": 0.30, "pool": 0.45}

    def _add(self, o, r, w):
        ex = [x for x in r if x.excl]
        if ex:
            r = [x for x in r if not x.excl]
            w = list(w) + [x for x in ex if x not in w]
        deps = []
        for x in r:
            if x.writer is not None:
                deps.append(x.writer)
        for x in w:
            if x.writer is not None:
                deps.append(x.writer)
            deps.extend(x.readers)
        for x in r:
            x.readers.append(o)
        for x in w:
            x.writer = o
            x.readers = []
        seen = set()
        for d in deps:
            if d is o or id(d) in seen:
                continue
            seen.add(id(d))
            o.deps.append(d)
        o.idx = len(self.ops)
        self.ops.append(o)
        return o

    def op(self, eng, fn, r=(), w=(), cost=None, tbl=None):
        o = Op(eng, fn, "c")
        o.cost = cost if cost is not None else self.DEFAULT_COST[eng]
        mc = COST_US.get(fn.__code__.co_firstlineno)
        if mc is not None:
            o.cost = (mc * PE_SCALE if eng == "pe" else mc) + 0.03
        o.tbl = tbl
        return self._add(o, r, w)

    def dma(self, q, out, in_, r=(), w=(), cost=3.0, **kw):
        e = self.E[q]
        o = Op(q, (lambda: e.dma_start(out=out, in_=in_, **kw)), "d")
        o.cost = cost
        o.tbl = None
        return self._add(o, r, w)

    def coll(self, kind, in_ap, out_ap, groups, r=(), w=()):
        g = self.nc.gpsimd
        o = Op("pool", (lambda: g.collective_compute(kind, ALU.bypass, groups, [in_ap], [out_ap])), "d")
        o.cost = 50.0
        o.tbl = None
        return self._add(o, r, w)

    def begin(self):
        nc = self.nc
        self.sem = {k: self.stack.enter_context(nc.semaphore("s_" + k)) for k in self.E}
        self.dsem = {q: [self.stack.enter_context(nc.semaphore("d_%s%d" % (q, i))) for i in range(n)]
                     for q, n in self.DMA_SLOTS.items()}
        self.cnt = {k: 0 for k in self.E}
        self.dcnt = {q: 0 for q in self.dsem}
        self.dhist = {q: [] for q in self.dsem}
        self.waited = {k: {} for k in self.E}
        self.barrier_tokens = []
        self.total_ops = 0

    def _wait(self, eng, tok):
        s, v = tok
        key = id(s)
        if self.waited[eng].get(key, 0) >= v:
            return
        self.waited[eng][key] = v
        self.E[eng].wait_ge(s, v)

    REORDER = True

    def _schedule(self):
        import heapq
        ops = self.ops
        if not self.REORDER:
            return list(ops)
        for o in ops:
            o.nd = len(o.deps)
            o.succ = []
            o.fin = 0.0
        for o in ops:
            for d in o.deps:
                d.succ.append(o)
        for o in reversed(ops):
            o.rem = o.cost + max([q.rem for q in o.succ], default=0.0)
        for k, o in enumerate(sorted(ops, key=lambda o: (-(o.rem + PRIO_W * (len(ops) - o.idx)), o.idx))):
            o.pidx = k
        engs = list(self.E.keys())
        fut = {e: [] for e in engs}
        now = {e: {} for e in engs}
        free = {e: 0.0 for e in engs}
        last_tbl = [None]
        LIMIT = 3.0
        for o in ops:
            if o.nd == 0:
                heapq.heappush(fut[o.eng], (0.0, o.pidx, o))
        order = []
        XLAT = 0.7
        n = len(ops)

        def pick_now(e):
            hs = now[e]
            if e != "act":
                h = hs.get(None)
                return (h[0][0], None) if h else None
            best = None
            cur = last_tbl[0]
            for tag in (cur, None):
                h = hs.get(tag)
                if h and (best is None or h[0][0] < best[0]):
                    best = (h[0][0], tag)
            other = None
            for tag, h in hs.items():
                if not h or tag in (cur, None):
                    continue
                if other is None or h[0][0] < other[0]:
                    other = (h[0][0], tag, h[0][1])
            if other is not None and (best is None or other[2] < free[e] - LIMIT):
                return (other[0], other[1])
            return best

        while len(order) < n:
            best = None
            for e in engs:
                f = fut[e]
                while f and f[0][0] <= free[e]:
                    rdy, ix, o = heapq.heappop(f)
                    tag = o.tbl if e == "act" else None
                    heapq.heappush(now[e].setdefault(tag, []), (ix, rdy, o))
                pk = pick_now(e)
                if pk is not None:
                    cand = (free[e], pk[0], e, 0, pk[1])
                elif f:
                    cand = (f[0][0], f[0][1], e, 1, None)
                else:
                    continue
                if best is None or cand[:2] < best[:2]:
                    best = cand
            st, _, e, which, tag = best
            if which == 0:
                _, _, o = heapq.heappop(now[e][tag])
            else:
                _, _, o = heapq.heappop(fut[e])
            c = o.cost
            if o.eng == "act" and o.tbl is not None:
                if last_tbl[0] is not None and last_tbl[0] != o.tbl:
                    c += 1.3
                last_tbl[0] = o.tbl
            if o.kind == "d":
                free[e] = st + 0.15
                o.fin = st + c
            else:
                free[e] = st + c
                o.fin = st + c
            order.append(o)
            for q in o.succ:
                q.nd -= 1
                if q.nd == 0:
                    rdy = 0.0
                    for d in q.deps:
                        t = d.fin + (XLAT if d.eng != q.eng else 0.0)
                        if t > rdy:
                            rdy = t
                    heapq.heappush(fut[q.eng], (rdy, q.pidx, q))
        self.est_us = max(o.fin for o in ops) if ops else 0.0
        return order

    def flush(self):
        self.phase_no = getattr(self, "phase_no", 0) + 1
        self.sem = {k: self.stack.enter_context(self.nc.semaphore("s%d_%s" % (self.phase_no, k))) for k in self.E}
        self.cnt = {k: 0 for k in self.E}
        order = self._schedule()
        if self.dry:
            print("DRY phase ops %d est_us %.1f" % (len(order), self.est_us))
            self.ops = []
            for r in self.all_res:
                r.writer = None
                r.readers = []
            return
        ecount = {k: 0 for k in self.E}
        for o in order:
            o.needed = False
            if o.kind == "c":
                ecount[o.eng] += 1
                o.eidx = ecount[o.eng]
        last = {}
        for o in order:
            if o.kind == "c":
                last[o.eng] = o
            best = {}
            keep = []
            for d in o.deps:
                if d.kind == "d":
                    keep.append(d)
                    continue
                if o.kind == "c" and d.eng == o.eng and o.eng == "pe":
                    continue
                b = best.get(d.eng)
                if b is None or d.eidx > b.eidx:
                    best[d.eng] = d
            keep.extend(best.values())
            o.waits = keep
            for d in keep:
                d.needed = True
        for o in last.values():
            o.needed = True
        first_seen = set()
        for o in order:
            if o.eng not in first_seen:
                first_seen.add(o.eng)
                for tok in self.barrier_tokens:
                    self._wait(o.eng, tok)
            for d in o.waits:
                self._wait(o.eng, d.token)
            if o.kind == "c":
                ins = o.fn()
                if o.needed:
                    self.cnt[o.eng] += 1
                    ins.then_inc(self.sem[o.eng], 1)
                    o.token = (self.sem[o.eng], self.cnt[o.eng])
            else:
                q = o.eng
                K = len(self.dsem[q])
                n = self.dcnt[q]
                if n >= K:
                    self._wait(q, self.dhist[q][n - K])
                s = self.dsem[q][n % K]
                ins = o.fn()
                ins.then_inc(s, 16)
                o.token = (s, 16 * (n // K + 1))
                self.dhist[q].append(o.token)
                self.dcnt[q] += 1
            o.fn = None
            o.deps = None
            o.succ = None
            o.waits = None
        toks = [o.token for o in last.values()]
        for q in self.dsem:
            toks.extend(self.dhist[q][-len(self.dsem[q]):])
        self.barrier_tokens = toks + [t for t in self.barrier_tokens]
        best = {}
        for s_, v in self.barrier_tokens:
            if id(s_) not in best or best[id(s_)][1] < v:
                best[id(s_)] = (s_, v)
        self.barrier_tokens = list(best.values())
        self.total_ops += len(self.ops)
        self.ops = []
        for r in self.all_res:
            r.writer = None
            r.readers = []

    def end(self):
        for tok in self.barrier_tokens:
            self._wait("sp", tok)


D = 1024
SEQ = 4096
NB_ = 4
DEPTH = 2
DFF = 2752
NFT = 22
TT = 256
NT = SEQ // TT
EPS = 1e-6
IN_COLS = 3352


class Ctx:
    pass


def dram_tiles(S, name, ap):
    c = Ctx()
    c.ap = ap
    c.res = [S.res("%s_%d" % (name, i)) for i in range(NT)]
    return c


def tok_tile(ap, i):
    return ap[i * TT:(i + 1) * TT, :].rearrange("(s p) d -> p s d", p=128)


def prologue_ss(S, nc, C, hin):
    with ExitStack() as ph:
        xt = [S.sb("pro_x%d" % k, [128, 2, D], F32, ph) for k in range(2)]
        junk = S.sb("pro_junk", [128, D], BF16, ph)
        for i in range(NT):
            t = xt[i % 2]
            S.dma("sp", t.t[:], tok_tile(hin.ap, i), r=[hin.res[i]], w=[t])
            for s in range(2):
                S.op("dve", (lambda t=t, s=s, i=i: nc.vector.scalar_tensor_tensor(
                    out=junk.t[:], in0=t.t[:, s, :], scalar=1.0, in1=t.t[:, s, :],
                    op0=ALU.mult, op1=ALU.mult, accum_out=C.ss.t[:, 2 * i + s:2 * i + s + 1])),
                    r=[t], w=[junk, C.ss])
        S.flush()


def rstd_from_ss(S, nc, C, ph):
    tmp = S.sb("rs_tmp", [128, 32], F32, ph)
    S.op("dve", lambda: nc.vector.tensor_scalar(out=tmp.t[:], in0=C.ss.t[:], scalar1=1.0 / D, scalar2=EPS,
                                                op0=ALU.mult, op1=ALU.add), r=[C.ss], w=[tmp])
    S.op("act", lambda: nc.scalar.activation(out=tmp.t[:], in_=tmp.t[:], func=AF.Ln), r=[tmp], w=[tmp])
    S.op("act", lambda: nc.scalar.activation(out=C.rstd.t[:], in_=tmp.t[:], func=AF.Exp, scale=-0.5), r=[tmp], w=[C.rstd])


def ffn_phase(S, nc, C, w_gu, w_down, gvec, hin, hout, tag):
    global PE_SCALE, PRIO_W; PE_SCALE = FFN_PE_SCALE; PRIO_W = 1e6
    with ExitStack() as ph:
        wgu = S.sb(tag + "wgu", [128, 8, 2 * DFF], BF16, ph)
        wd = S.sb(tag + "wd", [128, NFT, D], BF16, ph)
        gbc = S.sb(tag + "gbc", [128, D], F32, ph)
        ht = [S.sb(tag + "ht%d" % k, [128, 2, D], F32, ph) for k in range(2)]
        hn = [S.sb(tag + "hn%d" % k, [128, D], BF16, ph) for k in range(2)]
        hnT = [S.sb(tag + "hnT%d" % k, [128, 8, TT], BF16, ph) for k in range(2)]
        actT = [S.sb(tag + "actT%d" % k, [128, NFT, TT], BF16, ph) for k in range(2)]
        sg = [S.sb(tag + "sg%d" % k, [128, TT], F32, ph) for k in range(2)]
        junk = S.sb(tag + "junk", [128, D], BF16, ph)
        pt = S.ps(tag + "pt", [128, 1024], BF16, ph)
        gups = [S.ps(tag + "gu%d" % k, [128, 512], F32, ph) for k in range(3)]
        dps = [S.ps(tag + "dp%d" % k, [128, 512], F32, ph) for k in range(2)]

        rstd_from_ss(S, nc, C, ph)
        cgb = [0, 6, 12, 17, NFT]
        wguR = [[S.res(tag + "wguR%d_%d" % (k, kc)) for kc in range(8)] for k in range(4)]
        cg_of = [max(k for k in range(4) if cgb[k] <= j) for j in range(NFT)]
        for cg in range(4):
            a0, a1 = cgb[cg] * 128, min(cgb[cg + 1] * 128, DFF)
            for kc in range(8):
                S.dma("pool", wgu.t[:, kc, :].rearrange("p (h f) -> p h f", h=2)[:, :, a0:a1],
                      w_gu[kc * 128:(kc + 1) * 128, :].rearrange("p (h f) -> p h f", h=2)[:, :, a0:a1],
                      w=[wguR[cg][kc]], cost=6.0)
        S.dma("pool", wd.t[:, 0:NFT - 1, :], w_down[0:(NFT - 1) * 128, :].rearrange("(c p) n -> p c n", p=128), w=[wd])
        S.dma("pool", wd.t[0:64, NFT - 1, :], w_down[(NFT - 1) * 128:DFF, :], w=[wd])
        S.dma("sp", gbc.t[:], gvec.partition_broadcast(128), w=[gbc])

        def load(i):
            S.dma("sp", ht[i % 2].t[:], tok_tile(hin.ap, i), r=[hin.res[i]], w=[ht[i % 2]])

        load(0)
        gi = 0
        di = 0
        for i in range(NT):
            if i + 1 < NT:
                load(i + 1)
            h = ht[i % 2]
            xT = hnT[i % 2]
            aT = actT[i % 2]
            for s in range(2):
                n_ = hn[s]
                col = 2 * i + s
                S.op("dve", (lambda h=h, s=s, n_=n_, col=col: nc.vector.scalar_tensor_tensor(
                    out=n_.t[:], in0=h.t[:, s, :], scalar=C.rstd.t[:, col:col + 1], in1=gbc.t[:],
                    op0=ALU.mult, op1=ALU.mult)), r=[h, C.rstd, gbc], w=[n_])
                for kc in range(8):
                    S.op("pe", (lambda n_=n_, kc=kc: nc.tensor.transpose(
                        pt.t[:, kc * 128:(kc + 1) * 128], n_.t[:, kc * 128:(kc + 1) * 128], C.ident.t[:])),
                        r=[n_, C.ident], w=[pt])
                S.op("act", (lambda xT=xT, s=s: nc.scalar.copy(
                    out=xT.t[:, :, s * 128:(s + 1) * 128], in_=pt.t[:].rearrange("p (k t) -> p k t", k=8))),
                    r=[pt], w=[xT])
            for j in range(NFT):
                cw = 128 if j < NFT - 1 else 64
                g = gups[gi % 3]
                gi += 1
                for half in range(2):
                    c0 = half * DFF + j * 128
                    for kc in range(8):
                        S.op("pe", (lambda g=g, half=half, c0=c0, cw=cw, kc=kc, xT=xT: nc.tensor.matmul(
                            g.t[0:cw, half * TT:(half + 1) * TT], lhsT=wgu.t[:, kc, c0:c0 + cw], rhs=xT.t[:, kc, :],
                            start=(kc == 0), stop=(kc == 7))), r=[wguR[cg_of[j]][kc], xT], w=[g])
                sgt = sg[j % 2]
                S.op("act", (lambda g=g, cw=cw, sgt=sgt: nc.scalar.activation(
                    out=sgt.t[0:cw, :], in_=g.t[0:cw, 0:TT], func=AF.Silu)), r=[g], w=[sgt])
                S.op("dve", (lambda g=g, cw=cw, sgt=sgt, aT=aT, j=j: nc.vector.tensor_tensor(
                    out=aT.t[0:cw, j, :], in0=sgt.t[0:cw, :], in1=g.t[0:cw, TT:2 * TT], op=ALU.mult)),
                    r=[g, sgt], w=[aT])
            for s in range(2):
                for half in range(2):
                    dp = dps[di % 2]
                    di += 1
                    for c in range(NFT):
                        kw = 128 if c < NFT - 1 else 64
                        S.op("pe", (lambda dp=dp, kw=kw, c=c, s=s, half=half, aT=aT: nc.tensor.matmul(
                            dp.t[:, :], lhsT=aT.t[0:kw, c, s * 128:(s + 1) * 128],
                            rhs=wd.t[0:kw, c, half * 512:(half + 1) * 512],
                            start=(c == 0), stop=(c == NFT - 1))), r=[aT, wd], w=[dp])
                    S.op("dve", (lambda dp=dp, h=h, s=s, half=half: nc.vector.scalar_tensor_tensor(
                        out=h.t[:, s, half * 512:(half + 1) * 512], in0=dp.t[:, :], scalar=0.5,
                        in1=h.t[:, s, half * 512:(half + 1) * 512], op0=ALU.mult, op1=ALU.add)),
                        r=[dp, h], w=[h])
                col = 2 * i + s
                S.op("dve", (lambda h=h, s=s, col=col: nc.vector.scalar_tensor_tensor(
                    out=junk.t[:], in0=h.t[:, s, :], scalar=1.0, in1=h.t[:, s, :],
                    op0=ALU.mult, op1=ALU.mult, accum_out=C.ss.t[:, col:col + 1])), r=[h], w=[junk, C.ss])
            S.dma("sp", tok_tile(hout.ap, i), h.t[:], r=[h], w=[hout.res[i]])
        S.flush()


def final_phase(S, nc, C, gvec, hin, y):
    with ExitStack() as ph:
        gbc = S.sb("fin_gbc", [128, D], F32, ph)
        ht = [S.sb("fin_ht%d" % k, [128, 2, D], F32, ph) for k in range(2)]
        ot = [S.sb("fin_ot%d" % k, [128, 2, D], F32, ph) for k in range(2)]
        rstd_from_ss(S, nc, C, ph)
        S.dma("sp", gbc.t[:], gvec.partition_broadcast(128), w=[gbc])
        for i in range(NT):
            h = ht[i % 2]
            o = ot[i % 2]
            S.dma("sp", h.t[:], tok_tile(hin.ap, i), r=[hin.res[i]], w=[h])
            for s in range(2):
                col = 2 * i + s
                S.op("dve", (lambda h=h, o=o, s=s, col=col: nc.vector.scalar_tensor_tensor(
                    out=o.t[:, s, :], in0=h.t[:, s, :], scalar=C.rstd.t[:, col:col + 1], in1=gbc.t[:],
                    op0=ALU.mult, op1=ALU.mult)), r=[h, C.rstd, gbc], w=[o])
            S.dma("pool", tok_tile(y.ap, i), o.t[:], r=[o], w=[y.res[i]])
        S.flush()


WEIGHT_SPECS = [
    ("ffn1_norm", [DEPTH, D]), ("ffn1_w_gu", [DEPTH, D, 2 * DFF]), ("ffn1_w_down", [DEPTH, DFF, D]),
    ("mix_norm", [DEPTH, D]), ("w_in", [DEPTH, D, IN_COLS]),
    ("cmp_pos_k", [DEPTH, 32, 64]), ("cmp_pos_v", [DEPTH, 32, 64]),
    ("cmp_k_w1", [DEPTH, 2048, 256]), ("cmp_k_w2", [DEPTH, 256, 64]),
    ("cmp_v_w1", [DEPTH, 2048, 256]), ("cmp_v_w2", [DEPTH, 256, 64]),
    ("hgrn_lower_bound", [DEPTH, 512]), ("hgrn_out_norm", [DEPTH, 128]),
    ("w_out", [DEPTH, D, D]), ("ffn2_norm", [DEPTH, D]), ("ffn2_w_gu", [DEPTH, D, 2 * DFF]),
    ("ffn2_w_down", [DEPTH, DFF, D]), ("final_norm", [D]),
]


def build(stages=None):
    nc = bass.Bass("TRN2", target_bir_lowering=False)
    I = {}
    I["x"] = nc.dram_tensor("x", [SEQ, D], F32, kind="ExternalInput").ap()
    for name, shp in WEIGHT_SPECS:
        I[name] = nc.dram_tensor(name, shp, F32, kind="ExternalInput").ap()
    for name, arr in host_consts().items():
        I[name] = nc.dram_tensor(name, list(arr.shape), F32, kind="ExternalInput").ap()
    yap = nc.dram_tensor("y", [SEQ, D], F32, kind="ExternalOutput").ap()
    ha = nc.dram_tensor("h_a", [SEQ, D], F32, kind="Internal").ap()
    hb = nc.dram_tensor("h_b", [SEQ, D], F32, kind="Internal").ap()
    if stages is None:
        stages = ["f1_0", "mix_0", "f2_0", "f1_1", "mix_1", "f2_1"]
    with ExitStack() as st:
        S = Sched(nc, st)
        S.begin()
        C = Ctx()
        C.I = I
        C.ident = S.sb("ident_sb", [128, 128], BF16)
        C.ss = S.sb("ss_sb", [128, 32], F32)
        C.rstd = S.sb("rstd_sb", [128, 32], F32)
        X = dram_tiles(S, "x", I["x"])
        HA = dram_tiles(S, "ha", ha)
        HB = dram_tiles(S, "hb", hb)
        Y = dram_tiles(S, "y", yap)
        S.dma("pool", C.ident.t[:], I["ident"][:, :], w=[C.ident])
        prologue_ss(S, nc, C, X)
        cur = X
        nxt = HA
        for stg in stages:
            kind, l = stg.split("_")
            l = int(l)
            if kind == "f1":
                ffn_phase(S, nc, C, I["ffn1_w_gu"][l], I["ffn1_w_down"][l], I["ffn1_norm"][l], cur, nxt, stg)
            elif kind == "f2":
                ffn_phase(S, nc, C, I["ffn2_w_gu"][l], I["ffn2_w_down"][l], I["ffn2_norm"][l], cur, nxt, stg)
            else:
                mix_phase(S, nc, C, l, cur, nxt, stg)
            cur = nxt
            nxt = HB if cur is HA else HA
        final_phase(S, nc, C, I["final_norm"], cur, Y)
        S.end()
    return nc


_CONSTS = None


def host_consts():
    global _CONSTS
    if _CONSTS is not None:
        return _CONSTS
    c = {}
    c["ident"] = np.eye(128, dtype=np.float32)
    rm = np.ones((128, TT), np.float32)
    rm[:, ::64] = 0.0
    c["c_rmask"] = rm
    c["c_masku"] = np.triu(np.ones((64, 64), np.float32))
    slopes = np.array(SLOPES, np.float64)
    key = np.arange(SEQ)
    c["c_onehot"] = (key[None, :] // 64 == np.arange(64)[:, None]).astype(np.float32)
    c["c_ones"] = np.ones((1, 1024), np.float32)
    vc1 = np.ones((128, 2), np.float32)
    vc1[127, 1] = 0.0
    c["c_vc1"] = vc1
    cc = np.arange(256)[:, None]
    nn = np.arange(64)[None, :]
    m = ((cc >= 4 * nn) & (cc <= 4 * nn + 3)).astype(np.float32) + ((cc >= 4 * nn - 1) & (cc <= 4 * nn + 2)).astype(np.float32)
    m[255, :] = 0.0
    c["c_mslc"] = m
    q = np.arange(6144)[None, :] - 2048
    dist = q - 16 * np.arange(128)[:, None] - 31
    c["c_dtab"] = np.where(dist >= 0, -dist, -1e7).astype(np.float32)
    pos = np.arange(SEQ)[:, None]
    cur = pos // 64
    blk = np.arange(64)[None, :]
    forced = (blk == 0) | (blk == cur) | (blk == cur - 1)
    c["c_force"] = np.where(forced, 1e6, 0.0).astype(np.float32)
    p = np.arange(128)[:, None, None]
    idx = np.arange(34)[None, None, :]
    c["c_btab"] = (slopes[None, :, None] * (p - 128.0 * (idx - 1))).astype(np.float32).reshape(128, 8 * 34)
    ss_ = np.arange(2)[None, None, :]
    c["c_negb"] = (-BIG - slopes[None, :, None] * (128.0 * ss_ + p)).astype(np.float32).reshape(128, 16)
    c["c_qrow"] = (-slopes[:, None] * np.arange(TT)[None, :]).astype(np.float32)
    kk = np.arange(128)[:, None]
    qq = np.arange(128)[None, :]
    c["c_tri"] = (kk <= qq).astype(np.float32)
    c["c_strict"] = (kk > qq).astype(np.float32)
    _CONSTS = c
    return c


def kernel(stages=None, ncores=4, **inputs):
    nc = build(stages)
    consts = host_consts()
    shared = {k: np.ascontiguousarray(np.asarray(inputs[k], dtype=np.float32)) for k, _ in WEIGHT_SPECS}
    x = np.asarray(inputs["x"], dtype=np.float32)
    in_maps = []
    for b in range(ncores):
        m = dict(shared)
        m.update(consts)
        m["x"] = np.ascontiguousarray(x[b])
        in_maps.append(m)
    res = run_bass_kernel_spmd(nc, in_maps, core_ids=list(range(ncores)))
    out = np.stack([np.asarray(res.results[b]["y"], dtype=np.float32) for b in range(ncores)], axis=0)
    return out


OQ, OKC, OVC, OKS, OVS, OKW, OVW, OGT, OHQ, OHF, OHI, OHG = 0, 512, 640, 768, 896, 1024, 1152, 1280, 1304, 1816, 2328, 2840
BIG = 30000.0
ENABLE_NSA = True
NPT, NPF, NPS = 3, 2, 3
SLOPES = [2.0 ** (-(h + 1)) for h in range(8)]
GELU_C = 1.5957691216057308


def mix_phase(S, nc, C, l, hin, hout, tag):
    global PE_SCALE, PRIO_W; PE_SCALE = MIX_PE_SCALE; PRIO_W = 0.05; """h <- h + hybrid_mixer(rmsnorm(h)).  Per 256-token tile: a front end (norm, projections, compress,
    compressed attention, top-k, HGRN2) and a back end (selected/window attention, gating, output projection);
    tile-local buffers are double-buffered so the scheduler can overlap front end i+1 with back end i."""
    I = C.I
    with ExitStack() as ph:
        def sb(name, shape, dt):
            return S.sb(tag + name, shape, dt, ph)

        win = sb("win", [128, 8, IN_COLS], BF16)
        wout = sb("wout", [128, 8, D], BF16)
        gcol = sb("gcol", [128, 8], F32)
        ht = sb("ht", [128, 2, D], F32)
        hn = sb("hn", [128, D], BF16)
        hnT = sb("hnT", [128, 8, TT], BF16)
        oT = [sb("oT%d" % k, [128, 8, TT], BF16) for k in range(2)]
        hres = [sb("hres%d" % k, [128, 512], F32) for k in range(2)]
        ssh = sb("ssh", [128, 2], F32)
        lbt = sb("lbt", [128, 4], F32)
        omlt = sb("omlt", [128, 4], F32)
        nomlt = sb("nomlt", [128, 4], F32)
        lbtmp = sb("lbtmp", [128, 8], F32)
        rmask = sb("rmask", [128, TT], BF16)
        maskU = sb("maskU", [64, 64], F32)
        onbc = sb("onbc", [64, 512], F32)
        hv = sb("hv", [64, 4, 512], BF16)
        gn = sb("gn", [64, 4, 512], BF16)
        state = sb("state", [128, 4, 128], F32)
        statebf = sb("statebf", [128, 4, 128], BF16)
        osb = [sb("osb%d" % k, [64, 4, 128], F32) for k in range(2)]
        ssq = [sb("ssq%d" % k, [64, 4], F32) for k in range(2)]
        rsq = [sb("rsq%d" % k, [64, 4], F32) for k in range(2)]
        ohg = [sb("ohg%d" % k, [64, 128], BF16) for k in range(2)]
        ef = [sb("ef%d" % k, [128, TT], F32) for k in range(6)]
        sig2 = sb("sig2", [128, 2 * TT], F32)
        eb = [sb("eb%d" % k, [128, TT], BF16) for k in range(4)]
        aTm = [sb("aTm%d" % k, [64, 64], BF16) for k in range(4)]
        khT = sb("khT", [64, 512], BF16)
        junkh = sb("junkh", [64, 128], BF16)

        pt = S.ps(tag + "pt", [128, 1024], BF16, ph)
        poolF = [S.ps(tag + "pf%d" % k, [128, 512], F32, ph) for k in range(NPF)]
        poolS = [S.ps(tag + "psc%d" % k, [128, 512], F32, ph) for k in range(NPS)]
        acc = [S.ps(tag + "acc%d" % k, [128, 512], F32, ph) for k in range(2)]
        pcnt = [0, 0]

        def nb():
            pcnt[0] += 1
            return poolF[pcnt[0] % len(poolF)]

        def nbS():
            pcnt[1] += 1
            return poolS[pcnt[1] % len(poolS)]

        rstd_from_ss(S, nc, C, ph)
        wgb = [0, OKC, OHQ, IN_COLS]
        winR = [[S.res(tag + "winR%d_%d" % (k, kc)) for kc in range(8)] for k in range(3)]
        for cg in range(3):
            for kc in range(8):
                S.dma("pool", win.t[:, kc, wgb[cg]:wgb[cg + 1]], I["w_in"][l, kc * 128:(kc + 1) * 128, wgb[cg]:wgb[cg + 1]],
                      w=[winR[cg][kc]], cost=4.0)
        S.dma("pool", wout.t[:], I["w_out"][l].rearrange("(c p) n -> p c n", p=128), w=[wout], cost=10.0)
        S.dma("sp", gcol.t[:], I["mix_norm"][l].rearrange("(c p) -> p c", p=128), w=[gcol], allow_slow_non_contiguous=True)
        for cg in range(3):
            for kc in range(8):
                e_ = "dve" if kc % 2 == 0 else "pool"
                eng_ = nc.vector if kc % 2 == 0 else nc.gpsimd
                S.op(e_, (lambda kc=kc, eng_=eng_, cg=cg: eng_.tensor_scalar(
                    out=win.t[:, kc, wgb[cg]:wgb[cg + 1]], in0=win.t[:, kc, wgb[cg]:wgb[cg + 1]],
                    scalar1=gcol.t[:, kc:kc + 1], scalar2=None, op0=ALU.mult)),
                    r=[gcol, winR[cg][kc]], w=[winR[cg][kc]], cost=0.3 + 0.001 * (wgb[cg + 1] - wgb[cg]))
        S.dma("pool", rmask.t[:], I["c_rmask"][:, :], w=[rmask])
        S.dma("sp", maskU.t[:], I["c_masku"][:, :], w=[maskU])
        for hh in range(4):
            S.dma("sp", onbc.t[:, hh * 128:(hh + 1) * 128], I["hgrn_out_norm"][l].partition_broadcast(64), w=[onbc])
        S.dma("sp", lbtmp.t[:, 0:4], I["hgrn_lower_bound"][0].rearrange("(h d) -> d h", d=128), w=[lbtmp],
              allow_slow_non_contiguous=True)
        S.dma("sp", lbtmp.t[:, 4:8], I["hgrn_lower_bound"][1].rearrange("(h d) -> d h", d=128), w=[lbtmp],
              allow_slow_non_contiguous=True)
        if l == 0:
            S.op("dve", lambda: nc.vector.memset(lbt.t[:], 0.0), w=[lbt])
        else:
            S.op("dve", lambda: nc.vector.tensor_tensor(out=lbtmp.t[:, 0:4], in0=lbtmp.t[:, 4:8], in1=lbtmp.t[:, 0:4],
                                                        op=ALU.subtract), r=[lbtmp], w=[lbtmp])
            S.op("act", lambda: nc.scalar.activation(out=lbt.t[:], in_=lbtmp.t[:, 0:4], func=AF.Tanh, scale=0.5),
                 r=[lbtmp], w=[lbt], tbl="exp")
            S.op("dve", lambda: nc.vector.tensor_scalar(out=lbt.t[:], in0=lbt.t[:], scalar1=0.5, scalar2=0.5,
                                                        op0=ALU.mult, op1=ALU.add), r=[lbt], w=[lbt])
        S.op("dve", lambda: nc.vector.tensor_scalar(out=omlt.t[:], in0=lbt.t[:], scalar1=-1.0, scalar2=1.0,
                                                    op0=ALU.mult, op1=ALU.add), r=[lbt], w=[omlt])
        S.op("dve", lambda: nc.vector.tensor_scalar(out=nomlt.t[:], in0=omlt.t[:], scalar1=-1.0, scalar2=None,
                                                    op0=ALU.mult), r=[omlt], w=[nomlt])
        S.op("dve", lambda: nc.vector.memset(state.t[:], 0.0), w=[state])
        S.op("dve", lambda: nc.vector.memset(statebf.t[:], 0.0), w=[statebf])

        NS = None
        if ENABLE_NSA:
            NS = nsa_setup(S, nc, C, l, tag, ph, sb)

        def proj_fm(bank, c0, c1, col0, ncols):
            for kc in range(8):
                S.op("pe", (lambda kc=kc: nc.tensor.matmul(
                    bank.t[0:ncols, c0:c1], lhsT=win.t[:, kc, col0:col0 + ncols], rhs=hnT.t[:, kc, :],
                    start=(kc == 0), stop=(kc == 7))), r=[winR[0 if col0 < OKC else (1 if col0 < OHQ else 2)][kc], hnT], w=[bank], cost=0.12)

        def proj_tm(bank, t0, nt, col0, ncols, o0=0):
            for kc in range(8):
                S.op("pe", (lambda kc=kc: nc.tensor.matmul(
                    bank.t[0:nt, o0:o0 + ncols], lhsT=hnT.t[:, kc, t0:t0 + nt], rhs=win.t[:, kc, col0:col0 + ncols],
                    start=(kc == 0), stop=(kc == 7))), r=[winR[0 if col0 < OKC else (1 if col0 < OHQ else 2)][kc], hnT], w=[bank], cost=0.08 + ncols * 0.00045)

        for i in range(NT):
            par = i % 2
            oTi = oT[par]
            S.dma("sp", ht.t[:], tok_tile(hin.ap, i), r=[hin.res[i]], w=[ht])
            for s in range(2):
                col = 2 * i + s
                S.op("dve", (lambda s=s, col=col: nc.vector.tensor_scalar(
                    out=hn.t[:], in0=ht.t[:, s, :], scalar1=C.rstd.t[:, col:col + 1], scalar2=None, op0=ALU.mult)),
                    r=[ht, C.rstd], w=[hn], cost=0.6)
                for kc in range(8):
                    S.op("pe", (lambda kc=kc: nc.tensor.transpose(
                        pt.t[:, kc * 128:(kc + 1) * 128], hn.t[:, kc * 128:(kc + 1) * 128], C.ident.t[:])),
                        r=[hn, C.ident], w=[pt], cost=0.08)
                S.op("act", (lambda s=s: nc.scalar.copy(
                    out=hnT.t[:, :, s * 128:(s + 1) * 128], in_=pt.t[:].rearrange("p (k t) -> p k t", k=8))),
                    r=[pt], w=[hnT], cost=1.0)

            if ENABLE_NSA:
                nsa_front(S, nc, C, l, i, NS, win, hnT, pt, nb, acc, proj_fm, proj_tm)

            sgl = sig2
            for c in range(4):
                b1 = nb()
                proj_tm(b1, c * 64, 64, OHI, 512)
                S.op("act", (lambda b1=b1, c=c: nc.scalar.copy(out=hv.t[:, c, :], in_=b1.t[0:64, :])), r=[b1], w=[hv])
                b2 = nb()
                proj_tm(b2, c * 64, 64, OHG, 512)
                S.op("act", (lambda b2=b2: nc.scalar.activation(out=sgl.t[0:64, :], in_=b2.t[0:64, :], func=AF.Tanh, scale=0.5)),
                     r=[b2], w=[sgl], tbl="exp")
                S.op("pool", lambda: nc.gpsimd.tensor_scalar(out=sgl.t[0:64, :], in0=sgl.t[0:64, :], scalar1=0.5, scalar2=0.5,
                                                             op0=ALU.mult, op1=ALU.add), r=[sgl], w=[sgl], cost=0.6)
                S.op("dve", (lambda b2=b2: nc.vector.tensor_tensor(out=sgl.t[0:64, :], in0=sgl.t[0:64, :], in1=b2.t[0:64, :],
                                                                   op=ALU.mult)), r=[sgl, b2], w=[sgl], cost=0.55)
                S.op("pool", (lambda c=c: nc.gpsimd.tensor_tensor(out=gn.t[:, c, :], in0=sgl.t[0:64, :], in1=onbc.t[:],
                                                                  op=ALU.mult)), r=[sgl, onbc], w=[gn], cost=0.8)
            for hh in range(4):
                zq = nb()
                proj_fm(zq, 0, TT, OHF + hh * 128, 128)
                proj_fm(zq, TT, 2 * TT, OHQ + hh * 128, 128)
                f, logf, kf, b, bm, bl = ef
                e1, e2, e3, e4 = ef[0], ef[1], ef[4], ef[5]
                qt, kt, qd, kh = eb
                ob, sq_, rs_ = osb[hh % 2], ssq[hh % 2], rsq[hh % 2]
                S.op("act", (lambda zq=zq: nc.scalar.activation(out=sig2.t[:], in_=zq.t[:, 0:2 * TT], func=AF.Tanh, scale=0.5)),
                     r=[zq], w=[sig2], tbl="exp", cost=0.6)
                S.op("pool", lambda: nc.gpsimd.tensor_scalar(out=sig2.t[:], in0=sig2.t[:], scalar1=0.5, scalar2=0.5,
                                                             op0=ALU.mult, op1=ALU.add), r=[sig2], w=[sig2], cost=0.9)
                S.op("dve", (lambda zq=zq: nc.vector.tensor_tensor(out=sig2.t[:, TT:2 * TT], in0=sig2.t[:, TT:2 * TT],
                                                                   in1=zq.t[:, TT:2 * TT], op=ALU.mult)),
                     r=[zq, sig2], w=[sig2])
                S.op("dve", (lambda hh=hh: nc.vector.tensor_scalar(
                    out=f.t[:], in0=sig2.t[:, 0:TT], scalar1=omlt.t[:, hh:hh + 1], scalar2=lbt.t[:, hh:hh + 1],
                    op0=ALU.mult, op1=ALU.add)), r=[sig2, omlt, lbt], w=[f])
                S.op("act", lambda: nc.scalar.activation(out=logf.t[:], in_=f.t[:], func=AF.Ln), r=[f], w=[logf], tbl="exp")
                S.op("pool", (lambda hh=hh: nc.gpsimd.tensor_scalar(
                    out=kf.t[:], in0=sig2.t[:, 0:TT], scalar1=nomlt.t[:, hh:hh + 1], scalar2=omlt.t[:, hh:hh + 1],
                    op0=ALU.mult, op1=ALU.add)), r=[sig2, omlt, nomlt], w=[kf])
                S.op("dve", lambda: nc.vector.tensor_tensor_scan(out=b.t[:], data0=rmask.t[:], data1=logf.t[:],
                                                                 initial=0.0, op0=ALU.mult, op1=ALU.add),
                     r=[rmask, logf], w=[b], cost=0.6)
                b3 = b.t[:].rearrange("p (c t) -> p c t", c=4)
                S.op("dve", lambda: nc.vector.tensor_tensor(
                    out=bm.t[:].rearrange("p (c t) -> p c t", c=4), in0=b3,
                    in1=b3[:, :, 31:32].to_broadcast([128, 4, 64]), op=ALU.subtract), r=[b], w=[bm], cost=0.55)
                S.op("dve", lambda: nc.vector.tensor_tensor(
                    out=bl.t[:].rearrange("p (c t) -> p c t", c=4), in0=b3,
                    in1=b3[:, :, 63:64].to_broadcast([128, 4, 64]), op=ALU.subtract), r=[b], w=[bl], cost=0.55)
                S.op("act", lambda: nc.scalar.activation(out=e1.t[:], in_=bm.t[:], func=AF.Exp), r=[bm], w=[e1], tbl="exp")
                S.op("act", lambda: nc.scalar.activation(out=e2.t[:], in_=bm.t[:], func=AF.Exp, scale=-1.0),
                     r=[bm], w=[e2], tbl="exp")
                S.op("act", lambda: nc.scalar.activation(out=e4.t[:], in_=bl.t[:], func=AF.Exp, scale=-1.0),
                     r=[bl], w=[e4], tbl="exp")
                S.op("act", lambda: nc.scalar.activation(out=e3.t[:], in_=b.t[:], func=AF.Exp), r=[b], w=[e3], tbl="exp")
                S.op("pool", lambda: nc.gpsimd.tensor_tensor(out=qt.t[:], in0=sig2.t[:, TT:2 * TT], in1=e1.t[:], op=ALU.mult),
                     r=[sig2, e1], w=[qt])
                S.op("pool", lambda: nc.gpsimd.tensor_tensor(out=kt.t[:], in0=kf.t[:], in1=e2.t[:], op=ALU.mult),
                     r=[kf, e2], w=[kt])
                S.op("pool", lambda: nc.gpsimd.tensor_tensor(out=kh.t[:], in0=kf.t[:], in1=e4.t[:], op=ALU.mult),
                     r=[kf, e4], w=[kh])
                S.op("pool", lambda: nc.gpsimd.tensor_tensor(out=qd.t[:], in0=sig2.t[:, TT:2 * TT], in1=e3.t[:], op=ALU.mult),
                     r=[sig2, e3], w=[qd])
                for c in range(4):
                    cs = slice(c * 64, (c + 1) * 64)
                    ba = nb()
                    S.op("pe", (lambda ba=ba, cs=cs: nc.tensor.matmul(ba.t[0:64, 0:64], lhsT=kt.t[:, cs], rhs=qt.t[:, cs],
                                                                      start=True, stop=True)), r=[kt, qt], w=[ba], cost=0.07)
                    am = aTm[c]
                    S.op("dve", (lambda ba=ba, am=am: nc.vector.tensor_tensor(out=am.t[:], in0=ba.t[0:64, 0:64],
                                                                              in1=maskU.t[:], op=ALU.mult)),
                         r=[ba, maskU], w=[am], cost=0.12)
                    S.op("pe", (lambda cs=cs, c=c: nc.tensor.transpose(pt.t[0:64, c * 128:(c + 1) * 128], kh.t[:, cs],
                                                                       C.ident.t[:])), r=[kh, C.ident], w=[pt], cost=0.08)
                S.op("dve", lambda: nc.vector.tensor_copy(out=khT.t[:], in_=pt.t[0:64, 0:512]), r=[pt], w=[khT], cost=0.3)
                for c in range(4):
                    cs = slice(c * 64, (c + 1) * 64)
                    am = aTm[c]
                    bo = nb()
                    S.op("pe", (lambda bo=bo, am=am, c=c, hh=hh: nc.tensor.matmul(
                        bo.t[0:64, 0:128], lhsT=am.t[:], rhs=hv.t[:, c, hh * 128:(hh + 1) * 128],
                        start=True, stop=False)), r=[am, hv], w=[bo], cost=0.08)
                    S.op("pe", (lambda bo=bo, cs=cs, hh=hh: nc.tensor.matmul(
                        bo.t[0:64, 0:128], lhsT=qd.t[:, cs], rhs=statebf.t[:, hh, :],
                        start=False, stop=True)), r=[qd, statebf], w=[bo], cost=0.08)
                    S.op("pe", (lambda bo=bo, c=c, hh=hh: nc.tensor.matmul(
                        bo.t[:, 128:256], lhsT=khT.t[:, c * 128:(c + 1) * 128],
                        rhs=hv.t[:, c, hh * 128:(hh + 1) * 128], start=True, stop=True)), r=[khT, hv], w=[bo], cost=0.08)
                    S.op("dve", (lambda bo=bo, c=c, hh=hh: nc.vector.scalar_tensor_tensor(
                        out=state.t[:, hh, :], in0=state.t[:, hh, :], scalar=e3.t[:, c * 64 + 63:c * 64 + 64],
                        in1=bo.t[:, 128:256], op0=ALU.mult, op1=ALU.add)), r=[state, e3, bo], w=[state], cost=0.2)
                    S.op("pool", (lambda hh=hh: nc.gpsimd.tensor_copy(out=statebf.t[:, hh, :], in_=state.t[:, hh, :])),
                         r=[state], w=[statebf], cost=0.25)
                    S.op("act", (lambda bo=bo, c=c, ob=ob: nc.scalar.copy(out=ob.t[:, c, :], in_=bo.t[0:64, 0:128])),
                         r=[bo], w=[ob], cost=0.25)
                    S.op("dve", (lambda c=c, ob=ob, sq_=sq_: nc.vector.scalar_tensor_tensor(
                        out=junkh.t[:], in0=ob.t[:, c, :], scalar=1.0, in1=ob.t[:, c, :],
                        op0=ALU.mult, op1=ALU.mult, accum_out=sq_.t[:, c:c + 1])), r=[ob], w=[junkh, sq_], cost=0.25)
                S.op("dve", (lambda sq_=sq_, rs_=rs_: nc.vector.tensor_scalar(
                    out=rs_.t[:], in0=sq_.t[:], scalar1=1.0 / 128, scalar2=EPS, op0=ALU.mult, op1=ALU.add)),
                    r=[sq_], w=[rs_], cost=0.1)
                S.op("act", (lambda rs_=rs_: nc.scalar.activation(out=rs_.t[:], in_=rs_.t[:], func=AF.Ln)),
                     r=[rs_], w=[rs_], tbl="exp", cost=0.2)
                S.op("act", (lambda rs_=rs_: nc.scalar.activation(out=rs_.t[:], in_=rs_.t[:], func=AF.Exp, scale=-0.5)),
                     r=[rs_], w=[rs_], tbl="exp", cost=0.2)
                for c in range(4):
                    og = ohg[c % 2]
                    S.op("dve", (lambda og=og, c=c, hh=hh, ob=ob, rs_=rs_: nc.vector.scalar_tensor_tensor(
                        out=og.t[:], in0=ob.t[:, c, :], scalar=rs_.t[:, c:c + 1],
                        in1=gn.t[:, c, hh * 128:(hh + 1) * 128], op0=ALU.mult, op1=ALU.mult)),
                        r=[ob, rs_, gn], w=[og], cost=0.15)
                    S.op("pe", (lambda og=og, c=c: nc.tensor.transpose(
                        pt.t[:, c * 64:(c + 1) * 64], og.t[:], C.ident.t[0:64, 0:64])),
                        r=[og, C.ident], w=[pt], cost=0.08)
                S.op("act", (lambda hh=hh, oTi=oTi: nc.scalar.copy(out=oTi.t[:, 4 + hh, :], in_=pt.t[:, 0:TT])),
                     r=[pt], w=[oTi], cost=0.3)

            if ENABLE_NSA:
                nsa_back(S, nc, C, l, i, NS, oTi, pt, nbS, acc)
            else:
                S.op("dve", (lambda oTi=oTi: nc.vector.memset(oTi.t[:, 0:4, :], 0.0)), w=[oTi])

            ek = 0
            for s in range(2):
                for half in range(2):
                    hr = hres[ek % 2]
                    ek += 1
                    rows = slice(i * TT + s * 128, i * TT + (s + 1) * 128)
                    cols = slice(half * 512, (half + 1) * 512)
                    S.dma("sp", hr.t[:], hin.ap[rows, cols], r=[hin.res[i]], w=[hr])
                    dp = nbS()
                    for kc in range(8):
                        S.op("pe", (lambda dp=dp, kc=kc, s=s, half=half, oTi=oTi: nc.tensor.matmul(
                            dp.t[:, :], lhsT=oTi.t[:, kc, s * 128:(s + 1) * 128],
                            rhs=wout.t[:, kc, half * 512:(half + 1) * 512],
                            start=(kc == 0), stop=(kc == 7))), r=[oTi, wout], w=[dp], cost=0.23)
                    S.op("dve", (lambda dp=dp, hr=hr: nc.vector.tensor_tensor(
                        out=hr.t[:], in0=dp.t[:, :], in1=hr.t[:], op=ALU.add)), r=[dp, hr], w=[hr], cost=0.55)
                    jk = NS.onsa if ENABLE_NSA else hn
                    jk_ap = jk.t[:, 0, :] if ENABLE_NSA else jk.t[:, 0:512]
                    S.op("dve", (lambda hr=hr, half=half, jk_ap=jk_ap: nc.vector.scalar_tensor_tensor(
                        out=jk_ap, in0=hr.t[:], scalar=1.0, in1=hr.t[:],
                        op0=ALU.mult, op1=ALU.mult, accum_out=ssh.t[:, half:half + 1])), r=[hr], w=[jk, ssh], cost=0.6)
                    S.dma("sp", hout.ap[rows, cols], hr.t[:], r=[hr], w=[hout.res[i]])
                col = 2 * i + s
                S.op("dve", (lambda col=col: nc.vector.tensor_tensor(
                    out=C.ss.t[:, col:col + 1], in0=ssh.t[:, 0:1], in1=ssh.t[:, 1:2], op=ALU.add)),
                    r=[ssh], w=[C.ss], cost=0.1)
        S.flush()


def nsa_setup(S, nc, C, l, tag, ph, sb):
    I = C.I
    N = Ctx()
    N.KS = [sb("KS%d" % g, [128, SEQ], BF16) for g in range(2)]
    N.KSr = [[S.res("KSr%d_%d" % (g, k)) for k in range(32)] for g in range(2)]
    N.KW = [sb("KW%d" % g, [128, 1024], BF16) for g in range(2)]
    N.KWr = [[S.res("KWr%d_%d" % (g, k)) for k in range(8)] for g in range(2)]
    N.VS = sb("VS", [128, 2, 32, 65], BF16)
    N.VSr = [S.res("VSr%d" % k) for k in range(32)]
    N.VW = sb("VW", [128, 2, 8, 65], BF16)
    N.VWr = [S.res("VWr%d" % k) for k in range(8)]
    N.KcT = [sb("KcT%d" % g, [64, 256], BF16) for g in range(2)]
    N.VcA = sb("VcA", [128, 2, 2, 65], BF16)
    N.Mslc = sb("Mslc", [128, 2, 64], BF16)
    N.XC = [[sb("XC%d%d" % (kv, g), [128, 272], BF16) for g in range(2)] for kv in range(2)]
    N.cw1 = [sb("cw1_%d" % kv, [128, 16, 256], BF16) for kv in range(2)]
    N.cw2 = [sb("cw2_%d" % kv, [128, 2, 64], BF16) for kv in range(2)]
    N.posT = [sb("posT%d" % kv, [128, 16], BF16) for kv in range(2)]
    N.hb = [sb("hb%d" % kv, [128, 2], F32) for kv in range(2)]
    N.Dq = sb("Dq", [128, 2, TT], F32)
    N.force = sb("force", [128, 2, 64], F32)
    N.btab = sb("btab", [128, 8, 34], F32)
    N.negb = sb("negb", [128, 16], F32)
    N.QS = [[sb("QS%d_%d" % (p, h), [128, TT], BF16) for h in range(8)] for p in range(2)]
    N.tri = sb("tri", [128, 128], BF16)
    N.strict = sb("strict", [128, 128], BF16)
    N.gsig = [sb("gsig%d" % p, [128, 2, 24], F32) for p in range(2)]
    N.ocmp = [sb("ocmp%d" % p, [128, 2, 8, 64], F32) for p in range(2)]
    N.imp = sb("imp", [128, 2, 2, 64], F32)
    N.sbs = [sb("sbs%d" % k, [128, TT], F32) for k in range(1)]
    N.PT = [sb("PT%d" % k, [128, TT], BF16) for k in range(NPT)]
    N.PTc = [sb("PTc%d" % k, [128, TT], BF16) for k in range(2)]
    N.onsa = sb("onsa", [128, 2, 512], BF16)
    N.h1pad = sb("h1pad", [128, 2, 256], BF16)
    N.h1g = sb("h1g", [128, 2, 16], BF16)
    N.gx = sb("gx", [128, 32], F32)
    N.gt = sb("gt", [128, 32], F32)
    N.gs = sb("gs", [128, 32], F32)
    N.Mq = [sb("Mq%d" % k, [128, 128], BF16) for k in range(2)]
    N.scr = sb("scr", [128, 64], F32)
    N.scr2 = sb("scr2", [128, 64], F32)
    N.m8a = sb("m8a", [128, 8], F32)
    N.m8b = sb("m8b", [128, 8], F32)
    N.msk = sb("msk", [128, 64], F32)
    N.rzc = [sb("rzc%d" % k, [128, 2], F32) for k in range(2)]
    N.rz = [[sb("rz%d_%d" % (k, s), [128, 4], F32) for s in range(2)] for k in range(2)]
    N.fo = [[sb("fo%d_%d" % (k, s), [128, 64], F32) for s in range(2)] for k in range(2)]
    N.ptcnt = 0
    N.pccnt = 0
    N.sbcnt = 0

    for g in range(2):
        S.dma("pool", N.KS[g].t[64:128, :], I["c_onehot"][:, :], w=N.KSr[g], cost=8.0)
        S.op("pool", (lambda g=g: nc.gpsimd.memset(N.KW[g].t[:], 0.0)), w=N.KWr[g], cost=1.0)
        S.dma("pool", N.KW[g].t[64:65, :], I["c_ones"][:, :], w=N.KWr[g])
        S.op("pool", (lambda g=g: nc.gpsimd.memset(N.KcT[g].t[:], 0.0)), w=[N.KcT[g]])
        for kv in range(2):
            S.op("pool", (lambda g=g, kv=kv: nc.gpsimd.memset(N.XC[kv][g].t[:], 0.0)), w=[N.XC[kv][g]])
    S.op("pool", lambda: nc.gpsimd.memset(N.VS.t[:], 1.0), w=N.VSr, cost=4.0)
    S.op("pool", lambda: nc.gpsimd.memset(N.VW.t[:], 1.0), w=N.VWr, cost=1.0)
    S.op("pool", lambda: nc.gpsimd.memset(N.VcA.t[:], 0.0), w=[N.VcA])
    S.op("pool", lambda: nc.gpsimd.memset(N.h1pad.t[:], 0.0), w=[N.h1pad])
    for k in range(2):
        S.op("pool", (lambda k=k: nc.gpsimd.memset(N.Mq[k].t[:], 0.0)), w=[N.Mq[k]])
    for g in range(2):
        S.dma("pool", N.VcA.t[:, g, :, 64], I["c_vc1"][:, :], w=[N.VcA], allow_slow_non_contiguous=True)
    S.dma("pool", N.Mslc.t[:], I["c_mslc"].rearrange("(t p) n -> p t n", p=128), w=[N.Mslc])
    for kv in range(2):
        w1 = I["cmp_k_w1" if kv == 0 else "cmp_v_w1"][l]
        w2 = I["cmp_k_w2" if kv == 0 else "cmp_v_w2"][l]
        pe_ = I["cmp_pos_k" if kv == 0 else "cmp_pos_v"][l]
        S.dma("pool", N.cw1[kv].t[:], w1.rearrange("(c p) h -> p c h", p=128), w=[N.cw1[kv]], cost=8.0)
        S.dma("pool", N.cw2[kv].t[:], w2.rearrange("(c p) n -> p c n", p=128), w=[N.cw2[kv]])
        S.dma("pool", N.posT[kv].t[:], pe_.rearrange("(c two) d -> (two d) c", two=2), w=[N.posT[kv]],
              allow_slow_non_contiguous=True)
    S.dma("sp", N.btab.t[:], I["c_btab"].rearrange("p (h k) -> p h k", h=8), w=[N.btab])
    S.dma("sp", N.negb.t[:], I["c_negb"][:, :], w=[N.negb])
    S.dma("pool", N.tri.t[:], I["c_tri"][:, :], w=[N.tri])
    S.dma("pool", N.strict.t[:], I["c_strict"][:, :], w=[N.strict])
    return N


def nsa_hidden_bias(S, nc, N, nb):
    for kv in range(2):
        bank = nb()
        for half in range(2):
            for c in range(16):
                S.op("pe", (lambda kv=kv, half=half, c=c, bank=bank: nc.tensor.matmul(
                    bank.t[:, half:half + 1], lhsT=N.cw1[kv].t[:, c, half * 128:(half + 1) * 128],
                    rhs=N.posT[kv].t[:, c:c + 1], start=(c == 0), stop=(c == 15))),
                    r=[N.cw1[kv], N.posT[kv]], w=[bank], cost=0.07)
        S.op("dve", (lambda kv=kv, bank=bank: nc.vector.tensor_copy(out=N.hb[kv].t[:], in_=bank.t[:, 0:2])),
             r=[bank], w=[N.hb[kv]])


def nsa_front(S, nc, C, l, i, N, win, hnT, pt, nb, acc, proj_fm, proj_tm):
    I = C.I
    q0 = i * TT
    par = i % 2
    QS = N.QS[par]
    gsig = N.gsig[par]
    ocmp = N.ocmp[par]
    if i == 0:
        nsa_hidden_bias(S, nc, N, nb)

    S.dma("sp", N.Dq.t[:, 0, :], I["c_dtab"][:, 2048 + q0:2048 + q0 + TT], w=[N.Dq])
    S.dma("sp", N.Dq.t[:, 1, :], I["c_dtab"][:, q0:q0 + TT], w=[N.Dq])
    S.dma("sp", N.force.t[:], I["c_force"][q0:q0 + TT, :].rearrange("(s p) n -> p s n", p=128), w=[N.force])

    for hp in range(4):
        bank = nb()
        for k in range(2):
            proj_fm(bank, k * TT, (k + 1) * TT, OQ + 64 * (2 * hp + k), 64)
        for k in range(2):
            hq = 2 * hp + k
            if k == 0:
                S.op("act", (lambda bank=bank, k=k, hq=hq: nc.scalar.activation(
                    out=QS[hq].t[0:64, :], in_=bank.t[0:64, k * TT:(k + 1) * TT], func=AF.Copy, scale=0.125)),
                    r=[bank], w=[QS[hq]], cost=0.3)
            else:
                S.op("dve", (lambda bank=bank, k=k, hq=hq: nc.vector.tensor_scalar(
                    out=QS[hq].t[0:64, :], in0=bank.t[0:64, k * TT:(k + 1) * TT], scalar1=0.125, scalar2=None,
                    op0=ALU.mult)), r=[bank], w=[QS[hq]], cost=0.3)
    ws = ((2 * i) % 8) * 128
    for g in range(2):
        bank = nb()
        proj_fm(bank, 0, TT, OKS + 64 * g, 64)
        proj_fm(bank, TT, 2 * TT, OKW + 64 * g, 64)
        S.op("act", (lambda bank=bank, g=g: nc.scalar.copy(out=N.KS[g].t[0:64, q0:q0 + TT], in_=bank.t[0:64, 0:TT])),
             r=[bank], w=[N.KSr[g][2 * i], N.KSr[g][2 * i + 1]], cost=0.3)
        S.op("dve", (lambda bank=bank, g=g: nc.vector.tensor_copy(out=N.KW[g].t[0:64, ws:ws + TT],
                                                                  in_=bank.t[0:64, TT:2 * TT])),
             r=[bank], w=[N.KWr[g][(2 * i) % 8], N.KWr[g][(2 * i + 1) % 8]], cost=0.3)
    for g in range(2):
        bank = nb()
        proj_fm(bank, 0, TT, OKC + 64 * g, 64)
        proj_fm(bank, TT, 2 * TT, OVC + 64 * g, 64)
        for kv in range(2):
            xc = N.XC[kv][g]
            if kv == 0:
                S.op("act", (lambda bank=bank, xc=xc: nc.scalar.copy(out=xc.t[0:64, 16:272], in_=bank.t[0:64, 0:TT])),
                     r=[bank], w=[xc], cost=0.3)
            else:
                S.op("dve", (lambda bank=bank, xc=xc: nc.vector.tensor_copy(out=xc.t[0:64, 16:272],
                                                                            in_=bank.t[0:64, TT:2 * TT])),
                     r=[bank], w=[xc], cost=0.3)
            S.dma("sp", xc.t[64:128, 15:271], xc.t[0:64, 16:272], r=[xc], w=[xc], cost=2.5)
    for s in range(2):
        bank = nb()
        proj_tm(bank, s * 128, 128, OVS, 128, o0=0)
        proj_tm(bank, s * 128, 128, OVW, 128, o0=128)
        proj_tm(bank, s * 128, 128, OGT, 24, o0=256)
        kt = 2 * i + s
        S.op("act", (lambda bank=bank, kt=kt: nc.scalar.copy(
            out=N.VS.t[:, :, kt, 0:64], in_=bank.t[:, 0:128].rearrange("p (g d) -> p g d", g=2))),
            r=[bank], w=[N.VSr[kt]], cost=0.25)
        S.op("dve", (lambda bank=bank, kt=kt: nc.vector.tensor_copy(
            out=N.VW.t[:, :, kt % 8, 0:64], in_=bank.t[:, 128:256].rearrange("p (g d) -> p g d", g=2))),
            r=[bank], w=[N.VWr[kt % 8]], cost=0.2)
        S.op("act", (lambda bank=bank, s=s: nc.scalar.activation(out=gsig.t[:, s, :], in_=bank.t[:, 256:280],
                                                                 func=AF.Tanh, scale=0.5)), r=[bank], w=[gsig], tbl="exp", cost=0.2)
        S.op("dve", (lambda s=s: nc.vector.tensor_scalar(out=gsig.t[:, s, :], in0=gsig.t[:, s, :], scalar1=0.5, scalar2=0.5,
                                                         op0=ALU.mult, op1=ALU.add)), r=[gsig], w=[gsig], cost=0.1)
    nb0 = 16 * i - 1
    m0 = 1 if i == 0 else 0
    for kv in range(2):
        for g in range(2):
            xc = N.XC[kv][g]
            xv = xc.t[:, 0:272].rearrange("p (m t) -> p t m", t=16)
            bank = nb()
            for half in range(2):
                for c in range(16):
                    S.op("pe", (lambda bank=bank, half=half, c=c, kv=kv, xv=xv: nc.tensor.matmul(
                        bank.t[:, half * 16:(half + 1) * 16], lhsT=N.cw1[kv].t[:, c, half * 128:(half + 1) * 128],
                        rhs=xv[:, (2 * c) % 16, (2 * c) // 16:(2 * c) // 16 + 16], start=(c == 0), stop=(c == 15))),
                        r=[N.cw1[kv], xc], w=[bank], cost=0.08)
            for half in range(2):
                S.op("dve", (lambda bank=bank, half=half, kv=kv: nc.vector.tensor_scalar(
                    out=N.gx.t[:, half * 16:(half + 1) * 16], in0=bank.t[:, half * 16:(half + 1) * 16],
                    scalar1=N.hb[kv].t[:, half:half + 1], scalar2=None, op0=ALU.add)), r=[bank, N.hb[kv]], w=[N.gx],
                    cost=0.1)
            S.op("dve", lambda: nc.vector.tensor_tensor(out=N.gt.t[:], in0=N.gx.t[:], in1=N.gx.t[:], op=ALU.mult),
                 r=[N.gx], w=[N.gt], cost=0.1)
            S.op("dve", lambda: nc.vector.tensor_scalar(out=N.gt.t[:], in0=N.gt.t[:], scalar1=0.044715, scalar2=1.0,
                                                        op0=ALU.mult, op1=ALU.add), r=[N.gt], w=[N.gt], cost=0.1)
            S.op("dve", lambda: nc.vector.tensor_tensor(out=N.gt.t[:], in0=N.gt.t[:], in1=N.gx.t[:], op=ALU.mult),
                 r=[N.gt, N.gx], w=[N.gt], cost=0.1)
            S.op("act", lambda: nc.scalar.activation(out=N.gs.t[:], in_=N.gt.t[:], func=AF.Tanh, scale=0.5 * GELU_C),
                 r=[N.gt], w=[N.gs], tbl="exp", cost=0.2)
            S.op("dve", lambda: nc.vector.tensor_scalar(out=N.gs.t[:], in0=N.gs.t[:], scalar1=0.5, scalar2=0.5,
                                                        op0=ALU.mult, op1=ALU.add), r=[N.gs], w=[N.gs], cost=0.1)
            if kv == 0:
                S.op("dve", lambda: nc.vector.tensor_tensor(
                    out=N.h1g.t[:].rearrange("p a b -> p (a b)"), in0=N.gx.t[:], in1=N.gs.t[:], op=ALU.mult),
                    r=[N.gx, N.gs], w=[N.h1g], cost=0.1)
                b2 = nb()
                for half in range(2):
                    S.op("pe", (lambda b2=b2, half=half: nc.tensor.matmul(
                        b2.t[0:64, 0:16], lhsT=N.cw2[0].t[:, half, :], rhs=N.h1g.t[:, half, :],
                        start=(half == 0), stop=(half == 1))), r=[N.cw2[0], N.h1g], w=[b2], cost=0.07)
                S.op("act", (lambda b2=b2, g=g: nc.scalar.copy(out=N.KcT[g].t[:, nb0 + m0:nb0 + 16],
                                                                in_=b2.t[0:64, m0:16])), r=[b2], w=[N.KcT[g]], cost=0.2)
            else:
                S.op("dve", (lambda: nc.vector.tensor_tensor(
                    out=N.h1pad.t[:, :, nb0 + m0:nb0 + 16],
                    in0=N.gx.t[:].rearrange("p (a b) -> p a b", a=2)[:, :, m0:16],
                    in1=N.gs.t[:].rearrange("p (a b) -> p a b", a=2)[:, :, m0:16], op=ALU.mult)),
                    r=[N.gx, N.gs], w=[N.h1pad], cost=0.1)
                cts = sorted(set([(nb0 + m0) // 128, (nb0 + 15) // 128]))
                for ct in cts:
                    b2 = nb()
                    for half in range(2):
                        S.op("pe", (lambda b2=b2, half=half, ct=ct: nc.tensor.matmul(
                            b2.t[:, 0:64], lhsT=N.h1pad.t[:, half, ct * 128:(ct + 1) * 128], rhs=N.cw2[1].t[:, half, :],
                            start=(half == 0), stop=(half == 1))), r=[N.cw2[1], N.h1pad], w=[b2], cost=0.08)
                    S.op("dve", (lambda b2=b2, g=g, ct=ct: nc.vector.tensor_tensor(
                        out=N.VcA.t[:, g, ct, 0:64], in0=b2.t[:, 0:64], in1=N.VcA.t[:, g, ct, 0:64], op=ALU.add)),
                        r=[b2, N.VcA], w=[N.VcA], cost=0.12)
                S.op("dve", (lambda: nc.vector.memset(N.h1pad.t[:, :, nb0 + m0:nb0 + 16], 0.0)), w=[N.h1pad], cost=0.1)
            S.op("pool", (lambda xc=xc: nc.gpsimd.tensor_copy(out=xc.t[:, 0:16], in_=xc.t[:, 256:272])),
                 r=[xc], w=[xc], cost=0.2)

    nct = 2 if i >= 8 else 1
    A0 = acc[0]
    cjobs = []
    for g in range(2):
        for n in range(4):
            for ct in range(nct):
                cjobs.append((g, n, 4 * g + n, ct))
    cbank = {}

    def c_score(k):
        g, n, hq, ct = cjobs[k]
        sc = nb()
        cbank[k] = sc
        S.op("pe", (lambda: nc.tensor.matmul(
            sc.t[:, 0:TT], lhsT=N.KcT[g].t[0:64, ct * 128:(ct + 1) * 128], rhs=QS[hq].t[0:64, :],
            start=True, stop=True)), r=[N.KcT[g], QS[hq]], w=[sc], cost=0.13)

    c_score(0)
    for k, (g, n, hq, ct) in enumerate(cjobs):
        if k + 1 < len(cjobs):
            c_score(k + 1)
        sc = cbank[k]
        A = A0
        N.sbcnt += 1
        sbs = N.sbs[0]
        S.op("dve", (lambda sc=sc, sbs=sbs, ct=ct, hq=hq: nc.vector.scalar_tensor_tensor(
            out=sbs.t[:], in0=N.Dq.t[:, ct, :], scalar=SLOPES[hq], in1=sc.t[:, 0:TT],
            op0=ALU.mult, op1=ALU.add)), r=[N.Dq, sc], w=[sbs], cost=0.4)
        N.pccnt += 1
        P = N.PTc[N.pccnt % 2]
        S.op("act", (lambda sbs=sbs, P=P: nc.scalar.activation(out=P.t[:], in_=sbs.t[:], func=AF.Exp)),
             r=[sbs], w=[P], tbl="exp", cost=0.4)
        for s in range(2):
            first = (ct == 0 and s == 0)
            S.op("pe", (lambda P=P, s=s, g=g, ct=ct, first=first, A=A: nc.tensor.matmul(
                A.t[:, s * 65:(s + 1) * 65], lhsT=P.t[:, s * 128:(s + 1) * 128], rhs=N.VcA.t[:, g, ct, :],
                start=first, stop=False, skip_group_check=True)), r=[P, N.VcA], w=[A], cost=0.08)
            S.op("pe", (lambda P=P, s=s, ct=ct, A=A: nc.tensor.matmul(
                A.t[:, 130 + s * 64:130 + (s + 1) * 64], lhsT=P.t[:, s * 128:(s + 1) * 128],
                rhs=N.Mslc.t[:, ct, :], start=False, stop=False, skip_group_check=True)),
                r=[P, N.Mslc], w=[A], cost=0.08)
        if ct == nct - 1:
            rzc = N.rzc[hq % 2]
            S.op("dve", (lambda A=A, rzc=rzc: nc.vector.tensor_scalar(
                out=rzc.t[:], in0=A.t[:, 0:130].rearrange("p (s c) -> p s c", s=2)[:, :, 64], scalar1=1e-30,
                scalar2=None, op0=ALU.max)), r=[A], w=[rzc], cost=0.1)
            S.op("dve", (lambda rzc=rzc: nc.vector.reciprocal(out=rzc.t[:], in_=rzc.t[:])), r=[rzc], w=[rzc], cost=0.1)
            for s in range(2):
                S.op("dve", (lambda s=s, hq=hq, A=A, rzc=rzc: nc.vector.tensor_scalar(
                    out=ocmp.t[:, s, hq, :], in0=A.t[:, s * 65:s * 65 + 64], scalar1=rzc.t[:, s:s + 1], scalar2=None,
                    op0=ALU.mult)), r=[A, rzc], w=[ocmp], cost=0.12)
                if n == 0:
                    S.op("dve", (lambda s=s, g=g, A=A, rzc=rzc: nc.vector.tensor_scalar(
                        out=N.imp.t[:, s, g, :], in0=A.t[:, 130 + s * 64:130 + (s + 1) * 64],
                        scalar1=rzc.t[:, s:s + 1], scalar2=None, op0=ALU.mult)), r=[A, rzc], w=[N.imp], cost=0.12)
                else:
                    S.op("dve", (lambda s=s, g=g, A=A, rzc=rzc: nc.vector.scalar_tensor_tensor(
                        out=N.imp.t[:, s, g, :], in0=A.t[:, 130 + s * 64:130 + (s + 1) * 64],
                        scalar=rzc.t[:, s:s + 1], in1=N.imp.t[:, s, g, :], op0=ALU.mult, op1=ALU.add)),
                        r=[A, rzc, N.imp], w=[N.imp], cost=0.12)

    mq = 0
    for g in range(2):
        for s in range(2):
            S.op("dve", (lambda s=s, g=g: nc.vector.tensor_tensor(out=N.scr.t[:], in0=N.imp.t[:, s, g, :],
                                                                  in1=N.force.t[:, s, :], op=ALU.add)),
                 r=[N.imp, N.force], w=[N.scr], cost=0.12)
            S.op("dve", lambda: nc.vector.max(out=N.m8a.t[:], in_=N.scr.t[:]), r=[N.scr], w=[N.m8a], cost=0.12)
            S.op("dve", lambda: nc.vector.match_replace(out=N.scr2.t[:], in_to_replace=N.m8a.t[:], in_values=N.scr.t[:],
                                                        imm_value=-1e30), r=[N.scr, N.m8a], w=[N.scr2], cost=0.15)
            S.op("dve", lambda: nc.vector.max(out=N.m8b.t[:], in_=N.scr2.t[:]), r=[N.scr2], w=[N.m8b], cost=0.12)
            S.op("dve", lambda: nc.vector.tensor_scalar(out=N.msk.t[:], in0=N.scr.t[:], scalar1=N.m8b.t[:, 7:8],
                                                        scalar2=None, op0=ALU.is_ge), r=[N.scr, N.m8b], w=[N.msk], cost=0.12)
            for n in range(4):
                hq = 4 * g + n
                M = N.Mq[mq % 2]
                mq += 1
                S.op("dve", (lambda M=M, hq=hq, s=s: nc.vector.tensor_scalar(
                    out=M.t[:, 64:128], in0=N.msk.t[:], scalar1=BIG, scalar2=N.negb.t[:, hq * 2 + s:hq * 2 + s + 1],
                    op0=ALU.mult, op1=ALU.add)), r=[N.msk, N.negb], w=[M], cost=0.12)
                S.op("pe", (lambda M=M: nc.tensor.transpose(pt.t[:, 0:128], M.t[:], C.ident.t[:])),
                     r=[M, C.ident], w=[pt], cost=0.08)
                S.op("act", (lambda hq=hq, s=s: nc.scalar.copy(out=QS[hq].t[64:128, s * 128:(s + 1) * 128],
                                                                in_=pt.t[64:128, 0:128])), r=[pt], w=[QS[hq]], cost=0.2)


def nsa_back(S, nc, C, l, i, N, oTi, pt, nbS, acc):
    par = i % 2
    QS = N.QS[par]
    gsig = N.gsig[par]
    ocmp = N.ocmp[par]

    def next_pt():
        N.ptcnt += 1
        return N.PT[N.ptcnt % NPT]

    def run_jobs(jobs):
        banks = [None] * len(jobs)

        def issue_score(k):
            sc = nbS()
            banks[k] = sc
            jobs[k]["score"](sc)

        LA = NPS - 1
        for k in range(min(LA, len(jobs))):
            issue_score(k)
        for k, jb in enumerate(jobs):
            if k + LA < len(jobs):
                issue_score(k + LA)
            sc = banks[k]
            P = next_pt()
            c0, c1 = jb["c0"], jb["c1"]
            S.op("act", (lambda sc=sc, P=P, c0=c0, c1=c1, bias=jb["bias"]: nc.scalar.activation(
                out=P.t[:, c0:c1], in_=sc.t[:, c0:c1], func=AF.Exp, bias=bias)), r=[sc, N.btab], w=[P],
                tbl="exp", cost=0.3 + 0.0007 * (c1 - c0))
            for d0, mk in jb["masks"]:
                S.op("pool", (lambda P=P, d0=d0, mk=mk: nc.gpsimd.tensor_tensor(
                    out=P.t[:, d0:d0 + 128], in0=P.t[:, d0:d0 + 128], in1=mk.t[:], op=ALU.mult)),
                    r=[P, mk], w=[P], cost=0.43)
            for pv in jb["pv"]:
                pv(P)

    def sel_jobs(g, hq, A, fr):
        jobs = []
        for kt in range(2 * i + 2):
            k0 = kt * 128
            c0 = 128 if kt == 2 * i + 1 else 0
            idx = 2 * i - kt + 1

            def score(sc, g=g, hq=hq, k0=k0, c0=c0, kt=kt):
                S.op("pe", (lambda: nc.tensor.matmul(
                    sc.t[:, c0:TT], lhsT=N.KS[g].t[:, k0:k0 + 128], rhs=QS[hq].t[:, c0:TT],
                    start=True, stop=True)), r=[N.KSr[g][kt], QS[hq]], w=[sc], cost=0.08 + 0.0004 * (TT - c0))

            masks = []
            if kt >= 2 * i:
                masks.append(((kt - 2 * i) * 128, N.tri))
            pvs = []
            for s in range(c0 // 128, 2):
                def pv(P, s=s, g=g, kt=kt):
                    fresh = fr[0]
                    fr[0] = False
                    S.op("pe", (lambda: nc.tensor.matmul(
                        A.t[:, s * 65:(s + 1) * 65], lhsT=P.t[:, s * 128:(s + 1) * 128], rhs=N.VS.t[:, g, kt, :],
                        start=fresh, stop=False, skip_group_check=True)), r=[P, N.VSr[kt]], w=[A], cost=0.08)
                pvs.append(pv)
            jobs.append(dict(score=score, c0=c0, c1=TT, bias=N.btab.t[:, hq, idx:idx + 1], masks=masks, pv=pvs))
        return jobs

    def win_jobs(g, hq, A):
        jobs = []
        for j in range(6):
            kt = 2 * i - 4 + j
            if kt < 0:
                continue
            slot = kt % 8
            s_lo, s_hi = max(0, j - 4), min(1, j)
            c0, c1 = s_lo * 128, (s_hi + 1) * 128
            idx = 5 - j

            def score(sc, g=g, hq=hq, slot=slot, c0=c0, c1=c1):
                S.op("pe", (lambda: nc.tensor.matmul(
                    sc.t[:, c0:c1], lhsT=N.KW[g].t[:, slot * 128:(slot + 1) * 128], rhs=QS[hq].t[:, c0:c1],
                    start=True, stop=True)), r=[N.KWr[g][slot], QS[hq]], w=[sc], cost=0.08 + 0.0004 * (c1 - c0))

            masks = []
            for s in range(s_lo, s_hi + 1):
                if j == s:
                    masks.append((s * 128, N.strict))
                elif j == s + 4:
                    masks.append((s * 128, N.tri))
            pvs = []
            for s in range(s_lo, s_hi + 1):
                def pv(P, s=s, g=g, slot=slot):
                    S.op("pe", (lambda: nc.tensor.matmul(
                        A.t[:, 130 + s * 65:130 + (s + 1) * 65], lhsT=P.t[:, s * 128:(s + 1) * 128],
                        rhs=N.VW.t[:, g, slot, :], start=False, stop=False, skip_group_check=True)),
                        r=[P, N.VWr[slot]], w=[A], cost=0.08)
                pvs.append(pv)
            jobs.append(dict(score=score, c0=c0, c1=c1, bias=N.btab.t[:, hq, idx:idx + 1], masks=masks, pv=pvs))
        return jobs

    def interleave(a, b):
        out = []
        for k in range(max(len(a), len(b))):
            if k < len(a):
                out.append(a[k])
            if k < len(b):
                out.append(b[k])
        return out

    def combine(hq, A, slot):
        for s in range(2):
            fo = N.fo[slot][s]
            rz = N.rz[slot][s]
            Av = A.t[:, 0:260].rearrange("p (b c) -> p b c", b=2)
            S.op("dve", (lambda s=s, rz=rz, Av=Av: nc.vector.reciprocal(out=rz.t[:, 0:2], in_=Av[:, :, s * 65 + 64])),
                 r=[A], w=[rz], cost=0.1)
            S.op("dve", (lambda s=s, rz=rz: nc.vector.tensor_tensor(
                out=rz.t[:, 2:4], in0=rz.t[:, 0:2], in1=gsig.t[:, s, hq * 3 + 1:hq * 3 + 3], op=ALU.mult)),
                r=[rz, gsig], w=[rz], cost=0.1)
            S.op("dve", (lambda s=s, fo=fo, rz=rz: nc.vector.tensor_scalar(
                out=fo.t[:], in0=A.t[:, s * 65:s * 65 + 64], scalar1=rz.t[:, 2:3], scalar2=None, op0=ALU.mult)),
                r=[A, rz], w=[fo], cost=0.12)
            S.op("dve", (lambda s=s, fo=fo, rz=rz: nc.vector.scalar_tensor_tensor(
                out=fo.t[:], in0=A.t[:, 130 + s * 65:130 + s * 65 + 64], scalar=rz.t[:, 3:4], in1=fo.t[:],
                op0=ALU.mult, op1=ALU.add)), r=[A, rz, fo], w=[fo], cost=0.12)
            S.op("dve", (lambda s=s, fo=fo: nc.vector.scalar_tensor_tensor(
                out=N.onsa.t[:, s, hq * 64:(hq + 1) * 64], in0=ocmp.t[:, s, hq, :],
                scalar=gsig.t[:, s, hq * 3:hq * 3 + 1], in1=fo.t[:], op0=ALU.mult, op1=ALU.add)),
                r=[ocmp, gsig, fo], w=[N.onsa], cost=0.12)

    for g in range(2):
        for n in range(4):
            hq = 4 * g + n
            run_jobs(sel_jobs(g, hq, acc[1], [True]) + win_jobs(g, hq, acc[1]))
            combine(hq, acc[1], hq % 2)
    for s in range(2):
        bk = nbS()
        for k in range(4):
            S.op("pe", (lambda s=s, k=k, bk=bk: nc.tensor.matmul(
                bk.t[:, k * 128:(k + 1) * 128], lhsT=N.onsa.t[:, s, k * 128:(k + 1) * 128], rhs=C.ident.t[:],
                start=True, stop=True)), r=[N.onsa, C.ident], w=[bk], cost=0.1)
        S.op("act", (lambda s=s, bk=bk: nc.scalar.copy(out=oTi.t[:, 0:4, s * 128:(s + 1) * 128],
                                                       in_=bk.t[:, 0:512].rearrange("p (k t) -> p k t", k=4))),
             r=[bk], w=[oTi], cost=0.5)


COST_US.update({147: 0.458, 415: 1.234, 425: 0.181, 427: 0.229, 428: 0.229, 478: 1.24, 482: 0.073, 485: 0.945, 495: 0.126, 499: 0.377, 501: 0.404, 510: 0.23, 514: 0.676, 519: 1.211, 539: 1.251, 559: 0.138, 749: 9.962, 763: 0.006, 771: 0.17, 773: 0.172, 775: 0.485, 776: 0.485, 785: 0.147, 792: 0.177, 803: 0.986, 807: 0.114, 810: 0.998, 823: 0.552, 826: 0.568, 828: 0.616, 830: 0.665, 832: 1.264, 842: 0.597, 844: 0.623, 846: 0.384, 849: 0.417, 852: 0.351, 853: 0.771, 856: 0.674, 860: 0.385, 863: 0.453, 866: 0.36, 867: 0.324, 869: 0.384, 871: 0.413, 872: 0.718, 874: 0.717, 876: 0.738, 878: 0.693, 884: 0.136, 887: 0.171, 890: 0.147, 892: 0.397, 898: 0.24, 901: 0.253, 904: 0.212, 907: 0.341, 910: 0.588, 912: 0.182, 914: 0.303, 918: 0.191, 921: 0.185, 923: 0.196, 927: 0.328, 931: 0.182, 934: 0.299, 954: 0.268, 958: 0.653, 962: 0.726, 967: 0.168, 1029: 0.971, 1031: 0.261, 1033: 0.272, 1034: 3.581, 1035: 0.913, 1036: 0.329, 1037: 0.544, 1039: 0.189, 1064: 0.03, 1068: 0.062, 1095: 0.445, 1099: 0.403, 1108: 0.351, 1110: 0.4, 1121: 0.354, 1124: 0.4, 1135: 0.206, 1138: 0.262, 1141: 0.166, 1143: 1.914, 1155: 0.037, 1160: 0.23, 1165: 0.629, 1167: 0.2, 1169: 0.196, 1171: 0.224, 1173: 0.177, 1176: 0.199, 1181: 0.115, 1184: 0.139, 1188: 0.19, 1197: 0.102, 1200: 0.197, 1203: 0.056, 1205: 0.221, 1222: 0.324, 1234: 0.371, 1239: 0.378, 1243: 0.099, 1246: 0.048, 1253: 0.133, 1256: 0.164, 1258: 0.282, 1262: 0.282, 1266: 0.28, 1275: 0.214, 1278: 0.225, 1279: 0.286, 1281: 0.224, 1282: 0.313, 1288: 0.282, 1291: 0.163, 1293: 0.26, 1325: 0.389, 1329: 0.428, 1343: 0.217, 1355: 0.074, 1375: 0.219, 1388: 0.076, 1410: 0.151, 1412: 0.19, 1415: 0.282, 1418: 0.279, 1421: 0.249, 1436: 0.104, 1439: 0.571})
```

```python
import numpy as np
from contextlib import ExitStack
import concourse.bass as bass
import concourse.mybir as mybir
from concourse.bass_utils import run_bass_kernel_spmd
from concourse.alu_op_type import AluOpType as ALU

F32 = mybir.dt.float32
BF16 = mybir.dt.bfloat16
AF = mybir.ActivationFunctionType


COST_US = {}; PRIO_W = 0.0; PE_SCALE = 0.6; FFN_PE_SCALE = 0.05; MIX_PE_SCALE = 1.0


class Res:
    __slots__ = ("name", "t", "writer", "readers", "excl")

    def __init__(self, name, t=None):
        self.name = name
        self.t = t
        self.excl = False
        self.writer = None
        self.readers = []


class Op:
    __slots__ = ("eng", "fn", "kind", "deps", "needed", "token", "idx", "cost", "tbl", "eidx", "fin", "nd", "succ", "waits", "pidx", "rem")

    def __init__(self, eng, fn, kind):
        self.eng = eng
        self.fn = fn
        self.kind = kind
        self.deps = []
        self.needed = False
        self.token = None


class _FakeT:
    def __getitem__(self, k):
        return self

    def rearrange(self, *a, **k):
        return self

    def to_broadcast(self, *a, **k):
        return self


class Sched:
    DMA_SLOTS = {"sp": 12, "pool": 8, "act": 4}

    def __init__(self, nc, stack):
        self.nc = nc
        self.stack = stack
        self.E = dict(pe=nc.tensor, act=nc.scalar, dve=nc.vector, pool=nc.gpsimd, sp=nc.sync)
        self.ops = []
        self.all_res = []
        self.uid = 0
        self.dry = False

    def sb(self, name, shape, dtype, stack=None):
        if self.dry:
            r = Res(name, _FakeT())
            self.all_res.append(r)
            return r
        self.uid += 1
        name = "%s_u%d" % (name, self.uid)
        t = (stack or self.stack).enter_context(self.nc.sbuf_tensor(name, shape, dtype))
        r = Res(name, t)
        self.all_res.append(r)
        return r

    def ps(self, name, shape, dtype, stack=None):
        if self.dry:
            r = Res(name, _FakeT())
            r.excl = True
            self.all_res.append(r)
            return r
        self.uid += 1
        name = "%s_u%d" % (name, self.uid)
        t = (stack or self.stack).enter_context(self.nc.psum_tensor(name, shape, dtype))
        r = Res(name, t)
        r.excl = True
        self.all_res.append(r)
        return r

    def res(self, name, t=None):
        r = Res(name, t)
        self.all_res.append(r)
        return r

    DEFAULT_COST = {"pe": 0.16, "act": 0.45, "dve": 0.30, "pool": 0.45}

    def _add(self, o, r, w):
        ex = [x for x in r if x.excl]
        if ex:
            r = [x for x in r if not x.excl]
            w = list(w) + [x for x in ex if x not in w]
        deps = []
        for x in r:
            if x.writer is not None:
                deps.append(x.writer)
        for x in w:
            if x.writer is not None:
                deps.append(x.writer)
            deps.extend(x.readers)
        for x in r:
            x.readers.append(o)
        for x in w:
            x.writer = o
            x.readers = []
        seen = set()
        for d in deps:
            if d is o or id(d) in seen:
                continue
            seen.add(id(d))
            o.deps.append(d)
        o.idx = len(self.ops)
        self.ops.append(o)
        return o

    def op(self, eng, fn, r=(), w=(), cost=None, tbl=None):
        o = Op(eng, fn, "c")
        o.cost = cost if cost is not None else self.DEFAULT_COST[eng]
        mc = COST_US.get(fn.__code__.co_firstlineno)
        if mc is not None:
            o.cost = (mc * PE_SCALE if eng == "pe" else mc) + 0.03
        o.tbl = tbl
        return self._add(o, r, w)

    def dma(self, q, out, in_, r=(), w=(), cost=3.0, **kw):
        e = self.E[q]
        o = Op(q, (lambda: e.dma_start(out=out, in_=in_, **kw)), "d")
        o.cost = cost
        o.tbl = None
        return self._add(o, r, w)

    def coll(self, kind, in_ap, out_ap, groups, r=(), w=()):
        g = self.nc.gpsimd
        o = Op("pool", (lambda: g.collective_compute(kind, ALU.bypass, groups, [in_ap], [out_ap])), "d")
        o.cost = 50.0
        o.tbl = None
        return self._add(o, r, w)

    def begin(self):
        nc = self.nc
        self.sem = {k: self.stack.enter_context(nc.semaphore("s_" + k)) for k in self.E}
        self.dsem = {q: [self.stack.enter_context(nc.semaphore("d_%s%d" % (q, i))) for i in range(n)]
                     for q, n in self.DMA_SLOTS.items()}
        self.cnt = {k: 0 for k in self.E}
        self.dcnt = {q: 0 for q in self.dsem}
        self.dhist = {q: [] for q in self.dsem}
        self.waited = {k: {} for k in self.E}
        self.barrier_tokens = []
        self.total_ops = 0

    def _wait(self, eng, tok):
        s, v = tok
        key = id(s)
        if self.waited[eng].get(key, 0) >= v:
            return
        self.waited[eng][key] = v
        self.E[eng].wait_ge(s, v)

    REORDER = True

    def _schedule(self):
        import heapq
        ops = self.ops
        if not self.REORDER:
            return list(ops)
        for o in ops:
            o.nd = len(o.deps)
            o.succ = []
            o.fin = 0.0
        for o in ops:
            for d in o.deps:
                d.succ.append(o)
        for o in reversed(ops):
            o.rem = o.cost + max([q.rem for q in o.succ], default=0.0)
        for k, o in enumerate(sorted(ops, key=lambda o: (-(o.rem + PRIO_W * (len(ops) - o.idx)), o.idx))):
            o.pidx = k
        engs = list(self.E.keys())
        fut = {e: [] for e in engs}
        now = {e: {} for e in engs}
        free = {e: 0.0 for e in engs}
        last_tbl = [None]
        LIMIT = 3.0
        for o in ops:
            if o.nd == 0:
                heapq.heappush(fut[o.eng], (0.0, o.pidx, o))
        order = []
        XLAT = 0.7
        n = len(ops)

        def pick_now(e):
            hs = now[e]
            if e != "act":
                h = hs.get(None)
                return (h[0][0], None) if h else None
            best = None
            cur = last_tbl[0]
            for tag in (cur, None):
                h = hs.get(tag)
                if h and (best is None or h[0][0] < best[0]):
                    best = (h[0][0], tag)
            other = None
            for tag, h in hs.items():
                if not h or tag in (cur, None):
                    continue
                if other is None or h[0][0] < other[0]:
                    other = (h[0][0], tag, h[0][1])
            if other is not None and (best is None or other[2] < free[e] - LIMIT):
                return (other[0], other[1])
            return best

        while len(order) < n:
            best = None
            for e in engs:
                f = fut[e]
                while f and f[0][0] <= free[e]:
                    rdy, ix, o = heapq.heappop(f)
                    tag = o.tbl if e == "act" else None
                    heapq.heappush(now[e].setdefault(tag, []), (ix, rdy, o))
                pk = pick_now(e)
                if pk is not None:
                    cand = (free[e], pk[0], e, 0, pk[1])
                elif f:
                    cand = (f[0][0], f[0][1], e, 1, None)
                else:
                    continue
                if best is None or cand[:2] < best[:2]:
                    best = cand
            st, _, e, which, tag = best
            if which == 0:
                _, _, o = heapq.heappop(now[e][tag])
            else:
                _, _, o = heapq.heappop(fut[e])
            c = o.cost
            if o.eng == "act" and o.tbl is not None:
                if last_tbl[0] is not None and last_tbl[0] != o.tbl:
                    c += 1.3
                last_tbl[0] = o.tbl
            if o.kind == "d":
                free[e] = st + 0.15
                o.fin = st + c
            else:
                free[e] = st + c
                o.fin = st + c
            order.append(o)
            for q in o.succ:
                q.nd -= 1
                if q.nd == 0:
                    rdy = 0.0
                    for d in q.deps:
                        t = d.fin + (XLAT if d.eng != q.eng else 0.0)
                        if t > rdy:
                            rdy = t
                    heapq.heappush(fut[q.eng], (rdy, q.pidx, q))
        self.est_us = max(o.fin for o in ops) if ops else 0.0
        return order

    def flush(self):
        self.phase_no = getattr(self, "phase_no", 0) + 1
        self.sem = {k: self.stack.enter_context(self.nc.semaphore("s%d_%s" % (self.phase_no, k))) for k in self.E}
        self.cnt = {k: 0 for k in self.E}
        order = self._schedule()
        if self.dry:
            print("DRY phase ops %d est_us %.1f" % (len(order), self.est_us))
            self.ops = []
            for r in self.all_res:
                r.writer = None
                r.readers = []
            return
        ecount = {k: 0 for k in self.E}
        for o in order:
            o.needed = False
            if o.kind == "c":
                ecount[o.eng] += 1
                o.eidx = ecount[o.eng]
        last = {}
        for o in order:
            if o.kind == "c":
                last[o.eng] = o
            best = {}
            keep = []
            for d in o.deps:
                if d.kind == "d":
                    keep.append(d)
                    continue
                if o.kind == "c" and d.eng == o.eng and o.eng == "pe":
                    continue
                b = best.get(d.eng)
                if b is None or d.eidx > b.eidx:
                    best[d.eng] = d
            keep.extend(best.values())
            o.waits = keep
            for d in keep:
                d.needed = True
        for o in last.values():
            o.needed = True
        first_seen = set()
        for o in order:
            if o.eng not in first_seen:
                first_seen.add(o.eng)
                for tok in self.barrier_tokens:
                    self._wait(o.eng, tok)
            for d in o.waits:
                self._wait(o.eng, d.token)
            if o.kind == "c":
                ins = o.fn()
                if o.needed:
                    self.cnt[o.eng] += 1
                    ins.then_inc(self.sem[o.eng], 1)
                    o.token = (self.sem[o.eng], self.cnt[o.eng])
            else:
                q = o.eng
                K = len(self.dsem[q])
                n = self.dcnt[q]
                if n >= K:
                    self._wait(q, self.dhist[q][n - K])
                s = self.dsem[q][n % K]
                ins = o.fn()
                ins.then_inc(s, 16)
                o.token = (s, 16 * (n // K + 1))
                self.dhist[q].append(o.token)
                self.dcnt[q] += 1
            o.fn = None
            o.deps = None
            o.succ = None
            o.waits = None
        toks = [o.token for o in last.values()]
        for q in self.dsem:
            toks.extend(self.dhist[q][-len(self.dsem[q]):])
        self.barrier_tokens = toks + [t for t in self.barrier_tokens]
        best = {}
        for s_, v in self.barrier_tokens:
            if id(s_) not in best or best[id(s_)][1] < v:
                best[id(s_)] = (s_, v)
        self.barrier_tokens = list(best.values())
        self.total_ops += len(self.ops)
        self.ops = []
        for r in self.all_res:
            r.writer = None
            r.readers = []

    def end(self):
        for tok in self.barrier_tokens:
            self._wait("sp", tok)


D = 1024
SEQ = 4096
NB_ = 4
DEPTH = 2
DFF = 2752
NFT = 22
TT = 256
NT = SEQ // TT
EPS = 1e-6
IN_COLS = 3352


class Ctx:
    pass


def dram_tiles(S, name, ap):
    c = Ctx()
    c.ap = ap
    c.res = [S.res("%s_%d" % (name, i)) for i in range(NT)]
    return c


def tok_tile(ap, i):
    return ap[i * TT:(i + 1) * TT, :].rearrange("(s p) d -> p s d", p=128)


def prologue_ss(S, nc, C, hin):
    with ExitStack() as ph:
        xt = [S.sb("pro_x%d" % k, [128, 2, D], F32, ph) for k in range(2)]
        junk = S.sb("pro_junk", [128, D], BF16, ph)
        for i in range(NT):
            t = xt[i % 2]
            S.dma("sp", t.t[:], tok_tile(hin.ap, i), r=[hin.res[i]], w=[t])
            for s in range(2):
                S.op("dve", (lambda t=t, s=s, i=i: nc.vector.scalar_tensor_tensor(
                    out=junk.t[:], in0=t.t[:, s, :], scalar=1.0, in1=t.t[:, s, :],
                    op0=ALU.mult, op1=ALU.mult, accum_out=C.ss.t[:, 2 * i + s:2 * i + s + 1])),
                    r=[t], w=[junk, C.ss])
        S.flush()


def rstd_from_ss(S, nc, C, ph):
    tmp = S.sb("rs_tmp", [128, 32], F32, ph)
    S.op("dve", lambda: nc.vector.tensor_scalar(out=tmp.t[:], in0=C.ss.t[:], scalar1=1.0 / D, scalar2=EPS,
                                                op0=ALU.mult, op1=ALU.add), r=[C.ss], w=[tmp])
    S.op("act", lambda: nc.scalar.activation(out=tmp.t[:], in_=tmp.t[:], func=AF.Ln), r=[tmp], w=[tmp])
    S.op("act", lambda: nc.scalar.activation(out=C.rstd.t[:], in_=tmp.t[:], func=AF.Exp, scale=-0.5), r=[tmp], w=[C.rstd])


def ffn_phase(S, nc, C, w_gu, w_down, gvec, hin, hout, tag):
    global PE_SCALE, PRIO_W; PE_SCALE = FFN_PE_SCALE; PRIO_W = 1e6
    with ExitStack() as ph:
        wgu = S.sb(tag + "wgu", [128, 8, 2 * DFF], BF16, ph)
        wd = S.sb(tag + "wd", [128, NFT, D], BF16, ph)
        gbc = S.sb(tag + "gbc", [128, D], F32, ph)
        ht = [S.sb(tag + "ht%d" % k, [128, 2, D], F32, ph) for k in range(2)]
        hn = [S.sb(tag + "hn%d" % k, [128, D], BF16, ph) for k in range(2)]
        hnT = [S.sb(tag + "hnT%d" % k, [128, 8, TT], BF16, ph) for k in range(2)]
        actT = [S.sb(tag + "actT%d" % k, [128, NFT, TT], BF16, ph) for k in range(2)]
        sg = [S.sb(tag + "sg%d" % k, [128, TT], F32, ph) for k in range(2)]
        junk = S.sb(tag + "junk", [128, D], BF16, ph)
        pt = S.ps(tag + "pt", [128, 1024], BF16, ph)
        gups = [S.ps(tag + "gu%d" % k, [128, 512], F32, ph) for k in range(3)]
        dps = [S.ps(tag + "dp%d" % k, [128, 512], F32, ph) for k in range(2)]

        rstd_from_ss(S, nc, C, ph)
        cgb = [0, 6, 12, 17, NFT]
        wguR = [[S.res(tag + "wguR%d_%d" % (k, kc)) for kc in range(8)] for k in range(4)]
        cg_of = [max(k for k in range(4) if cgb[k] <= j) for j in range(NFT)]
        for cg in range(4):
            a0, a1 = cgb[cg] * 128, min(cgb[cg + 1] * 128, DFF)
            for kc in range(8):
                S.dma("pool", wgu.t[:, kc, :].rearrange("p (h f) -> p h f", h=2)[:, :, a0:a1],
                      w_gu[kc * 128:(kc + 1) * 128, :].rearrange("p (h f) -> p h f", h=2)[:, :, a0:a1],
                      w=[wguR[cg][kc]], cost=6.0)
        S.dma("pool", wd.t[:, 0:NFT - 1, :], w_down[0:(NFT - 1) * 128, :].rearrange("(c p) n -> p c n", p=128), w=[wd])
        S.dma("pool", wd.t[0:64, NFT - 1, :], w_down[(NFT - 1) * 128:DFF, :], w=[wd])
        S.dma("sp", gbc.t[:], gvec.partition_broadcast(128), w=[gbc])

        def load(i):
            S.dma("sp", ht[i % 2].t[:], tok_tile(hin.ap, i), r=[hin.res[i]], w=[ht[i % 2]])

        load(0)
        gi = 0
        di = 0
        for i in range(NT):
            if i + 1 < NT:
                load(i + 1)
            h = ht[i % 2]
            xT = hnT[i % 2]
            aT = actT[i % 2]
            for s in range(2):
                n_ = hn[s]
                col = 2 * i + s
                S.op("dve", (lambda h=h, s=s, n_=n_, col=col: nc.vector.scalar_tensor_tensor(
                    out=n_.t[:], in0=h.t[:, s, :], scalar=C.rstd.t[:, col:col + 1], in1=gbc.t[:],
                    op0=ALU.mult, op1=ALU.mult)), r=[h, C.rstd, gbc], w=[n_])
                for kc in range(8):
                    S.op("pe", (lambda n_=n_, kc=kc: nc.tensor.transpose(
                        pt.t[:, kc * 128:(kc + 1) * 128], n_.t[:, kc * 128:(kc + 1) * 128], C.ident.t[:])),
                        r=[n_, C.ident], w=[pt])
                S.op("act", (lambda xT=xT, s=s: nc.scalar.copy(
                    out=xT.t[:, :, s * 128:(s + 1) * 128], in_=pt.t[:].rearrange("p (k t) -> p k t", k=8))),
                    r=[pt], w=[xT])
            for j in range(NFT):
                cw = 128 if j < NFT - 1 else 64
                g = gups[gi % 3]
                gi += 1
                for half in range(2):
                    c0 = half * DFF + j * 128
                    for kc in range(8):
                        S.op("pe", (lambda g=g, half=half, c0=c0, cw=cw, kc=kc, xT=xT: nc.tensor.matmul(
                            g.t[0:cw, half * TT:(half + 1) * TT], lhsT=wgu.t[:, kc, c0:c0 + cw], rhs=xT.t[:, kc, :],
                            start=(kc == 0), stop=(kc == 7))), r=[wguR[cg_of[j]][kc], xT], w=[g])
                sgt = sg[j % 2]
                S.op("act", (lambda g=g, cw=cw, sgt=sgt: nc.scalar.activation(
                    out=sgt.t[0:cw, :], in_=g.t[0:cw, 0:TT], func=AF.Silu)), r=[g], w=[sgt])
                S.op("dve", (lambda g=g, cw=cw, sgt=sgt, aT=aT, j=j: nc.vector.tensor_tensor(
                    out=aT.t[0:cw, j, :], in0=sgt.t[0:cw, :], in1=g.t[0:cw, TT:2 * TT], op=ALU.mult)),
                    r=[g, sgt], w=[aT])
            for s in range(2):
                for half in range(2):
                    dp = dps[di % 2]
                    di += 1
                    for c in range(NFT):
                        kw = 128 if c < NFT - 1 else 64
                        S.op("pe", (lambda dp=dp, kw=kw, c=c, s=s, half=half, aT=aT: nc.tensor.matmul(
                            dp.t[:, :], lhsT=aT.t[0:kw, c, s * 128:(s + 1) * 128],
                            rhs=wd.t[0:kw, c, half * 512:(half + 1) * 512],
                            start=(c == 0), stop=(c == NFT - 1))), r=[aT, wd], w=[dp])
                    S.op("dve", (lambda dp=dp, h=h, s=s, half=half: nc.vector.scalar_tensor_tensor(
                        out=h.t[:, s, half * 512:(half + 1) * 512], in0=dp.t[:, :], scalar=0.5,
                        in1=h.t[:, s, half * 512:(half + 1) * 512], op0=ALU.mult, op1=ALU.add)),
                        r=[dp, h], w=[h])
                col = 2 * i + s
                S.op("dve", (lambda h=h, s=s, col=col: nc.vector.scalar_tensor_tensor(
                    out=junk.t[:], in0=h.t[:, s, :], scalar=1.0, in1=h.t[:, s, :],
                    op0=ALU.mult, op1=ALU.mult, accum_out=C.ss.t[:, col:col + 1])), r=[h], w=[junk, C.ss])
            S.dma("sp", tok_tile(hout.ap, i), h.t[:], r=[h], w=[hout.res[i]])
        S.flush()


def final_phase(S, nc, C, gvec, hin, y):
    with ExitStack() as ph:
        gbc = S.sb("fin_gbc", [128, D], F32, ph)
        ht = [S.sb("fin_ht%d" % k, [128, 2, D], F32, ph) for k in range(2)]
        ot = [S.sb("fin_ot%d" % k, [128, 2, D], F32, ph) for k in range(2)]
        rstd_from_ss(S, nc, C, ph)
        S.dma("sp", gbc.t[:], gvec.partition_broadcast(128), w=[gbc])
        for i in range(NT):
            h = ht[i % 2]
            o = ot[i % 2]
            S.dma("sp", h.t[:], tok_tile(hin.ap, i), r=[hin.res[i]], w=[h])
            for s in range(2):
                col = 2 * i + s
                S.op("dve", (lambda h=h, o=o, s=s, col=col: nc.vector.scalar_tensor_tensor(
                    out=o.t[:, s, :], in0=h.t[:, s, :], scalar=C.rstd.t[:, col:col + 1], in1=gbc.t[:],
                    op0=ALU.mult, op1=ALU.mult)), r=[h, C.rstd, gbc], w=[o])
            S.dma("pool", tok_tile(y.ap, i), o.t[:], r=[o], w=[y.res[i]])
        S.flush()


WEIGHT_SPECS = [
    ("ffn1_norm", [DEPTH, D]), ("ffn1_w_gu", [DEPTH, D, 2 * DFF]), ("ffn1_w_down", [DEPTH, DFF, D]),
    ("mix_norm", [DEPTH, D]), ("w_in", [DEPTH, D, IN_COLS]),
    ("cmp_pos_k", [DEPTH, 32, 64]), ("cmp_pos_v", [DEPTH, 32, 64]),
    ("cmp_k_w1", [DEPTH, 2048, 256]), ("cmp_k_w2", [DEPTH, 256, 64]),
    ("cmp_v_w1", [DEPTH, 2048, 256]), ("cmp_v_w2", [DEPTH, 256, 64]),
    ("hgrn_lower_bound", [DEPTH, 512]), ("hgrn_out_norm", [DEPTH, 128]),
    ("w_out", [DEPTH, D, D]), ("ffn2_norm", [DEPTH, D]), ("ffn2_w_gu", [DEPTH, D, 2 * DFF]),
    ("ffn2_w_down", [DEPTH, DFF, D]), ("final_norm", [D]),
]


def build(stages=None):
    nc = bass.Bass("TRN2", target_bir_lowering=False)
    I = {}
    I["x"] = nc.dram_tensor("x", [SEQ, D], F32, kind="ExternalInput").ap()
    for name, shp in WEIGHT_SPECS:
        I[name] = nc.dram_tensor(name, shp, F32, kind="ExternalInput").ap()
    for name, arr in host_consts().items():
        I[name] = nc.dram_tensor(name, list(arr.shape), F32, kind="ExternalInput").ap()
    yap = nc.dram_tensor("y", [SEQ, D], F32, kind="ExternalOutput").ap()
    ha = nc.dram_tensor("h_a", [SEQ, D], F32, kind="Internal").ap()
    hb = nc.dram_tensor("h_b", [SEQ, D], F32, kind="Internal").ap()
    if stages is None:
        stages = ["f1_0", "mix_0", "f2_0", "f1_1", "mix_1", "f2_1"]
    with ExitStack() as st:
        S = Sched(nc, st)
        S.begin()
        C = Ctx()
        C.I = I
        C.ident = S.sb("ident_sb", [128, 128], BF16)
        C.ss = S.sb("ss_sb", [128, 32], F32)
        C.rstd = S.sb("rstd_sb", [128, 32], F32)
        X = dram_tiles(S, "x", I["x"])
        HA = dram_tiles(S, "ha", ha)
        HB = dram_tiles(S, "hb", hb)
        Y = dram_tiles(S, "y", yap)
        S.dma("pool", C.ident.t[:], I["ident"][:, :], w=[C.ident])
        prologue_ss(S, nc, C, X)
        cur = X
        nxt = HA
        for stg in stages:
            kind, l = stg.split("_")
            l = int(l)
            if kind == "f1":
                ffn_phase(S, nc, C, I["ffn1_w_gu"][l], I["ffn1_w_down"][l], I["ffn1_norm"][l], cur, nxt, stg)
            elif kind == "f2":
                ffn_phase(S, nc, C, I["ffn2_w_gu"][l], I["ffn2_w_down"][l], I["ffn2_norm"][l], cur, nxt, stg)
            else:
                mix_phase(S, nc, C, l, cur, nxt, stg)
            cur = nxt
            nxt = HB if cur is HA else HA
        final_phase(S, nc, C, I["final_norm"], cur, Y)
        S.end()
    return nc


_CONSTS = None


def host_consts():
    global _CONSTS
    if _CONSTS is not None:
        return _CONSTS
    c = {}
    c["ident"] = np.eye(128, dtype=np.float32)
    rm = np.ones((128, TT), np.float32)
    rm[:, ::64] = 0.0
    c["c_rmask"] = rm
    c["c_masku"] = np.triu(np.ones((64, 64), np.float32))
    slopes = np.array(SLOPES, np.float64)
    key = np.arange(SEQ)
    c["c_onehot"] = (key[None, :] // 64 == np.arange(64)[:, None]).astype(np.float32)
    c["c_ones"] = np.ones((1, 1024), np.float32)
    vc1 = np.ones((128, 2), np.float32)
    vc1[127, 1] = 0.0
    c["c_vc1"] = vc1
    cc = np.arange(256)[:, None]
    nn = np.arange(64)[None, :]
    m = ((cc >= 4 * nn) & (cc <= 4 * nn + 3)).astype(np.float32) + ((cc >= 4 * nn - 1) & (cc <= 4 * nn + 2)).astype(np.float32)
    m[255, :] = 0.0
    c["c_mslc"] = m
    q = np.arange(6144)[None, :] - 2048
    dist = q - 16 * np.arange(128)[:, None] - 31
    c["c_dtab"] = np.where(dist >= 0, -dist, -1e7).astype(np.float32)
    pos = np.arange(SEQ)[:, None]
    cur = pos // 64
    blk = np.arange(64)[None, :]
    forced = (blk == 0) | (blk == cur) | (blk == cur - 1)
    c["c_force"] = np.where(forced, 1e6, 0.0).astype(np.float32)
    p = np.arange(128)[:, None, None]
    idx = np.arange(34)[None, None, :]
    c["c_btab"] = (slopes[None, :, None] * (p - 128.0 * (idx - 1))).astype(np.float32).reshape(128, 8 * 34)
    ss_ = np.arange(2)[None, None, :]
    c["c_negb"] = (-BIG - slopes[None, :, None] * (128.0 * ss_ + p)).astype(np.float32).reshape(128, 16)
    c["c_qrow"] = (-slopes[:, None] * np.arange(TT)[None, :]).astype(np.float32)
    kk = np.arange(128)[:, None]
    qq = np.arange(128)[None, :]
    c["c_tri"] = (kk <= qq).astype(np.float32)
    c["c_strict"] = (kk > qq).astype(np.float32)
    _CONSTS = c
    return c


def kernel(stages=None, ncores=4, **inputs):
    nc = build(stages)
    consts = host_consts()
    shared = {k: np.ascontiguousarray(np.asarray(inputs[k], dtype=np.float32)) for k, _ in WEIGHT_SPECS}
    x = np.asarray(inputs["x"], dtype=np.float32)
    in_maps = []
    for b in range(ncores):
        m = dict(shared)
        m.update(consts)
        m["x"] = np.ascontiguousarray(x[b])
        in_maps.append(m)
    res = run_bass_kernel_spmd(nc, in_maps, core_ids=list(range(ncores)))
    out = np.stack([np.asarray(res.results[b]["y"], dtype=np.float32) for b in range(ncores)], axis=0)
    return out


OQ, OKC, OVC, OKS, OVS, OKW, OVW, OGT, OHQ, OHF, OHI, OHG = 0, 512, 640, 768, 896, 1024, 1152, 1280, 1304, 1816, 2328, 2840
BIG = 30000.0
ENABLE_NSA = True
NPT, NPF, NPS = 3, 2, 3
SLOPES = [2.0 ** (-(h + 1)) for h in range(8)]
GELU_C = 1.5957691216057308


def mix_phase(S, nc, C, l, hin, hout, tag):
    global PE_SCALE, PRIO_W; PE_SCALE = MIX_PE_SCALE; PRIO_W = 0.05; """h <- h + hybrid_mixer(rmsnorm(h)).  Per 256-token tile: a front end (norm, projections, compress,
    compressed attention, top-k, HGRN2) and a back end (selected/window attention, gating, output projection);
    tile-local buffers are double-buffered so the scheduler can overlap front end i+1 with back end i."""
    I = C.I
    with ExitStack() as ph:
        def sb(name, shape, dt):
            return S.sb(tag + name, shape, dt, ph)

        win = sb("win", [128, 8, IN_COLS], BF16)
        wout = sb("wout", [128, 8, D], BF16)
        gcol = sb("gcol", [128, 8], F32)
        ht = sb("ht", [128, 2, D], F32)
        hn = sb("hn", [128, D], BF16)
        hnT = sb("hnT", [128, 8, TT], BF16)
        oT = [sb("oT%d" % k, [128, 8, TT], BF16) for k in range(2)]
        hres = [sb("hres%d" % k, [128, 512], F32) for k in range(2)]
        ssh = sb("ssh", [128, 2], F32)
        lbt = sb("lbt", [128, 4], F32)
        omlt = sb("omlt", [128, 4], F32)
        nomlt = sb("nomlt", [128, 4], F32)
        lbtmp = sb("lbtmp", [128, 8], F32)
        rmask = sb("rmask", [128, TT], BF16)
        maskU = sb("maskU", [64, 64], F32)
        onbc = sb("onbc", [64, 512], F32)
        hv = sb("hv", [64, 4, 512], BF16)
        gn = sb("gn", [64, 4, 512], BF16)
        state = sb("state", [128, 4, 128], F32)
        statebf = sb("statebf", [128, 4, 128], BF16)
        osb = [sb("osb%d" % k, [64, 4, 128], F32) for k in range(2)]
        ssq = [sb("ssq%d" % k, [64, 4], F32) for k in range(2)]
        rsq = [sb("rsq%d" % k, [64, 4], F32) for k in range(2)]
        ohg = [sb("ohg%d" % k, [64, 128], BF16) for k in range(2)]
        ef = [sb("ef%d" % k, [128, TT], F32) for k in range(6)]
        sig2 = sb("sig2", [128, 2 * TT], F32)
        eb = [sb("eb%d" % k, [128, TT], BF16) for k in range(4)]
        aTm = [sb("aTm%d" % k, [64, 64], BF16) for k in range(4)]
        khT = sb("khT", [64, 512], BF16)
        junkh = sb("junkh", [64, 128], BF16)

        pt = S.ps(tag + "pt", [128, 1024], BF16, ph)
        poolF = [S.ps(tag + "pf%d" % k, [128, 512], F32, ph) for k in range(NPF)]
        poolS = [S.ps(tag + "psc%d" % k, [128, 512], F32, ph) for k in range(NPS)]
        acc = [S.ps(tag + "acc%d" % k, [128, 512], F32, ph) for k in range(2)]
        pcnt = [0, 0]

        def nb():
            pcnt[0] += 1
            return poolF[pcnt[0] % len(poolF)]

        def nbS():
            pcnt[1] += 1
            return poolS[pcnt[1] % len(poolS)]

        rstd_from_ss(S, nc, C, ph)
        wgb = [0, OKC, OHQ, IN_COLS]
        winR = [[S.res(tag + "winR%d_%d" % (k, kc)) for kc in range(8)] for k in range(3)]
        for cg in range(3):
            for kc in range(8):
                S.dma("pool", win.t[:, kc, wgb[cg]:wgb[cg + 1]], I["w_in"][l, kc * 128:(kc + 1) * 128, wgb[cg]:wgb[cg + 1]],
                      w=[winR[cg][kc]], cost=4.0)
        S.dma("pool", wout.t[:], I["w_out"][l].rearrange("(c p) n -> p c n", p=128), w=[wout], cost=10.0)
        S.dma("sp", gcol.t[:], I["mix_norm"][l].rearrange("(c p) -> p c", p=128), w=[gcol], allow_slow_non_contiguous=True)
        for cg in range(3):
            for kc in range(8):
                e_ = "dve" if kc % 2 == 0 else "pool"
                eng_ = nc.vector if kc % 2 == 0 else nc.gpsimd
                S.op(e_, (lambda kc=kc, eng_=eng_, cg=cg: eng_.tensor_scalar(
                    out=win.t[:, kc, wgb[cg]:wgb[cg + 1]], in0=win.t[:, kc, wgb[cg]:wgb[cg + 1]],
                    scalar1=gcol.t[:, kc:kc + 1], scalar2=None, op0=ALU.mult)),
                    r=[gcol, winR[cg][kc]], w=[winR[cg][kc]], cost=0.3 + 0.001 * (wgb[cg + 1] - wgb[cg]))
        S.dma("pool", rmask.t[:], I["c_rmask"][:, :], w=[rmask])
        S.dma("sp", maskU.t[:], I["c_masku"][:, :], w=[maskU])
        for hh in range(4):
            S.dma("sp", onbc.t[:, hh * 128:(hh + 1) * 128], I["hgrn_out_norm"][l].partition_broadcast(64), w=[onbc])
        S.dma("sp", lbtmp.t[:, 0:4], I["hgrn_lower_bound"][0].rearrange("(h d) -> d h", d=128), w=[lbtmp],
              allow_slow_non_contiguous=True)
        S.dma("sp", lbtmp.t[:, 4:8], I["hgrn_lower_bound"][1].rearrange("(h d) -> d h", d=128), w=[lbtmp],
              allow_slow_non_contiguous=True)
        if l == 0:
            S.op("dve", lambda: nc.vector.memset(lbt.t[:], 0.0), w=[lbt])
        else:
            S.op("dve", lambda: nc.vector.tensor_tensor(out=lbtmp.t[:, 0:4], in0=lbtmp.t[:, 4:8], in1=lbtmp.t[:, 0:4],
                                                        op=ALU.subtract), r=[lbtmp], w=[lbtmp])
            S.op("act", lambda: nc.scalar.activation(out=lbt.t[:], in_=lbtmp.t[:, 0:4], func=AF.Tanh, scale=0.5),
                 r=[lbtmp], w=[lbt], tbl="exp")
            S.op("dve", lambda: nc.vector.tensor_scalar(out=lbt.t[:], in0=lbt.t[:], scalar1=0.5, scalar2=0.5,
                                                        op0=ALU.mult, op1=ALU.add), r=[lbt], w=[lbt])
        S.op("dve", lambda: nc.vector.tensor_scalar(out=omlt.t[:], in0=lbt.t[:], scalar1=-1.0, scalar2=1.0,
                                                    op0=ALU.mult, op1=ALU.add), r=[lbt], w=[omlt])
        S.op("dve", lambda: nc.vector.tensor_scalar(out=nomlt.t[:], in0=omlt.t[:], scalar1=-1.0, scalar2=None,
                                                    op0=ALU.mult), r=[omlt], w=[nomlt])
        S.op("dve", lambda: nc.vector.memset(state.t[:], 0.0), w=[state])
        S.op("dve", lambda: nc.vector.memset(statebf.t[:], 0.0), w=[statebf])

        NS = None
        if ENABLE_NSA:
            NS = nsa_setup(S, nc, C, l, tag, ph, sb)

        def proj_fm(bank, c0, c1, col0, ncols):
            for kc in range(8):
                S.op("pe", (lambda kc=kc: nc.tensor.matmul(
                    bank.t[0:ncols, c0:c1], lhsT=win.t[:, kc, col0:col0 + ncols], rhs=hnT.t[:, kc, :],
                    start=(kc == 0), stop=(kc == 7))), r=[winR[0 if col0 < OKC else (1 if col0 < OHQ else 2)][kc], hnT], w=[bank], cost=0.12)

        def proj_tm(bank, t0, nt, col0, ncols, o0=0):
            for kc in range(8):
                S.op("pe", (lambda kc=kc: nc.tensor.matmul(
                    bank.t[0:nt, o0:o0 + ncols], lhsT=hnT.t[:, kc, t0:t0 + nt], rhs=win.t[:, kc, col0:col0 + ncols],
                    start=(kc == 0), stop=(kc == 7))), r=[winR[0 if col0 < OKC else (1 if col0 < OHQ else 2)][kc], hnT], w=[bank], cost=0.08 + ncols * 0.00045)

        for i in range(NT):
            par = i % 2
            oTi = oT[par]
            S.dma("sp", ht.t[:], tok_tile(hin.ap, i), r=[hin.res[i]], w=[ht])
            for s in range(2):
                col = 2 * i + s
                S.op("dve", (lambda s=s, col=col: nc.vector.tensor_scalar(
                    out=hn.t[:], in0=ht.t[:, s, :], scalar1=C.rstd.t[:, col:col + 1], scalar2=None, op0=ALU.mult)),
                    r=[ht, C.rstd], w=[hn], cost=0.6)
                for kc in range(8):
                    S.op("pe", (lambda kc=kc: nc.tensor.transpose(
                        pt.t[:, kc * 128:(kc + 1) * 128], hn.t[:, kc * 128:(kc + 1) * 128], C.ident.t[:])),
                        r=[hn, C.ident], w=[pt], cost=0.08)
                S.op("act", (lambda s=s: nc.scalar.copy(
                    out=hnT.t[:, :, s * 128:(s + 1) * 128], in_=pt.t[:].rearrange("p (k t) -> p k t", k=8))),
                    r=[pt], w=[hnT], cost=1.0)

            if ENABLE_NSA:
                nsa_front(S, nc, C, l, i, NS, win, hnT, pt, nb, acc, proj_fm, proj_tm)

            sgl = sig2
            for c in range(4):
                b1 = nb()
                proj_tm(b1, c * 64, 64, OHI, 512)
                S.op("act", (lambda b1=b1, c=c: nc.scalar.copy(out=hv.t[:, c, :], in_=b1.t[0:64, :])), r=[b1], w=[hv])
                b2 = nb()
                proj_tm(b2, c * 64, 64, OHG, 512)
                S.op("act", (lambda b2=b2: nc.scalar.activation(out=sgl.t[0:64, :], in_=b2.t[0:64, :], func=AF.Tanh, scale=0.5)),
                     r=[b2], w=[sgl], tbl="exp")
                S.op("pool", lambda: nc.gpsimd.tensor_scalar(out=sgl.t[0:64, :], in0=sgl.t[0:64, :], scalar1=0.5, scalar2=0.5,
                                                             op0=ALU.mult, op1=ALU.add), r=[sgl], w=[sgl], cost=0.6)
                S.op("dve", (lambda b2=b2: nc.vector.tensor_tensor(out=sgl.t[0:64, :], in0=sgl.t[0:64, :], in1=b2.t[0:64, :],
                                                                   op=ALU.mult)), r=[sgl, b2], w=[sgl], cost=0.55)
                S.op("pool", (lambda c=c: nc.gpsimd.tensor_tensor(out=gn.t[:, c, :], in0=sgl.t[0:64, :], in1=onbc.t[:],
                                                                  op=ALU.mult)), r=[sgl, onbc], w=[gn], cost=0.8)
            for hh in range(4):
                zq = nb()
                proj_fm(zq, 0, TT, OHF + hh * 128, 128)
                proj_fm(zq, TT, 2 * TT, OHQ + hh * 128, 128)
                f, logf, kf, b, bm, bl = ef
                e1, e2, e3, e4 = ef[0], ef[1], ef[4], ef[5]
                qt, kt, qd, kh = eb
                ob, sq_, rs_ = osb[hh % 2], ssq[hh % 2], rsq[hh % 2]
                S.op("act", (lambda zq=zq: nc.scalar.activation(out=sig2.t[:], in_=zq.t[:, 0:2 * TT], func=AF.Tanh, scale=0.5)),
                     r=[zq], w=[sig2], tbl="exp", cost=0.6)
                S.op("pool", lambda: nc.gpsimd.tensor_scalar(out=sig2.t[:], in0=sig2.t[:], scalar1=0.5, scalar2=0.5,
                                                             op0=ALU.mult, op1=ALU.add), r=[sig2], w=[sig2], cost=0.9)
                S.op("dve", (lambda zq=zq: nc.vector.tensor_tensor(out=sig2.t[:, TT:2 * TT], in0=sig2.t[:, TT:2 * TT],
                                                                   in1=zq.t[:, TT:2 * TT], op=ALU.mult)),
                     r=[zq, sig2], w=[sig2])
                S.op("dve", (lambda hh=hh: nc.vector.tensor_scalar(
                    out=f.t[:], in0=sig2.t[:, 0:TT], scalar1=omlt.t[:, hh:hh + 1], scalar2=lbt.t[:, hh:hh + 1],
                    op0=ALU.mult, op1=ALU.add)), r=[sig2, omlt, lbt], w=[f])
                S.op("act", lambda: nc.scalar.activation(out=logf.t[:], in_=f.t[:], func=AF.Ln), r=[f], w=[logf], tbl="exp")
                S.op("pool", (lambda hh=hh: nc.gpsimd.tensor_scalar(
                    out=kf.t[:], in0=sig2.t[:, 0:TT], scalar1=nomlt.t[:, hh:hh + 1], scalar2=omlt.t[:, hh:hh + 1],
                    op0=ALU.mult, op1=ALU.add)), r=[sig2, omlt, nomlt], w=[kf])
                S.op("dve", lambda: nc.vector.tensor_tensor_scan(out=b.t[:], data0=rmask.t[:], data1=logf.t[:],
                                                                 initial=0.0, op0=ALU.mult, op1=ALU.add),
                     r=[rmask, logf], w=[b], cost=0.6)
                b3 = b.t[:].rearrange("p (c t) -> p c t", c=4)
                S.op("dve", lambda: nc.vector.tensor_tensor(
                    out=bm.t[:].rearrange("p (c t) -> p c t", c=4), in0=b3,
                    in1=b3[:, :, 31:32].to_broadcast([128, 4, 64]), op=ALU.subtract), r=[b], w=[bm], cost=0.55)
                S.op("dve", lambda: nc.vector.tensor_tensor(
                    out=bl.t[:].rearrange("p (c t) -> p c t", c=4), in0=b3,
                    in1=b3[:, :, 63:64].to_broadcast([128, 4, 64]), op=ALU.subtract), r=[b], w=[bl], cost=0.55)
                S.op("act", lambda: nc.scalar.activation(out=e1.t[:], in_=bm.t[:], func=AF.Exp), r=[bm], w=[e1], tbl="exp")
                S.op("act", lambda: nc.scalar.activation(out=e2.t[:], in_=bm.t[:], func=AF.Exp, scale=-1.0),
                     r=[bm], w=[e2], tbl="exp")
                S.op("act", lambda: nc.scalar.activation(out=e4.t[:], in_=bl.t[:], func=AF.Exp, scale=-1.0),
                     r=[bl], w=[e4], tbl="exp")
                S.op("act", lambda: nc.scalar.activation(out=e3.t[:], in_=b.t[:], func=AF.Exp), r=[b], w=[e3], tbl="exp")
                S.op("pool", lambda: nc.gpsimd.tensor_tensor(out=qt.t[:], in0=sig2.t[:, TT:2 * TT], in1=e1.t[:], op=ALU.mult),
                     r=[sig2, e1], w=[qt])
                S.op("pool", lambda: nc.gpsimd.tensor_tensor(out=kt.t[:], in0=kf.t[:], in1=e2.t[:], op=ALU.mult),
                     r=[kf, e2], w=[kt])
                S.op("pool", lambda: nc.gpsimd.tensor_tensor(out=kh.t[:], in0=kf.t[:], in1=e4.t[:], op=ALU.mult),
                     r=[kf, e4], w=[kh])
                S.op("pool", lambda: nc.gpsimd.tensor_tensor(out=qd.t[:], in0=sig2.t[:, TT:2 * TT], in1=e3.t[:], op=ALU.mult),
                     r=[sig2, e3], w=[qd])
                for c in range(4):
                    cs = slice(c * 64, (c + 1) * 64)
                    ba = nb()
                    S.op("pe", (lambda ba=ba, cs=cs: nc.tensor.matmul(ba.t[0:64, 0:64], lhsT=kt.t[:, cs], rhs=qt.t[:, cs],
                                                                      start=True, stop=True)), r=[kt, qt], w=[ba], cost=0.07)
                    am = aTm[c]
                    S.op("dve", (lambda ba=ba, am=am: nc.vector.tensor_tensor(out=am.t[:], in0=ba.t[0:64, 0:64],
                                                                              in1=maskU.t[:], op=ALU.mult)),
                         r=[ba, maskU], w=[am], cost=0.12)
                    S.op("pe", (lambda cs=cs, c=c: nc.tensor.transpose(pt.t[0:64, c * 128:(c + 1) * 128], kh.t[:, cs],
                                                                       C.ident.t[:])), r=[kh, C.ident], w=[pt], cost=0.08)
                S.op("dve", lambda: nc.vector.tensor_copy(out=khT.t[:], in_=pt.t[0:64, 0:512]), r=[pt], w=[khT], cost=0.3)
                for c in range(4):
                    cs = slice(c * 64, (c + 1) * 64)
                    am = aTm[c]
                    bo = nb()
                    S.op("pe", (lambda bo=bo, am=am, c=c, hh=hh: nc.tensor.matmul(
                        bo.t[0:64, 0:128], lhsT=am.t[:], rhs=hv.t[:, c, hh * 128:(hh + 1) * 128],
                        start=True, stop=False)), r=[am, hv], w=[bo], cost=0.08)
                    S.op("pe", (lambda bo=bo, cs=cs, hh=hh: nc.tensor.matmul(
                        bo.t[0:64, 0:128], lhsT=qd.t[:, cs], rhs=statebf.t[:, hh, :],
                        start=False, stop=True)), r=[qd, statebf], w=[bo], cost=0.08)
                    S.op("pe", (lambda bo=bo, c=c, hh=hh: nc.tensor.matmul(
                        bo.t[:, 128:256], lhsT=khT.t[:, c * 128:(c + 1) * 128],
                        rhs=hv.t[:, c, hh * 128:(hh + 1) * 128], start=True, stop=True)), r=[khT, hv], w=[bo], cost=0.08)
                    S.op("dve", (lambda bo=bo, c=c, hh=hh: nc.vector.scalar_tensor_tensor(
                        out=state.t[:, hh, :], in0=state.t[:, hh, :], scalar=e3.t[:, c * 64 + 63:c * 64 + 64],
                        in1=bo.t[:, 128:256], op0=ALU.mult, op1=ALU.add)), r=[state, e3, bo], w=[state], cost=0.2)
                    S.op("pool", (lambda hh=hh: nc.gpsimd.tensor_copy(out=statebf.t[:, hh, :], in_=state.t[:, hh, :])),
                         r=[state], w=[statebf], cost=0.25)
                    S.op("act", (lambda bo=bo, c=c, ob=ob: nc.scalar.copy(out=ob.t[:, c, :], in_=bo.t[0:64, 0:128])),
                         r=[bo], w=[ob], cost=0.25)
                    S.op("dve", (lambda c=c, ob=ob, sq_=sq_: nc.vector.scalar_tensor_tensor(
                        out=junkh.t[:], in0=ob.t[:, c, :], scalar=1.0, in1=ob.t[:, c, :],
                        op0=ALU.mult, op1=ALU.mult, accum_out=sq_.t[:, c:c + 1])), r=[ob], w=[junkh, sq_], cost=0.25)
                S.op("dve", (lambda sq_=sq_, rs_=rs_: nc.vector.tensor_scalar(
                    out=rs_.t[:], in0=sq_.t[:], scalar1=1.0 / 128, scalar2=EPS, op0=ALU.mult, op1=ALU.add)),
                    r=[sq_], w=[rs_], cost=0.1)
                S.op("act", (lambda rs_=rs_: nc.scalar.activation(out=rs_.t[:], in_=rs_.t[:], func=AF.Ln)),
                     r=[rs_], w=[rs_], tbl="exp", cost=0.2)
                S.op("act", (lambda rs_=rs_: nc.scalar.activation(out=rs_.t[:], in_=rs_.t[:], func=AF.Exp, scale=-0.5)),
                     r=[rs_], w=[rs_], tbl="exp", cost=0.2)
                for c in range(4):
                    og = ohg[c % 2]
                    S.op("dve", (lambda og=og, c=c, hh=hh, ob=ob, rs_=rs_: nc.vector.scalar_tensor_tensor(
                        out=og.t[:], in0=ob.t[:, c, :], scalar=rs_.t[:, c:c + 1],
                        in1=gn.t[:, c, hh * 128:(hh + 1) * 128], op0=ALU.mult, op1=ALU.mult)),
                        r=[ob, rs_, gn], w=[og], cost=0.15)
                    S.op("pe", (lambda og=og, c=c: nc.tensor.transpose(
                        pt.t[:, c * 64:(c + 1) * 64], og.t[:], C.ident.t[0:64, 0:64])),
                        r=[og, C.ident], w=[pt], cost=0.08)
                S.op("act", (lambda hh=hh, oTi=oTi: nc.scalar.copy(out=oTi.t[:, 4 + hh, :], in_=pt.t[:, 0:TT])),
                     r=[pt], w=[oTi], cost=0.3)

            if ENABLE_NSA:
                nsa_back(S, nc, C, l, i, NS, oTi, pt, nbS, acc)
            else:
                S.op("dve", (lambda oTi=oTi: nc.vector.memset(oTi.t[:, 0:4, :], 0.0)), w=[oTi])

            ek = 0
            for s in range(2):
                for half in range(2):
                    hr = hres[ek % 2]
                    ek += 1
                    rows = slice(i * TT + s * 128, i * TT + (s + 1) * 128)
                    cols = slice(half * 512, (half + 1) * 512)
                    S.dma("sp", hr.t[:], hin.ap[rows, cols], r=[hin.res[i]], w=[hr])
                    dp = nbS()
                    for kc in range(8):
                        S.op("pe", (lambda dp=dp, kc=kc, s=s, half=half, oTi=oTi: nc.tensor.matmul(
                            dp.t[:, :], lhsT=oTi.t[:, kc, s * 128:(s + 1) * 128],
                            rhs=wout.t[:, kc, half * 512:(half + 1) * 512],
                            start=(kc == 0), stop=(kc == 7))), r=[oTi, wout], w=[dp], cost=0.23)
                    S.op("dve", (lambda dp=dp, hr=hr: nc.vector.tensor_tensor(
                        out=hr.t[:], in0=dp.t[:, :], in1=hr.t[:], op=ALU.add)), r=[dp, hr], w=[hr], cost=0.55)
                    jk = NS.onsa if ENABLE_NSA else hn
                    jk_ap = jk.t[:, 0, :] if ENABLE_NSA else jk.t[:, 0:512]
                    S.op("dve", (lambda hr=hr, half=half, jk_ap=jk_ap: nc.vector.scalar_tensor_tensor(
                        out=jk_ap, in0=hr.t[:], scalar=1.0, in1=hr.t[:],
                        op0=ALU.mult, op1=ALU.mult, accum_out=ssh.t[:, half:half + 1])), r=[hr], w=[jk, ssh], cost=0.6)
                    S.dma("sp", hout.ap[rows, cols], hr.t[:], r=[hr], w=[hout.res[i]])
                col = 2 * i + s
                S.op("dve", (lambda col=col: nc.vector.tensor_tensor(
                    out=C.ss.t[:, col:col + 1], in0=ssh.t[:, 0:1], in1=ssh.t[:, 1:2], op=ALU.add)),
                    r=[ssh], w=[C.ss], cost=0.1)
        S.flush()


def nsa_setup(S, nc, C, l, tag, ph, sb):
    I = C.I
    N = Ctx()
    N.KS = [sb("KS%d" % g, [128, SEQ], BF16) for g in range(2)]
    N.KSr = [[S.res("KSr%d_%d" % (g, k)) for k in range(32)] for g in range(2)]
    N.KW = [sb("KW%d" % g, [128, 1024], BF16) for g in range(2)]
    N.KWr = [[S.res("KWr%d_%d" % (g, k)) for k in range(8)] for g in range(2)]
    N.VS = sb("VS", [128, 2, 32, 65], BF16)
    N.VSr = [S.res("VSr%d" % k) for k in range(32)]
    N.VW = sb("VW", [128, 2, 8, 65], BF16)
    N.VWr = [S.res("VWr%d" % k) for k in range(8)]
    N.KcT = [sb("KcT%d" % g, [64, 256], BF16) for g in range(2)]
    N.VcA = sb("VcA", [128, 2, 2, 65], BF16)
    N.Mslc = sb("Mslc", [128, 2, 64], BF16)
    N.XC = [[sb("XC%d%d" % (kv, g), [128, 272], BF16) for g in range(2)] for kv in range(2)]
    N.cw1 = [sb("cw1_%d" % kv, [128, 16, 256], BF16) for kv in range(2)]
    N.cw2 = [sb("cw2_%d" % kv, [128, 2, 64], BF16) for kv in range(2)]
    N.posT = [sb("posT%d" % kv, [128, 16], BF16) for kv in range(2)]
    N.hb = [sb("hb%d" % kv, [128, 2], F32) for kv in range(2)]
    N.Dq = sb("Dq", [128, 2, TT], F32)
    N.force = sb("force", [128, 2, 64], F32)
    N.btab = sb("btab", [128, 8, 34], F32)
    N.negb = sb("negb", [128, 16], F32)
    N.QS = [[sb("QS%d_%d" % (p, h), [128, TT], BF16) for h in range(8)] for p in range(2)]
    N.tri = sb("tri", [128, 128], BF16)
    N.strict = sb("strict", [128, 128], BF16)
    N.gsig = [sb("gsig%d" % p, [128, 2, 24], F32) for p in range(2)]
    N.ocmp = [sb("ocmp%d" % p, [128, 2, 8, 64], F32) for p in range(2)]
    N.imp = sb("imp", [128, 2, 2, 64], F32)
    N.sbs = [sb("sbs%d" % k, [128, TT], F32) for k in range(1)]
    N.PT = [sb("PT%d" % k, [128, TT], BF16) for k in range(NPT)]
    N.PTc = [sb("PTc%d" % k, [128, TT], BF16) for k in range(2)]
    N.onsa = sb("onsa", [128, 2, 512], BF16)
    N.h1pad = sb("h1pad", [128, 2, 256], BF16)
    N.h1g = sb("h1g", [128, 2, 16], BF16)
    N.gx = sb("gx", [128, 32], F32)
    N.gt = sb("gt", [128, 32], F32)
    N.gs = sb("gs", [128, 32], F32)
    N.Mq = [sb("Mq%d" % k, [128, 128], BF16) for k in range(2)]
    N.scr = sb("scr", [128, 64], F32)
    N.scr2 = sb("scr2", [128, 64], F32)
    N.m8a = sb("m8a", [128, 8], F32)
    N.m8b = sb("m8b", [128, 8], F32)
    N.msk = sb("msk", [128, 64], F32)
    N.rzc = [sb("rzc%d" % k, [128, 2], F32) for k in range(2)]
    N.rz = [[sb("rz%d_%d" % (k, s), [128, 4], F32) for s in range(2)] for k in range(2)]
    N.fo = [[sb("fo%d_%d" % (k, s), [128, 64], F32) for s in range(2)] for k in range(2)]
    N.ptcnt = 0
    N.pccnt = 0
    N.sbcnt = 0

    for g in range(2):
        S.dma("pool", N.KS[g].t[64:128, :], I["c_onehot"][:, :], w=N.KSr[g], cost=8.0)
        S.op("pool", (lambda g=g: nc.gpsimd.memset(N.KW[g].t[:], 0.0)), w=N.KWr[g], cost=1.0)
        S.dma("pool", N.KW[g].t[64:65, :], I["c_ones"][:, :], w=N.KWr[g])
        S.op("pool", (lambda g=g: nc.gpsimd.memset(N.KcT[g].t[:], 0.0)), w=[N.KcT[g]])
        for kv in range(2):
            S.op("pool", (lambda g=g, kv=kv: nc.gpsimd.memset(N.XC[kv][g].t[:], 0.0)), w=[N.XC[kv][g]])
    S.op("pool", lambda: nc.gpsimd.memset(N.VS.t[:], 1.0), w=N.VSr, cost=4.0)
    S.op("pool", lambda: nc.gpsimd.memset(N.VW.t[:], 1.0), w=N.VWr, cost=1.0)
    S.op("pool", lambda: nc.gpsimd.memset(N.VcA.t[:], 0.0), w=[N.VcA])
    S.op("pool", lambda: nc.gpsimd.memset(N.h1pad.t[:], 0.0), w=[N.h1pad])
    for k in range(2):
        S.op("pool", (lambda k=k: nc.gpsimd.memset(N.Mq[k].t[:], 0.0)), w=[N.Mq[k]])
    for g in range(2):
        S.dma("pool", N.VcA.t[:, g, :, 64], I["c_vc1"][:, :], w=[N.VcA], allow_slow_non_contiguous=True)
    S.dma("pool", N.Mslc.t[:], I["c_mslc"].rearrange("(t p) n -> p t n", p=128), w=[N.Mslc])
    for kv in range(2):
        w1 = I["cmp_k_w1" if kv == 0 else "cmp_v_w1"][l]
        w2 = I["cmp_k_w2" if kv == 0 else "cmp_v_w2"][l]
        pe_ = I["cmp_pos_k" if kv == 0 else "cmp_pos_v"][l]
        S.dma("pool", N.cw1[kv].t[:], w1.rearrange("(c p) h -> p c h", p=128), w=[N.cw1[kv]], cost=8.0)
        S.dma("pool", N.cw2[kv].t[:], w2.rearrange("(c p) n -> p c n", p=128), w=[N.cw2[kv]])
        S.dma("pool", N.posT[kv].t[:], pe_.rearrange("(c two) d -> (two d) c", two=2), w=[N.posT[kv]],
              allow_slow_non_contiguous=True)
    S.dma("sp", N.btab.t[:], I["c_btab"].rearrange("p (h k) -> p h k", h=8), w=[N.btab])
    S.dma("sp", N.negb.t[:], I["c_negb"][:, :], w=[N.negb])
    S.dma("pool", N.tri.t[:], I["c_tri"][:, :], w=[N.tri])
    S.dma("pool", N.strict.t[:], I["c_strict"][:, :], w=[N.strict])
    return N


def nsa_hidden_bias(S, nc, N, nb):
    for kv in range(2):
        bank = nb()
        for half in range(2):
            for c in range(16):
                S.op("pe", (lambda kv=kv, half=half, c=c, bank=bank: nc.tensor.matmul(
                    bank.t[:, half:half + 1], lhsT=N.cw1[kv].t[:, c, half * 128:(half + 1) * 128],
                    rhs=N.posT[kv].t[:, c:c + 1], start=(c == 0), stop=(c == 15))),
                    r=[N.cw1[kv], N.posT[kv]], w=[bank], cost=0.07)
        S.op("dve", (lambda kv=kv, bank=bank: nc.vector.tensor_copy(out=N.hb[kv].t[:], in_=bank.t[:, 0:2])),
             r=[bank], w=[N.hb[kv]])


def nsa_front(S, nc, C, l, i, N, win, hnT, pt, nb, acc, proj_fm, proj_tm):
    I = C.I
    q0 = i * TT
    par = i % 2
    QS = N.QS[par]
    gsig = N.gsig[par]
    ocmp = N.ocmp[par]
    if i == 0:
        nsa_hidden_bias(S, nc, N, nb)

    S.dma("sp", N.Dq.t[:, 0, :], I["c_dtab"][:, 2048 + q0:2048 + q0 + TT], w=[N.Dq])
    S.dma("sp", N.Dq.t[:, 1, :], I["c_dtab"][:, q0:q0 + TT], w=[N.Dq])
    S.dma("sp", N.force.t[:], I["c_force"][q0:q0 + TT, :].rearrange("(s p) n -> p s n", p=128), w=[N.force])

    for hp in range(4):
        bank = nb()
        for k in range(2):
            proj_fm(bank, k * TT, (k + 1) * TT, OQ + 64 * (2 * hp + k), 64)
        for k in range(2):
            hq = 2 * hp + k
            if k == 0:
                S.op("act", (lambda bank=bank, k=k, hq=hq: nc.scalar.activation(
                    out=QS[hq].t[0:64, :], in_=bank.t[0:64, k * TT:(k + 1) * TT], func=AF.Copy, scale=0.125)),
                    r=[bank], w=[QS[hq]], cost=0.3)
            else:
                S.op("dve", (lambda bank=bank, k=k, hq=hq: nc.vector.tensor_scalar(
                    out=QS[hq].t[0:64, :], in0=bank.t[0:64, k * TT:(k + 1) * TT], scalar1=0.125, scalar2=None,
                    op0=ALU.mult)), r=[bank], w=[QS[hq]], cost=0.3)
    ws = ((2 * i) % 8) * 128
    for g in range(2):
        bank = nb()
        proj_fm(bank, 0, TT, OKS + 64 * g, 64)
        proj_fm(bank, TT, 2 * TT, OKW + 64 * g, 64)
        S.op("act", (lambda bank=bank, g=g: nc.scalar.copy(out=N.KS[g].t[0:64, q0:q0 + TT], in_=bank.t[0:64, 0:TT])),
             r=[bank], w=[N.KSr[g][2 * i], N.KSr[g][2 * i + 1]], cost=0.3)
        S.op("dve", (lambda bank=bank, g=g: nc.vector.tensor_copy(out=N.KW[g].t[0:64, ws:ws + TT],
                                                                  in_=bank.t[0:64, TT:2 * TT])),
             r=[bank], w=[N.KWr[g][(2 * i) % 8], N.KWr[g][(2 * i + 1) % 8]], cost=0.3)
    for g in range(2):
        bank = nb()
        proj_fm(bank, 0, TT, OKC + 64 * g, 64)
        proj_fm(bank, TT, 2 * TT, OVC + 64 * g, 64)
        for kv in range(2):
            xc = N.XC[kv][g]
            if kv == 0:
                S.op("act", (lambda bank=bank, xc=xc: nc.scalar.copy(out=xc.t[0:64, 16:272], in_=bank.t[0:64, 0:TT])),
                     r=[bank], w=[xc], cost=0.3)
            else:
                S.op("dve", (lambda bank=bank, xc=xc: nc.vector.tensor_copy(out=xc.t[0:64, 16:272],
                                                                            in_=bank.t[0:64, TT:2 * TT])),
                     r=[bank], w=[xc], cost=0.3)
            S.dma("sp", xc.t[64:128, 15:271], xc.t[0:64, 16:272], r=[xc], w=[xc], cost=2.5)
    for s in range(2):
        bank = nb()
        proj_tm(bank, s * 128, 128, OVS, 128, o0=0)
        proj_tm(bank, s * 128, 128, OVW, 128, o0=128)
        proj_tm(bank, s * 128, 128, OGT, 24, o0=256)
        kt = 2 * i + s
        S.op("act", (lambda bank=bank, kt=kt: nc.scalar.copy(
            out=N.VS.t[:, :, kt, 0:64], in_=bank.t[:, 0:128].rearrange("p (g d) -> p g d", g=2))),
            r=[bank], w=[N.VSr[kt]], cost=0.25)
        S.op("dve", (lambda bank=bank, kt=kt: nc.vector.tensor_copy(
            out=N.VW.t[:, :, kt % 8, 0:64], in_=bank.t[:, 128:256].rearrange("p (g d) -> p g d", g=2))),
            r=[bank], w=[N.VWr[kt % 8]], cost=0.2)
        S.op("act", (lambda bank=bank, s=s: nc.scalar.activation(out=gsig.t[:, s, :], in_=bank.t[:, 256:280],
                                                                 func=AF.Tanh, scale=0.5)), r=[bank], w=[gsig], tbl="exp", cost=0.2)
        S.op("dve", (lambda s=s: nc.vector.tensor_scalar(out=gsig.t[:, s, :], in0=gsig.t[:, s, :], scalar1=0.5, scalar2=0.5,
                                                         op0=ALU.mult, op1=ALU.add)), r=[gsig], w=[gsig], cost=0.1)
    nb0 = 16 * i - 1
    m0 = 1 if i == 0 else 0
    for kv in range(2):
        for g in range(2):
            xc = N.XC[kv][g]
            xv = xc.t[:, 0:272].rearrange("p (m t) -> p t m", t=16)
            bank = nb()
            for half in range(2):
                for c in range(16):
                    S.op("pe", (lambda bank=bank, half=half, c=c, kv=kv, xv=xv: nc.tensor.matmul(
                        bank.t[:, half * 16:(half + 1) * 16], lhsT=N.cw1[kv].t[:, c, half * 128:(half + 1) * 128],
                        rhs=xv[:, (2 * c) % 16, (2 * c) // 16:(2 * c) // 16 + 16], start=(c == 0), stop=(c == 15))),
                        r=[N.cw1[kv], xc], w=[bank], cost=0.08)
            for half in range(2):
                S.op("dve", (lambda bank=bank, half=half, kv=kv: nc.vector.tensor_scalar(
                    out=N.gx.t[:, half * 16:(half + 1) * 16], in0=bank.t[:, half * 16:(half + 1) * 16],
                    scalar1=N.hb[kv].t[:, half:half + 1], scalar2=None, op0=ALU.add)), r=[bank, N.hb[kv]], w=[N.gx],
                    cost=0.1)
            S.op("dve", lambda: nc.vector.tensor_tensor(out=N.gt.t[:], in0=N.gx.t[:], in1=N.gx.t[:], op=ALU.mult),
                 r=[N.gx], w=[N.gt], cost=0.1)
            S.op("dve", lambda: nc.vector.tensor_scalar(out=N.gt.t[:], in0=N.gt.t[:], scalar1=0.044715, scalar2=1.0,
                                                        op0=ALU.mult, op1=ALU.add), r=[N.gt], w=[N.gt], cost=0.1)
            S.op("dve", lambda: nc.vector.tensor_tensor(out=N.gt.t[:], in0=N.gt.t[:], in1=N.gx.t[:], op=ALU.mult),
                 r=[N.gt, N.gx], w=[N.gt], cost=0.1)
            S.op("act", lambda: nc.scalar.activation(out=N.gs.t[:], in_=N.gt.t[:], func=AF.Tanh, scale=0.5 * GELU_C),
                 r=[N.gt], w=[N.gs], tbl="exp", cost=0.2)
            S.op("dve", lambda: nc.vector.tensor_scalar(out=N.gs.t[:], in0=N.gs.t[:], scalar1=0.5, scalar2=0.5,
                                                        op0=ALU.mult, op1=ALU.add), r=[N.gs], w=[N.gs], cost=0.1)
            if kv == 0:
                S.op("dve", lambda: nc.vector.tensor_tensor(
                    out=N.h1g.t[:].rearrange("p a b -> p (a b)"), in0=N.gx.t[:], in1=N.gs.t[:], op=ALU.mult),
                    r=[N.gx, N.gs], w=[N.h1g], cost=0.1)
                b2 = nb()
                for half in range(2):
                    S.op("pe", (lambda b2=b2, half=half: nc.tensor.matmul(
                        b2.t[0:64, 0:16], lhsT=N.cw2[0].t[:, half, :], rhs=N.h1g.t[:, half, :],
                        start=(half == 0), stop=(half == 1))), r=[N.cw2[0], N.h1g], w=[b2], cost=0.07)
                S.op("act", (lambda b2=b2, g=g: nc.scalar.copy(out=N.KcT[g].t[:, nb0 + m0:nb0 + 16],
                                                                in_=b2.t[0:64, m0:16])), r=[b2], w=[N.KcT[g]], cost=0.2)
            else:
                S.op("dve", (lambda: nc.vector.tensor_tensor(
                    out=N.h1pad.t[:, :, nb0 + m0:nb0 + 16],
                    in0=N.gx.t[:].rearrange("p (a b) -> p a b", a=2)[:, :, m0:16],
                    in1=N.gs.t[:].rearrange("p (a b) -> p a b", a=2)[:, :, m0:16], op=ALU.mult)),
                    r=[N.gx, N.gs], w=[N.h1pad], cost=0.1)
                cts = sorted(set([(nb0 + m0) // 128, (nb0 + 15) // 128]))
                for ct in cts:
                    b2 = nb()
                    for half in range(2):
                        S.op("pe", (lambda b2=b2, half=half, ct=ct: nc.tensor.matmul(
                            b2.t[:, 0:64], lhsT=N.h1pad.t[:, half, ct * 128:(ct + 1) * 128], rhs=N.cw2[1].t[:, half, :],
                            start=(half == 0), stop=(half == 1))), r=[N.cw2[1], N.h1pad], w=[b2], cost=0.08)
                    S.op("dve", (lambda b2=b2, g=g, ct=ct: nc.vector.tensor_tensor(
                        out=N.VcA.t[:, g, ct, 0:64], in0=b2.t[:, 0:64], in1=N.VcA.t[:, g, ct, 0:64], op=ALU.add)),
                        r=[b2, N.VcA], w=[N.VcA], cost=0.12)
                S.op("dve", (lambda: nc.vector.memset(N.h1pad.t[:, :, nb0 + m0:nb0 + 16], 0.0)), w=[N.h1pad], cost=0.1)
            S.op("pool", (lambda xc=xc: nc.gpsimd.tensor_copy(out=xc.t[:, 0:16], in_=xc.t[:, 256:272])),
                 r=[xc], w=[xc], cost=0.2)

    nct = 2 if i >= 8 else 1
    A0 = acc[0]
    cjobs = []
    for g in range(2):
        for n in range(4):
            for ct in range(nct):
                cjobs.append((g, n, 4 * g + n, ct))
    cbank = {}

    def c_score(k):
        g, n, hq, ct = cjobs[k]
        sc = nb()
        cbank[k] = sc
        S.op("pe", (lambda: nc.tensor.matmul(
            sc.t[:, 0:TT], lhsT=N.KcT[g].t[0:64, ct * 128:(ct + 1) * 128], rhs=QS[hq].t[0:64, :],
            start=True, stop=True)), r=[N.KcT[g], QS[hq]], w=[sc], cost=0.13)

    c_score(0)
    for k, (g, n, hq, ct) in enumerate(cjobs):
        if k + 1 < len(cjobs):
            c_score(k + 1)
        sc = cbank[k]
        A = A0
        N.sbcnt += 1
        sbs = N.sbs[0]
        S.op("dve", (lambda sc=sc, sbs=sbs, ct=ct, hq=hq: nc.vector.scalar_tensor_tensor(
            out=sbs.t[:], in0=N.Dq.t[:, ct, :], scalar=SLOPES[hq], in1=sc.t[:, 0:TT],
            op0=ALU.mult, op1=ALU.add)), r=[N.Dq, sc], w=[sbs], cost=0.4)
        N.pccnt += 1
        P = N.PTc[N.pccnt % 2]
        S.op("act", (lambda sbs=sbs, P=P: nc.scalar.activation(out=P.t[:], in_=sbs.t[:], func=AF.Exp)),
             r=[sbs], w=[P], tbl="exp", cost=0.4)
        for s in range(2):
            first = (ct == 0 and s == 0)
            S.op("pe", (lambda P=P, s=s, g=g, ct=ct, first=first, A=A: nc.tensor.matmul(
                A.t[:, s * 65:(s + 1) * 65], lhsT=P.t[:, s * 128:(s + 1) * 128], rhs=N.VcA.t[:, g, ct, :],
                start=first, stop=False, skip_group_check=True)), r=[P, N.VcA], w=[A], cost=0.08)
            S.op("pe", (lambda P=P, s=s, ct=ct, A=A: nc.tensor.matmul(
                A.t[:, 130 + s * 64:130 + (s + 1) * 64], lhsT=P.t[:, s * 128:(s + 1) * 128],
                rhs=N.Mslc.t[:, ct, :], start=False, stop=False, skip_group_check=True)),
                r=[P, N.Mslc], w=[A], cost=0.08)
        if ct == nct - 1:
            rzc = N.rzc[hq % 2]
            S.op("dve", (lambda A=A, rzc=rzc: nc.vector.tensor_scalar(
                out=rzc.t[:], in0=A.t[:, 0:130].rearrange("p (s c) -> p s c", s=2)[:, :, 64], scalar1=1e-30,
                scalar2=None, op0=ALU.max)), r=[A], w=[rzc], cost=0.1)
            S.op("dve", (lambda rzc=rzc: nc.vector.reciprocal(out=rzc.t[:], in_=rzc.t[:])), r=[rzc], w=[rzc], cost=0.1)
            for s in range(2):
                S.op("dve", (lambda s=s, hq=hq, A=A, rzc=rzc: nc.vector.tensor_scalar(
                    out=ocmp.t[:, s, hq, :], in0=A.t[:, s * 65:s * 65 + 64], scalar1=rzc.t[:, s:s + 1], scalar2=None,
                    op0=ALU.mult)), r=[A, rzc], w=[ocmp], cost=0.12)
                if n == 0:
                    S.op("dve", (lambda s=s, g=g, A=A, rzc=rzc: nc.vector.tensor_scalar(
                        out=N.imp.t[:, s, g, :], in0=A.t[:, 130 + s * 64:130 + (s + 1) * 64],
                        scalar1=rzc.t[:, s:s + 1], scalar2=None, op0=ALU.mult)), r=[A, rzc], w=[N.imp], cost=0.12)
                else:
                    S.op("dve", (lambda s=s, g=g, A=A, rzc=rzc: nc.vector.scalar_tensor_tensor(
                        out=N.imp.t[:, s, g, :], in0=A.t[:, 130 + s * 64:130 + (s + 1) * 64],
                        scalar=rzc.t[:, s:s + 1], in1=N.imp.t[:, s, g, :], op0=ALU.mult, op1=ALU.add)),
                        r=[A, rzc, N.imp], w=[N.imp], cost=0.12)

    mq = 0
    for g in range(2):
        for s in range(2):
            S.op("dve", (lambda s=s, g=g: nc.vector.tensor_tensor(out=N.scr.t[:], in0=N.imp.t[:, s, g, :],
                                                                  in1=N.force.t[:, s, :], op=ALU.add)),
                 r=[N.imp, N.force], w=[N.scr], cost=0.12)
            S.op("dve", lambda: nc.vector.max(out=N.m8a.t[:], in_=N.scr.t[:]), r=[N.scr], w=[N.m8a], cost=0.12)
            S.op("dve", lambda: nc.vector.match_replace(out=N.scr2.t[:], in_to_replace=N.m8a.t[:], in_values=N.scr.t[:],
                                                        imm_value=-1e30), r=[N.scr, N.m8a], w=[N.scr2], cost=0.15)
            S.op("dve", lambda: nc.vector.max(out=N.m8b.t[:], in_=N.scr2.t[:]), r=[N.scr2], w=[N.m8b], cost=0.12)
            S.op("dve", lambda: nc.vector.tensor_scalar(out=N.msk.t[:], in0=N.scr.t[:], scalar1=N.m8b.t[:, 7:8],
                                                        scalar2=None, op0=ALU.is_ge), r=[N.scr, N.m8b], w=[N.msk], cost=0.12)
            for n in range(4):
                hq = 4 * g + n
                M = N.Mq[mq % 2]
                mq += 1
                S.op("dve", (lambda M=M, hq=hq, s=s: nc.vector.tensor_scalar(
                    out=M.t[:, 64:128], in0=N.msk.t[:], scalar1=BIG, scalar2=N.negb.t[:, hq * 2 + s:hq * 2 + s + 1],
                    op0=ALU.mult, op1=ALU.add)), r=[N.msk, N.negb], w=[M], cost=0.12)
                S.op("pe", (lambda M=M: nc.tensor.transpose(pt.t[:, 0:128], M.t[:], C.ident.t[:])),
                     r=[M, C.ident], w=[pt], cost=0.08)
                S.op("act", (lambda hq=hq, s=s: nc.scalar.copy(out=QS[hq].t[64:128, s * 128:(s + 1) * 128],
                                                                in_=pt.t[64:128, 0:128])), r=[pt], w=[QS[hq]], cost=0.2)


def nsa_back(S, nc, C, l, i, N, oTi, pt, nbS, acc):
    par = i % 2
    QS = N.QS[par]
    gsig = N.gsig[par]
    ocmp = N.ocmp[par]

    def next_pt():
        N.ptcnt += 1
        return N.PT[N.ptcnt % NPT]

    def run_jobs(jobs):
        banks = [None] * len(jobs)

        def issue_score(k):
            sc = nbS()
            banks[k] = sc
            jobs[k]["score"](sc)

        LA = NPS - 1
        for k in range(min(LA, len(jobs))):
            issue_score(k)
        for k, jb in enumerate(jobs):
            if k + LA < len(jobs):
                issue_score(k + LA)
            sc = banks[k]
            P = next_pt()
            c0, c1 = jb["c0"], jb["c1"]
            S.op("act", (lambda sc=sc, P=P, c0=c0, c1=c1, bias=jb["bias"]: nc.scalar.activation(
                out=P.t[:, c0:c1], in_=sc.t[:, c0:c1], func=AF.Exp, bias=bias)), r=[sc, N.btab], w=[P],
                tbl="exp", cost=0.3 + 0.0007 * (c1 - c0))
            for d0, mk in jb["masks"]:
                S.op("pool", (lambda P=P, d0=d0, mk=mk: nc.gpsimd.tensor_tensor(
                    out=P.t[:, d0:d0 + 128], in0=P.t[:, d0:d0 + 128], in1=mk.t[:], op=ALU.mult)),
                    r=[P, mk], w=[P], cost=0.43)
            for pv in jb["pv"]:
                pv(P)

    def sel_jobs(g, hq, A, fr):
        jobs = []
        for kt in range(2 * i + 2):
            k0 = kt * 128
            c0 = 128 if kt == 2 * i + 1 else 0
            idx = 2 * i - kt + 1

            def score(sc, g=g, hq=hq, k0=k0, c0=c0, kt=kt):
                S.op("pe", (lambda: nc.tensor.matmul(
                    sc.t[:, c0:TT], lhsT=N.KS[g].t[:, k0:k0 + 128], rhs=QS[hq].t[:, c0:TT],
                    start=True, stop=True)), r=[N.KSr[g][kt], QS[hq]], w=[sc], cost=0.08 + 0.0004 * (TT - c0))

            masks = []
            if kt >= 2 * i:
                masks.append(((kt - 2 * i) * 128, N.tri))
            pvs = []
            for s in range(c0 // 128, 2):
                def pv(P, s=s, g=g, kt=kt):
                    fresh = fr[0]
                    fr[0] = False
                    S.op("pe", (lambda: nc.tensor.matmul(
                        A.t[:, s * 65:(s + 1) * 65], lhsT=P.t[:, s * 128:(s + 1) * 128], rhs=N.VS.t[:, g, kt, :],
                        start=fresh, stop=False, skip_group_check=True)), r=[P, N.VSr[kt]], w=[A], cost=0.08)
                pvs.append(pv)
            jobs.append(dict(score=score, c0=c0, c1=TT, bias=N.btab.t[:, hq, idx:idx + 1], masks=masks, pv=pvs))
        return jobs

    def win_jobs(g, hq, A):
        jobs = []
        for j in range(6):
            kt = 2 * i - 4 + j
            if kt < 0:
                continue
            slot = kt % 8
            s_lo, s_hi = max(0, j - 4), min(1, j)
            c0, c1 = s_lo * 128, (s_hi + 1) * 128
            idx = 5 - j

            def score(sc, g=g, hq=hq, slot=slot, c0=c0, c1=c1):
                S.op("pe", (lambda: nc.tensor.matmul(
                    sc.t[:, c0:c1], lhsT=N.KW[g].t[:, slot * 128:(slot + 1) * 128], rhs=QS[hq].t[:, c0:c1],
                    start=True, stop=True)), r=[N.KWr[g][slot], QS[hq]], w=[sc], cost=0.08 + 0.0004 * (c1 - c0))

            masks = []
            for s in range(s_lo, s_hi + 1):
                if j == s:
                    masks.append((s * 128, N.strict))
                elif j == s + 4:
                    masks.append((s * 128, N.tri))
            pvs = []
            for s in range(s_lo, s_hi + 1):
                def pv(P, s=s, g=g, slot=slot):
                    S.op("pe", (lambda: nc.tensor.matmul(
                        A.t[:, 130 + s * 65:130 + (s + 1) * 65], lhsT=P.t[:, s * 128:(s + 1) * 128],
                        rhs=N.VW.t[:, g, slot, :], start=False, stop=False, skip_group_check=True)),
                        r=[P, N.VWr[slot]], w=[A], cost=0.08)
                pvs.append(pv)
            jobs.append(dict(score=score, c0=c0, c1=c1, bias=N.btab.t[:, hq, idx:idx + 1], masks=masks, pv=pvs))
        return jobs

    def interleave(a, b):
        out = []
        for k in range(max(len(a), len(b))):
            if k < len(a):
                out.append(a[k])
            if k < len(b):
                out.append(b[k])
        return out

    def combine(hq, A, slot):
        for s in range(2):
            fo = N.fo[slot][s]
            rz = N.rz[slot][s]
            Av = A.t[:, 0:260].rearrange("p (b c) -> p b c", b=2)
            S.op("dve", (lambda s=s, rz=rz, Av=Av: nc.vector.reciprocal(out=rz.t[:, 0:2], in_=Av[:, :, s * 65 + 64])),
                 r=[A], w=[rz], cost=0.1)
            S.op("dve", (lambda s=s, rz=rz: nc.vector.tensor_tensor(
                out=rz.t[:, 2:4], in0=rz.t[:, 0:2], in1=gsig.t[:, s, hq * 3 + 1:hq * 3 + 3], op=ALU.mult)),
                r=[rz, gsig], w=[rz], cost=0.1)
            S.op("dve", (lambda s=s, fo=fo, rz=rz: nc.vector.tensor_scalar(
                out=fo.t[:], in0=A.t[:, s * 65:s * 65 + 64], scalar1=rz.t[:, 2:3], scalar2=None, op0=ALU.mult)),
                r=[A, rz], w=[fo], cost=0.12)
            S.op("dve", (lambda s=s, fo=fo, rz=rz: nc.vector.scalar_tensor_tensor(
                out=fo.t[:], in0=A.t[:, 130 + s * 65:130 + s * 65 + 64], scalar=rz.t[:, 3:4], in1=fo.t[:],
                op0=ALU.mult, op1=ALU.add)), r=[A, rz, fo], w=[fo], cost=0.12)
            S.op("dve", (lambda s=s, fo=fo: nc.vector.scalar_tensor_tensor(
                out=N.onsa.t[:, s, hq * 64:(hq + 1) * 64], in0=ocmp.t[:, s, hq, :],
                scalar=gsig.t[:, s, hq * 3:hq * 3 + 1], in1=fo.t[:], op0=ALU.mult, op1=ALU.add)),
                r=[ocmp, gsig, fo], w=[N.onsa], cost=0.12)

    for g in range(2):
        for n in range(4):
            hq = 4 * g + n
            run_jobs(sel_jobs(g, hq, acc[1], [True]) + win_jobs(g, hq, acc[1]))
            combine(hq, acc[1], hq % 2)
    for s in range(2):
        bk = nbS()
        for k in range(4):
            S.op("pe", (lambda s=s, k=k, bk=bk: nc.tensor.matmul(
                bk.t[:, k * 128:(k + 1) * 128], lhsT=N.onsa.t[:, s, k * 128:(k + 1) * 128], rhs=C.ident.t[:],
                start=True, stop=True)), r=[N.onsa, C.ident], w=[bk], cost=0.1)
        S.op("act", (lambda s=s, bk=bk: nc.scalar.copy(out=oTi.t[:, 0:4, s * 128:(s + 1) * 128],
                                                       in_=bk.t[:, 0:512].rearrange("p (k t) -> p k t", k=4))),
             r=[bk], w=[oTi], cost=0.5)


COST_US.update({147: 0.458, 415: 1.234, 425: 0.181, 427: 0.229, 428: 0.229, 478: 1.24, 482: 0.073, 485: 0.945, 495: 0.126, 499: 0.377, 501: 0.404, 510: 0.23, 514: 0.676, 519: 1.211, 539: 1.251, 559: 0.138, 749: 9.962, 763: 0.006, 771: 0.17, 773: 0.172, 775: 0.485, 776: 0.485, 785: 0.147, 792: 0.177, 803: 0.986, 807: 0.114, 810: 0.998, 823: 0.552, 826: 0.568, 828: 0.616, 830: 0.665, 832: 1.264, 842: 0.597, 844: 0.623, 846: 0.384, 849: 0.417, 852: 0.351, 853: 0.771, 856: 0.674, 860: 0.385, 863: 0.453, 866: 0.36, 867: 0.324, 869: 0.384, 871: 0.413, 872: 0.718, 874: 0.717, 876: 0.738, 878: 0.693, 884: 0.136, 887: 0.171, 890: 0.147, 892: 0.397, 898: 0.24, 901: 0.253, 904: 0.212, 907: 0.341, 910: 0.588, 912: 0.182, 914: 0.303, 918: 0.191, 921: 0.185, 923: 0.196, 927: 0.328, 931: 0.182, 934: 0.299, 954: 0.268, 958: 0.653, 962: 0.726, 967: 0.168, 1029: 0.971, 1031: 0.261, 1033: 0.272, 1034: 3.581, 1035: 0.913, 1036: 0.329, 1037: 0.544, 1039: 0.189, 1064: 0.03, 1068: 0.062, 1095: 0.445, 1099: 0.403, 1108: 0.351, 1110: 0.4, 1121: 0.354, 1124: 0.4, 1135: 0.206, 1138: 0.262, 1141: 0.166, 1143: 1.914, 1155: 0.037, 1160: 0.23, 1165: 0.629, 1167: 0.2, 1169: 0.196, 1171: 0.224, 1173: 0.177, 1176: 0.199, 1181: 0.115, 1184: 0.139, 1188: 0.19, 1197: 0.102, 1200: 0.197, 1203: 0.056, 1205: 0.221, 1222: 0.324, 1234: 0.371, 1239: 0.378, 1243: 0.099, 1246: 0.048, 1253: 0.133, 1256: 0.164, 1258: 0.282, 1262: 0.282, 1266: 0.28, 1275: 0.214, 1278: 0.225, 1279: 0.286, 1281: 0.224, 1282: 0.313, 1288: 0.282, 1291: 0.163, 1293: 0.26, 1325: 0.389, 1329: 0.428, 1343: 0.217, 1355: 0.074, 1375: 0.219, 1388: 0.076, 1410: 0.151, 1412: 0.19, 1415: 0.282, 1418: 0.279, 1421: 0.249, 1436: 0.104, 1439: 0.571})
```
